# Optimizing a Trainium2 kernel written in Bass

```python
import math
import jax, jax.numpy as jnp
from jax import lax
import numpy as np

D_MODEL = 1024
BATCH = 8
SEQ = 8192
DEPTH = 2
DEC_BATCH = 32
DEC_SEQ = 32
PAST_LEN = 4096

CHUNK = 64
Q_BLOCK = 128
N_EVEN = (DEPTH + 1) // 2
N_ODD = DEPTH // 2
H_DIFF = 4
DH_QK = 64
DV_DIFF = 2 * DH_QK
D_DIFF = H_DIFF * DV_DIFF
Q_DIFF = 2 * H_DIFF * DH_QK
D_CONV = D_MODEL - D_DIFF
CONV_W = 3
SPLIT_EVEN = [Q_DIFF, 2 * Q_DIFF, 2 * Q_DIFF + D_DIFF, 2 * Q_DIFF + D_DIFF + D_CONV, 2 * Q_DIFF + D_DIFF + 2 * D_CONV]
EVEN_IN = 2 * Q_DIFF + D_DIFF + 3 * D_CONV
H_SB = 16
DH_SB = 64
D_SB = H_SB * DH_SB
N_MEM = 256
H_MEM = 4
DH_MEM = D_MODEL // H_MEM
D_FF = 4 * D_MODEL
ROPE_THETA = 10000.0
EPS = 1e-6
SUBLN_EPS = 1e-5
NEG_INF = -1e30

kernel_name = 'hybrid_stream_diffconv_stickbreak_step'


def rmsnorm(x, g, eps=EPS):
    xf = x.astype(jnp.float32)
    y = xf * lax.rsqrt(jnp.mean(xf * xf, axis=-1, keepdims=True) + eps)
    return (y * g.astype(jnp.float32)).astype(x.dtype)


def rope(x, pos):
    d = x.shape[-1]
    half = d // 2
    inv = jnp.power(ROPE_THETA, -jnp.arange(half, dtype=jnp.float32) * (2.0 / d))
    ang = pos.astype(jnp.float32)[:, None] * inv[None, :]
    cos = jnp.cos(ang)[None, :, None, :]
    sin = jnp.sin(ang)[None, :, None, :]
    xf = x.astype(jnp.float32)
    x1, x2 = xf[..., :half], xf[..., half:]
    return jnp.concatenate([x1 * cos - x2 * sin, x2 * cos + x1 * sin], axis=-1).astype(x.dtype)


def sweep_queries(fn, q, q_pos):
    b, t = q.shape[0], q.shape[1]
    if t <= Q_BLOCK:
        return fn(q, q_pos)
    nb = t // Q_BLOCK
    qb = jnp.swapaxes(q.reshape((b, nb, Q_BLOCK) + q.shape[2:]), 0, 1)
    pb = q_pos.reshape(nb, Q_BLOCK)
    out = lax.map(lambda args: fn(args[0], args[1]), (qb, pb))
    return jnp.swapaxes(out, 0, 1).reshape((b, t) + out.shape[3:])


def diff_attn_block(q, k, v, q_pos, k_pos, lam):
    s = jnp.einsum('bqhd,bkhd->bhqk', q, k).astype(jnp.float32) * (DH_QK ** -0.5)
    visible = (k_pos[None, :] // CHUNK) <= (q_pos[:, None] // CHUNK)
    p = jax.nn.softmax(jnp.where(visible, s, NEG_INF), axis=-1)
    b, _, tq, tk = p.shape
    p = p.reshape(b, H_DIFF, 2, tq, tk)
    a = p[:, :, 0] - lam * p[:, :, 1]
    return jnp.einsum('bhqk,bkhd->bqhd', a.astype(v.dtype), v)


def stick_breaking_block(q, k, v, q_pos, k_pos):
    z = jnp.einsum('bqhd,bkhd->bhqk', q, k).astype(jnp.float32) * (DH_SB ** -0.5)
    visible = k_pos[None, :] < q_pos[:, None]
    log_stay = jnp.where(visible, jax.nn.log_sigmoid(-z), 0.0)
    after = lax.cumsum(log_stay, axis=3, reverse=True) - log_stay
    w = jnp.where(visible, jnp.exp(jax.nn.log_sigmoid(z) + after), 0.0)
    return jnp.einsum('bhqk,bkhd->bqhd', w.astype(v.dtype), v)


def causal_short_conv(u, prev, w):
    t = u.shape[1]
    ext = jnp.concatenate([prev, u], axis=1)
    y = ext[:, 0:t] * w[0]
    for j in range(1, CONV_W):
        y = y + ext[:, j:j + t] * w[j]
    return y, ext[:, ext.shape[1] - (CONV_W - 1):]


def even_mixer(h, pos, past_k, past_v, prev_conv, w_in, w_out, lq1, lk1, lq2, lk2, g_sub, conv_w, lam_init):
    b, t, _ = h.shape
    q, k, v, gate_b, gate_c, u = jnp.split(h @ w_in, SPLIT_EVEN, axis=-1)
    q = rope(q.reshape(b, t, 2 * H_DIFF, DH_QK), pos)
    k = rope(k.reshape(b, t, 2 * H_DIFF, DH_QK), pos)
    v = v.reshape(b, t, H_DIFF, DV_DIFF)
    k_all = k if past_k is None else jnp.concatenate([past_k, k], axis=1)
    v_all = v if past_v is None else jnp.concatenate([past_v, v], axis=1)
    k_pos = jnp.arange(k_all.shape[1])
    f32 = jnp.float32
    lam = (jnp.exp(jnp.sum(lq1.astype(f32) * lk1.astype(f32)))
           - jnp.exp(jnp.sum(lq2.astype(f32) * lk2.astype(f32))) + lam_init)
    o = sweep_queries(lambda qb, pb: diff_attn_block(qb, k_all, v_all, pb, k_pos, lam), q, pos)
    o = (rmsnorm(o, g_sub, SUBLN_EPS) * (1.0 - lam_init)).reshape(b, t, D_DIFF)
    conv_out, new_conv = causal_short_conv(gate_c * u, prev_conv, conv_w)
    y = jnp.concatenate([o, gate_b * conv_out], axis=-1) @ w_out
    return y, k, v, new_conv


def odd_mixer(h, pos, past_k, past_v, w_in, w_out):
    b, t, _ = h.shape
    q, k, v = jnp.split(h @ w_in, 3, axis=-1)
    q = q.reshape(b, t, H_SB, DH_SB)
    k = k.reshape(b, t, H_SB, DH_SB)
    v = v.reshape(b, t, H_SB, DH_SB)
    k_all = k if past_k is None else jnp.concatenate([past_k, k], axis=1)
    v_all = v if past_v is None else jnp.concatenate([past_v, v], axis=1)
    k_pos = jnp.arange(k_all.shape[1])
    o = sweep_queries(lambda qb, pb: stick_breaking_block(qb, k_all, v_all, pb, k_pos), q, pos)
    return o.reshape(b, t, D_SB) @ w_out, k, v


def mem_project(mem, g, wk, wv):
    b = mem.shape[0]
    m = rmsnorm(mem, g)
    return (m @ wk).reshape(b, N_MEM, H_MEM, DH_MEM), (m @ wv).reshape(b, N_MEM, H_MEM, DH_MEM)


def mem_attend(h, mk, mv, wq, wo):
    b, t, _ = h.shape
    q = (h @ wq).reshape(b, t, H_MEM, DH_MEM)
    s = jnp.einsum('bqhd,bkhd->bhqk', q, mk).astype(jnp.float32) * (DH_MEM ** -0.5)
    p = jax.nn.softmax(s, axis=-1)
    o = jnp.einsum('bhqk,bkhd->bqhd', p.astype(mv.dtype), mv).reshape(b, t, D_MODEL)
    return o @ wo


def setup_inputs(seed: int = 0) -> dict:
    key = jax.random.key(seed)
    ks = list(jax.random.split(key, 40))

    def nrm(shape, scale=1.0):
        return jax.random.normal(ks.pop(), shape, jnp.float32) * scale

    def gain(shape):
        return 1.0 + 0.02 * nrm(shape)

    return {
        'x_prompt': nrm((BATCH, SEQ, D_MODEL)),
        'x_sample': nrm((DEC_BATCH, DEC_SEQ, D_MODEL)),
        'cache_diff_k': nrm((N_EVEN, DEC_BATCH, PAST_LEN, 2 * H_DIFF, DH_QK)),
        'cache_diff_v': nrm((N_EVEN, DEC_BATCH, PAST_LEN, H_DIFF, DV_DIFF)),
        'state_conv': nrm((N_EVEN, DEC_BATCH, CONV_W - 1, D_CONV)),
        'cache_sb_k': nrm((N_ODD, DEC_BATCH, PAST_LEN, H_SB, DH_SB)),
        'cache_sb_v': nrm((N_ODD, DEC_BATCH, PAST_LEN, H_SB, DH_SB)),
        'cache_mem_k': nrm((DEPTH, DEC_BATCH, N_MEM, H_MEM, DH_MEM)),
        'cache_mem_v': nrm((DEPTH, DEC_BATCH, N_MEM, H_MEM, DH_MEM)),
        'mem_prompt': nrm((BATCH, N_MEM, D_MODEL)),
        'w_in_even': nrm((N_EVEN, D_MODEL, EVEN_IN), D_MODEL ** -0.5),
        'w_out_even': nrm((N_EVEN, D_DIFF + D_CONV, D_MODEL), (D_DIFF + D_CONV) ** -0.5),
        'lambda_q1': nrm((N_EVEN, DH_QK), 0.1),
        'lambda_k1': nrm((N_EVEN, DH_QK), 0.1),
        'lambda_q2': nrm((N_EVEN, DH_QK), 0.1),
        'lambda_k2': nrm((N_EVEN, DH_QK), 0.1),
        'subln_gain': gain((N_EVEN, DV_DIFF)),
        'conv_w': nrm((N_EVEN, CONV_W, D_CONV), CONV_W ** -0.5),
        'w_in_odd': nrm((N_ODD, D_MODEL, 3 * D_SB), D_MODEL ** -0.5),
        'w_out_odd': nrm((N_ODD, D_SB, D_MODEL), D_SB ** -0.5),
        'norm_mix': gain((DEPTH, D_MODEL)),
        'norm_mem': gain((DEPTH, D_MODEL)),
        'norm_cross': gain((DEPTH, D_MODEL)),
        'w_q_mem': nrm((DEPTH, D_MODEL, D_MODEL), D_MODEL ** -0.5),
        'w_k_mem': nrm((DEPTH, D_MODEL, D_MODEL), D_MODEL ** -0.5),
        'w_v_mem': nrm((DEPTH, D_MODEL, D_MODEL), D_MODEL ** -0.5),
        'w_o_mem': nrm((DEPTH, D_MODEL, D_MODEL), D_MODEL ** -0.5),
        'norm_ffn': gain((DEPTH, D_MODEL)),
        'w_ffn_up': nrm((DEPTH, D_MODEL, D_FF), D_MODEL ** -0.5),
        'w_ffn_down': nrm((DEPTH, D_FF, D_MODEL), D_FF ** -0.5),
        'norm_final': gain((D_MODEL,)),
    }


def reference(x_prompt, x_sample, cache_diff_k, cache_diff_v, state_conv, cache_sb_k, cache_sb_v,
              cache_mem_k, cache_mem_v, mem_prompt, w_in_even, w_out_even, lambda_q1, lambda_k1,
              lambda_q2, lambda_k2, subln_gain, conv_w, w_in_odd, w_out_odd, norm_mix, norm_mem,
              norm_cross, w_q_mem, w_k_mem, w_v_mem, w_o_mem, norm_ffn, w_ffn_up, w_ffn_down, norm_final):

    def run(x, mem_k, mem_v, past):
        b, t, _ = x.shape
        past_len = 0 if past is None else past[0].shape[2]
        pos = jnp.arange(past_len, past_len + t)
        dks, dvs, convs, sks, svs = [], [], [], [], []
        for i in range(DEPTH):
            h = rmsnorm(x, norm_mix[i])
            if i % 2 == 0:
                e = i // 2
                if past is None:
                    pk, pv, pc = None, None, jnp.zeros((b, CONV_W - 1, D_CONV), x.dtype)
                else:
                    pk, pv, pc = past[0][e], past[1][e], past[2][e]
                lam_init = 0.8 - 0.6 * math.exp(-0.3 * i)
                mix, nk, nv, nc = even_mixer(h, pos, pk, pv, pc, w_in_even[e], w_out_even[e],
                                             lambda_q1[e], lambda_k1[e], lambda_q2[e], lambda_k2[e],
                                             subln_gain[e], conv_w[e], lam_init)
                dks.append(nk)
                dvs.append(nv)
                convs.append(nc)
            else:
                o = i // 2
                pk = None if past is None else past[3][o]
                pv = None if past is None else past[4][o]
                mix, nk, nv = odd_mixer(h, pos, pk, pv, w_in_odd[o], w_out_odd[o])
                sks.append(nk)
                svs.append(nv)
            x = x + mix
            x = x + mem_attend(rmsnorm(x, norm_cross[i]), mem_k[i], mem_v[i], w_q_mem[i], w_o_mem[i])
            hf = rmsnorm(x, norm_ffn[i])
            x = x + jnp.square(jax.nn.relu(hf @ w_ffn_up[i])) @ w_ffn_down[i]
        return (rmsnorm(x, norm_final), jnp.stack(dks), jnp.stack(dvs), jnp.stack(convs),
                jnp.stack(sks), jnp.stack(svs))

    mkv = [mem_project(mem_prompt, norm_mem[i], w_k_mem[i], w_v_mem[i]) for i in range(DEPTH)]
    p_mem_k = jnp.stack([m[0] for m in mkv])
    p_mem_v = jnp.stack([m[1] for m in mkv])

    y_prompt, p_diff_k, p_diff_v, p_conv, p_sb_k, p_sb_v = run(x_prompt, p_mem_k, p_mem_v, None)
    y_sample, s_diff_k, s_diff_v, s_conv, s_sb_k, s_sb_v = run(
        x_sample, cache_mem_k, cache_mem_v,
        (cache_diff_k, cache_diff_v, state_conv, cache_sb_k, cache_sb_v))

    return (y_prompt, y_sample, p_diff_k, p_diff_v, p_conv, p_sb_k, p_sb_v, p_mem_k, p_mem_v,
            s_diff_k, s_diff_v, s_conv, s_sb_k, s_sb_v)
```

```python
import contextlib
import numpy as np
import concourse.bass as bass
import concourse.mybir as mybir
from concourse.bass_utils import run_bass_kernel_spmd

F32 = mybir.dt.float32
BF16 = mybir.dt.bfloat16
AF = mybir.ActivationFunctionType
ALU = mybir.AluOpType

D = 1024
NSEQ = 4
LS = 32
EPS = 1e-6
SUBLN_EPS = 1e-5
LAM_INIT0 = 0.2
NWS = 3
NKV = 3
NSTG = 3
NFS = 8
NBS = 8


class Sched:
    ENGS = ["sync", "scalar", "vector", "gpsimd", "tensor"]

    def __init__(self):
        self.ops = []
        self.state = {}
        self.dma_count = {}
        self.frozen = False
        import os
        self.maxops = int(os.environ.get("KOPS", "100000000"))

    def add(self, eng, fn, reads=(), writes=(), dsem=None, partial=False):
        if self.frozen or len(self.ops) >= self.maxops:
            return -1
        idx = len(self.ops)
        deps = set()
        for k in reads:
            st = self.state.setdefault(k, [[], [], []])
            deps.update(st[0])
            if k[0] == "PS":
                deps.update(r for r in st[1] if self.ops[r]["eng"] != eng)
        for k in writes:
            st = self.state.setdefault(k, [[], [], []])
            if st[1] or not partial:
                deps.update(st[1])
                deps.update(st[0])
                st[2] = list(st[1]) + list(st[0])
                st[0] = [idx]
                st[1] = []
            else:
                deps.update(st[2])
                st[0].append(idx)
        for k in reads:
            self.state[k][1].append(idx)
        waits = []
        for d in deps:
            od = self.ops[d]
            if od["dsem"] is not None:
                waits.append(("dma", od["dsem"], 16 * self.dma_count[od["dsem"]]))
            else:
                if od["eng"] == eng and eng == "tensor":
                    continue
                od["signaled"] = True
                waits.append(("eng", d))
        op = dict(eng=eng, fn=fn, dsem=dsem, signaled=False, waits=waits)
        if dsem is not None:
            self.dma_count[dsem] = self.dma_count.get(dsem, 0) + 1
        self.ops.append(op)
        return idx

    def emit(self, nc, es):
        cnt = {e: 0 for e in self.ENGS}
        for op in self.ops:
            if op["dsem"] is None and op["signaled"]:
                cnt[op["eng"]] += 1
                op["sig"] = cnt[op["eng"]]
        esem = {e: es.enter_context(nc.semaphore("se_" + e)) for e in self.ENGS}
        dsem = {k: es.enter_context(nc.semaphore("sd_%d" % i)) for i, k in enumerate(sorted(self.dma_count, key=str))}
        block = es.enter_context(nc.Block())
        ops = self.ops

        def run(engname):
            def body(e):
                seen = {}
                for op in ops:
                    if op["eng"] != engname:
                        continue
                    for w in op["waits"]:
                        if w[0] == "dma":
                            s, v = dsem[w[1]], w[2]
                            key = ("d", w[1])
                        else:
                            od = ops[w[1]]
                            s, v = esem[od["eng"]], od["sig"]
                            key = ("e", od["eng"])
                        if seen.get(key, 0) >= v:
                            continue
                        seen[key] = v
                        e.wait_ge(s, v)
                    ins = op["fn"](e)
                    if op["dsem"] is not None:
                        ins.then_inc(dsem[op["dsem"]], 16)
                    elif op["signaled"]:
                        ins.then_inc(esem[engname], 1)
                if engname == "sync":
                    for k, c in self.dma_count.items():
                        e.wait_ge(dsem[k], 16 * c)
            return body

        block.sync(run("sync"))
        block.scalar(run("scalar"))
        block.vector(run("vector"))
        block.gpsimd(run("gpsimd"))
        block.tensor(run("tensor"))


class _Stop(Exception):
    pass


def build(T, PAST):
    import os
    STOP = int(os.environ.get("KSTOP", "9999"))

    def ck(n):
        if n >= STOP:
            S.frozen = True
    NB = T // 512
    NKS = PAST + LS
    nc = bass.Bass("TRN2", target_bir_lowering=False)
    es = contextlib.ExitStack()
    S = Sched()

    def din(name, shape, dt=F32):
        return nc.dram_tensor(name, list(shape), dt, kind="ExternalInput").ap()

    def dout(name, shape):
        return nc.dram_tensor(name, list(shape), F32, kind="ExternalOutput").ap()

    def dscr(name, shape, dt=BF16):
        return nc.dram_tensor(name, list(shape), dt).ap()

    xp = din("xp", [T, D]); xs = din("xs", [128, D])
    cdk = din("cdk", [NSEQ, PAST, 512]); cdv = din("cdv", [NSEQ, PAST, 512])
    sconv = din("sconv", [NSEQ * 2, 512])
    csk = din("csk", [NSEQ, PAST, 1024]); csv = din("csv", [NSEQ, PAST, 1024])
    cmk = din("cmk", [2, NSEQ, 256, 1024]); cmv = din("cmv", [2, NSEQ, 256, 1024])
    memp = din("memp", [256, D])
    w_in0 = din("w_in0", [D + 1, 3072]); w_out0 = din("w_out0", [D + 1, D])
    lam4 = din("lam4", [4, 64]); subln = din("subln", [128, 1]); convw = din("convw", [3, 512])
    w_in1 = din("w_in1", [D + 1, 3072]); w_out1 = din("w_out1", [D + 1, D])
    nmix = din("nmix", [2, D]); nmem = din("nmem", [2, D]); ncross = din("ncross", [2, D]); nffn = din("nffn", [2, D])
    wq = din("wq", [2, D + 1, D]); wk = din("wk", [2, D + 1, D]); wv = din("wv", [2, D + 1, D]); wo = din("wo", [2, D + 1, D])
    wup = din("wup", [2, D + 1, 4096]); wdn = din("wdn", [2, 4097, D]); nfin = din("nfin", [1, D])
    c_mats = din("c_mats", [128, 4, 128])
    c_mask = din("c_mask", [128, 10, 512])
    c_rope = din("c_rope", [T + 129, 256])

    y_p = dout("y_p", [T, D]); y_s = dout("y_s", [128, D])
    pdk = dout("pdk", [T, 512]); pdv = dout("pdv", [T, 512]); pconv = dout("pconv", [2, 512])
    psk = dout("psk", [T, 1024]); psv = dout("psv", [T, 1024])
    pmk = dout("pmk", [2, 256, 1024]); pmv = dout("pmv", [2, 256, 1024])
    sdk = dout("sdk", [128, 512]); sdv = dout("sdv", [128, 512]); sconvo = dout("sconvo", [NSEQ * 2, 512])
    ssk = dout("ssk", [128, 1024]); ssv = dout("ssv", [128, 1024])

    wb_in = [dscr("wb_in0", [D, 3072]), dscr("wb_in1", [D, 3072])]
    wb_out = [dscr("wb_out0", [D, D]), dscr("wb_out1", [D, D])]
    wb_q = [dscr("wb_q%d" % l, [D, D]) for l in range(2)]
    wb_k = [dscr("wb_k%d" % l, [D, D]) for l in range(2)]
    wb_v = [dscr("wb_v%d" % l, [D, D]) for l in range(2)]
    wb_o = [dscr("wb_o%d" % l, [D, D]) for l in range(2)]
    wb_up = [dscr("wb_up%d" % l, [D, 4096]) for l in range(2)]
    wb_dn = [dscr("wb_dn%d" % l, [4096, D]) for l in range(2)]
    NP = [4, 8]
    DV = [512, 1024]
    KTp = [dscr("KTp%d" % l, [128, NP[l], T]) for l in range(2)]
    Vp = [dscr("Vp%d" % l, [T, DV[l]]) for l in range(2)]
    KTs = [[dscr("KTs%d_%d" % (l, j), [128, NP[l], NKS]) for j in range(NSEQ)] for l in range(2)]
    Vs = [[dscr("Vs%d_%d" % (l, j), [NKS, DV[l]]) for j in range(NSEQ)] for l in range(2)]
    MKs = [[dscr("MKs%d_%d" % (l, j), [128, 8, 256]) for j in range(NSEQ)] for l in range(2)]
    MVs = [[dscr("MVs%d_%d" % (l, j), [256, 1024]) for j in range(NSEQ)] for l in range(2)]

    _dm = int(os.environ.get("KDUMMY", "0"))
    if _dm:
        dummies = [dscr("dummy_scr%d" % i, [_dm * 1024, 512]) for i in range(int(os.environ.get("KDUMMYN", "1")))]
    def sb(name, shape, dt):
        return es.enter_context(nc.sbuf_tensor(name, list(shape), dt))

    XR = sb("XR", [128, 4, D], F32)
    HN = sb("HN", [128, 2, D], BF16)
    HT = sb("HT", [128, 8, 512], BF16)
    WS = sb("WS", [128, NWS, 8, 512], BF16)
    R32 = sb("R32", [128, 32, 512], BF16)
    ATT = sb("ATT", [128, 8, 512], BF16)
    MEMK = sb("MEMK", [128, 2, 8, 256], BF16)
    MEMV = sb("MEMV", [128, 2, 2, 1024], BF16)
    STG = sb("STG", [128, NSTG, D], F32)
    FS = sb("FS", [128, NFS, 512], F32)
    BS = sb("BS", [128, NBS, 512], BF16)
    ACC = sb("ACC", [128, 2, 512], BF16)
    KVK = sb("KVK", [128, NKV, 512], BF16)
    KVV0 = sb("KVV0", [128, NKV, 4, 128], BF16)
    KVV1 = sb("KVV1", [128, NKV, 4, 2, 128], BF16)
    MASK = sb("MASK", [128, 9, 512], BF16)
    CM = sb("CM", [128, 4, 128], BF16)
    IDF = sb("IDF", [128, 128], F32)
    ROPE = sb("ROPE", [128, 4, 256], F32)
    GF = sb("GF", [128, D], F32)
    SM = sb("SM", [128, 64], F32)
    CW = sb("CW", [128, 4, 3], F32)
    GN = sb("GN", [128, 9, 8], F32)
    CU = sb("CU", [128, 4, 520], F32)
    LAMT = sb("LAMT", [128, 4, 64], F32)
    PSB = [es.enter_context(nc.psum_tensor("PS%d" % b, [128, 512], F32)) for b in range(8)]

    IDB = CM[:, 0, :]
    TRI = CM[:, 1, :]
    ONES = CM[:, 2, :]

    def PS(b):
        return PSB[b][:, :]

    def PSbf(b):
        return PSB[b][:, :].bitcast(BF16)

    def MM(out, lhsT, rhs, start, stop):
        return lambda e: e.matmul(out, lhsT=lhsT, rhs=rhs, start=start, stop=stop, skip_group_check=True)

    def TR(out, in_, ident):
        return lambda e: e.transpose(out, in_, ident)

    def ACTF(out, in_, func, scale=1.0, bias=None, accum=None):
        def f(e):
            kw = {}
            if bias is not None:
                kw["bias"] = bias
            if accum is not None:
                kw["accum_out"] = accum
            return e.activation(out=out, in_=in_, func=func, scale=scale, **kw)
        return f

    def TT(out, a, b, op):
        return lambda e: e.tensor_tensor(out=out, in0=a, in1=b, op=op)

    def TS(out, a, s1, op0, s2=None, op1=None):
        if op1 is None:
            return lambda e: e.tensor_scalar(out=out, in0=a, scalar1=s1, scalar2=None, op0=op0)
        return lambda e: e.tensor_scalar(out=out, in0=a, scalar1=s1, scalar2=s2, op0=op0, op1=op1)

    def STT(out, a, scalar, b, op0, op1):
        return lambda e: e.scalar_tensor_tensor(out=out, in0=a, scalar=scalar, in1=b, op0=op0, op1=op1)

    def CP(out, in_):
        return lambda e: e.tensor_copy(out=out, in_=in_)

    def RECIP(out, in_):
        return lambda e: e.reciprocal(out=out, in_=in_)

    def MSET(ap, v):
        return lambda e: e.memset(ap, v)

    def DMA(out, in_, slow=False):
        if slow:
            return lambda e: e.dma_start(out=out, in_=in_, allow_slow_non_contiguous=True)
        return lambda e: e.dma_start(out=out, in_=in_)

    ctr = dict(fs=0, bs=0, stg=0, ps=0, tp=0, hn=0, evac=0)

    def nfs():
        ctr["fs"] = (ctr["fs"] + 1) % NFS
        return ctr["fs"]

    def nbs():
        ctr["bs"] = (ctr["bs"] + 1) % NBS
        return ctr["bs"]

    def nstg():
        ctr["stg"] = (ctr["stg"] + 1) % NSTG
        return ctr["stg"]

    def nps():
        ctr["ps"] = (ctr["ps"] + 1) % 6
        return ctr["ps"]

    def ntp():
        ctr["tp"] = (ctr["tp"] + 1) % 2
        return 6 + ctr["tp"]

    def evac_eng():
        ctr["evac"] += 1
        return "vector" if ctr["evac"] % 2 else "scalar"

    def copy_op(eng, out, in_, scale=None):
        if eng == "scalar":
            return ACTF(out, in_, AF.Copy, scale=1.0 if scale is None else scale)
        if scale is None:
            return CP(out, in_)
        return TS(out, in_, float(scale), ALU.mult)

    Rk = lambda i: ("R", i)

    panels = []
    wstate = dict(issued=0, cur=-1)

    def panel_ap(w, r0, c0):
        return w[r0:r0 + 1024, c0:c0 + 512].rearrange("(c p) n -> p c n", p=128)

    def block_panels():
        pl = []
        for c0 in (0, 512, 1024, 2048, 2560, 1536):
            pl.append(panel_ap(wb_in[0], 0, c0))
        for l in range(2):
            if l == 1:
                for c0 in range(0, 3072, 512):
                    pl.append(panel_ap(wb_in[1], 0, c0))
            for c0 in (0, 512):
                pl.append(panel_ap(wb_out[l], 0, c0))
            for c0 in (0, 512):
                pl.append(panel_ap(wb_q[l], 0, c0))
            for c0 in (0, 512):
                pl.append(panel_ap(wb_o[l], 0, c0))
            for c0 in range(0, 4096, 512):
                pl.append(panel_ap(wb_up[l], 0, c0))
            for c0 in (0, 512):
                for r0 in range(0, 4096, 1024):
                    pl.append(panel_ap(wb_dn[l], r0, c0))
        return pl

    def issue_panels(upto):
        while wstate["issued"] <= min(upto, len(panels) - 1):
            i = wstate["issued"]
            s = i % NWS
            S.add("sync", DMA(WS[:, s, :, :], panels[i]), reads=[("WSCR",)], writes=[("WS", s)], dsem=("WS", s))
            wstate["issued"] += 1

    def next_panel():
        wstate["cur"] += 1
        i = wstate["cur"]
        issue_panels(i + NWS - 1)
        return i % NWS

    def rms_to_HT(ntt):
        for tt in range(ntt):
            s = ctr["hn"] = (ctr["hn"] + 1) % 2
            S.add("scalar", ACTF(HN[:, s, :], XR[:, tt, :], AF.Square, accum=SM[:, tt:tt + 1]),
                  reads=[("XR", tt)], writes=[("HN", s), ("SMa", tt)])
            S.add("scalar", ACTF(SM[:, 4 + tt:5 + tt], SM[:, tt:tt + 1], AF.Ln, scale=1.0 / D, bias=EPSC[:, 0:1]),
                  reads=[("SMa", tt), ("SMK",)], writes=[("SMb", tt)])
            S.add("scalar", ACTF(SM[:, 8 + tt:9 + tt], SM[:, 4 + tt:5 + tt], AF.Exp, scale=-0.5),
                  reads=[("SMb", tt)], writes=[("SMc", tt)])
            S.add("vector", TS(HN[:, s, :], XR[:, tt, :], SM[:, 8 + tt:9 + tt], ALU.mult),
                  reads=[("XR", tt), ("SMc", tt)], writes=[("HN", s)])
            b = ntp()
            for c in range(8):
                S.add("tensor", TR(PSbf(b)[:, c * 128:(c + 1) * 128], HN[:, s, c * 128:(c + 1) * 128], IDB),
                      reads=[("HN", s)], writes=[("PS", b)], partial=True)
            eng = evac_eng()
            S.add(eng, copy_op(eng, HT[:, :, tt * 128:(tt + 1) * 128],
                               PSbf(b).rearrange("p (c t) -> p c t", t=128)),
                  reads=[("PS", b)], writes=[("HT", tt)])

    def linear_tm(ntt, lhs_fn, lhs_keys_fn, npanels, evac):
        for pi in range(npanels):
            s = next_panel()
            for tt in range(ntt):
                b = nps()
                for c in range(8):
                    S.add("tensor", MM(PS(b), lhs_fn(c, tt), WS[:, s, c, :], c == 0, c == 7),
                          reads=[("WS", s)] + lhs_keys_fn(c, tt), writes=[("PS", b)], partial=True)
                evac(pi, tt, b)

    def linear_fm(ntt, npanels, evac):
        TB = ntt * 128
        for pi in range(npanels):
            s = next_panel()
            for j in range(4):
                b = nps()
                for c in range(8):
                    S.add("tensor", MM(PS(b)[:, 0:TB], WS[:, s, c, j * 128:(j + 1) * 128], HT[:, c, 0:TB], c == 0, c == 7),
                          reads=[("WS", s)] + [("HT", t) for t in range(ntt)], writes=[("PS", b)], partial=True)
                evac(pi, j, b)

    HT_lhs = lambda c, tt: HT[:, c, tt * 128:(tt + 1) * 128]
    HT_keys = lambda c, tt: [("HT", tt)]
    ATT_lhs = lambda c, tt: ATT[:, c, tt * 128:(tt + 1) * 128]
    ATT_keys = lambda c, tt: [("ATT", c)]

    def resid_add_evac(ntt):
        def ev(pi, tt, b):
            S.add("vector", TT(XR[:, tt, pi * 512:(pi + 1) * 512], PS(b), XR[:, tt, pi * 512:(pi + 1) * 512], ALU.add),
                  reads=[("PS", b), ("XR", tt)], writes=[("XR", tt)])
        return ev

    def store_rows(dst_ap, stg_s, ncols):
        S.add("sync", DMA(dst_ap, STG[:, stg_s, 0:ncols]), reads=[("STG", stg_s)], writes=[("OUT",)],
              dsem=("STG", stg_s), partial=True)

    def transposes_to(src_ap_fn, src_keys, n, dst_ap, dst_keys, scale_neg_dst=None, neg_keys=None):
        b = ntp()
        for j in range(n):
            S.add("tensor", TR(PSbf(b)[:, j * 128:(j + 1) * 128], src_ap_fn(j), IDB),
                  reads=src_keys, writes=[("PS", b)], partial=True)
        src = PSbf(b)[:, 0:n * 128].rearrange("p (c t) -> p c t", t=128)
        S.add("vector", CP(dst_ap, src), reads=[("PS", b)], writes=dst_keys)
        if scale_neg_dst is not None:
            S.add("scalar", ACTF(scale_neg_dst, src, AF.Copy, scale=-1.0), reads=[("PS", b)], writes=neg_keys)

    kvstate = dict(n=0)

    def attention(kind, ntt, streams):
        TB = ntt * 128
        npairs = NP[kind]
        QT = lambda p: R32[:, p, :]
        QN = lambda p: R32[:, 8 + p, :]
        for p in range(npairs):
            qkeys = [Rk(p)] + ([Rk(8 + p)] if kind == 1 else [])
            first_o = True
            for (qc0, nq, KTd, Vd, chunks, maskfn) in streams:
                first_tile = True
                ntiles_total = sum((nkc + 127) // 128 for (_, nkc) in chunks)
                tcount = 0
                for (k0, nkc) in chunks:
                    kvstate["n"] += 1
                    sl = kvstate["n"] % NKV
                    S.add("sync", DMA(KVK[:, sl, 0:nkc], KTd[:, p, k0:k0 + nkc]),
                          reads=[("KVSCR",)], writes=[("KVK", sl)], dsem=("KV", sl), partial=True)
                    nt = (nkc + 127) // 128
                    if kind == 0:
                        if nkc >= 128:
                            S.add("sync", DMA(KVV0[:, sl, 0:nt, :],
                                              Vd[k0:k0 + nkc, p * 128:(p + 1) * 128].rearrange("(t q) d -> q t d", q=128)),
                                  reads=[("KVSCR",)], writes=[("KVV", sl)], dsem=("KV", sl), partial=True)
                        else:
                            S.add("sync", DMA(KVV0[0:nkc, sl, 0, :], Vd[k0:k0 + nkc, p * 128:(p + 1) * 128]),
                                  reads=[("KVSCR",)], writes=[("KVV", sl)], dsem=("KV", sl), partial=True)
                    else:
                        for a in range(2):
                            c0 = p * 128 + a * 64
                            if nkc >= 128:
                                S.add("sync", DMA(KVV1[:, sl, 0:nt, a, a * 64:(a + 1) * 64],
                                                  Vd[k0:k0 + nkc, c0:c0 + 64].rearrange("(t q) d -> q t d", q=128)),
                                      reads=[("KVSCR",)], writes=[("KVV", sl)], dsem=("KV", sl), partial=True)
                            else:
                                S.add("sync", DMA(KVV1[0:nkc, sl, 0, a, a * 64:(a + 1) * 64], Vd[k0:k0 + nkc, c0:c0 + 64]),
                                      reads=[("KVSCR",)], writes=[("KVV", sl)], dsem=("KV", sl), partial=True)
                    for t in range(nt - 1, -1, -1):
                        nk = min(128, nkc - t * 128)
                        kc = slice(t * 128, t * 128 + nk)
                        mask = maskfn(k0 + t * 128)
                        tcount += 1
                        last_tile = tcount == ntiles_total
                        for a in range(2):
                            pa = slice(64 * a, 64 * a + 64)
                            if kind == 0:
                                zb = 4 + (ctr["ps"] % 4)
                                ctr["ps"] += 1
                                Z = PS(zb)[0:nk, qc0:qc0 + nq]
                                S.add("tensor", MM(Z, KVK[pa, sl, kc], QT(p)[pa, qc0:qc0 + nq], True, True),
                                      reads=[("KVK", sl)] + qkeys, writes=[("PS", zb)])
                                pb = nbs()
                                Pt = BS[0:nk, pb, 0:nq]
                                S.add("scalar", ACTF(Pt, Z, AF.Exp), reads=[("PS", zb)], writes=[("BS", pb)])
                                if mask is not None:
                                    S.add("gpsimd", TT(Pt, Pt, mask[0:nk, qc0:qc0 + nq], ALU.mult),
                                          reads=[("BS", pb), ("MASK",)], writes=[("BS", pb)])
                                S.add("tensor", MM(PS(a)[:, qc0:qc0 + nq], KVV0[0:nk, sl, t, :], Pt, first_tile, last_tile),
                                      reads=[("KVV", sl), ("BS", pb)], writes=[("PS", a)], partial=True)
                                S.add("tensor", MM(PS(2 + a)[:, qc0:qc0 + nq], ONES[0:nk, :], Pt, first_tile, last_tile),
                                      reads=[("BS", pb), ("CM",)], writes=[("PS", 2 + a)], partial=True)
                            else:
                                zb = 1 + (ctr["ps"] % 3)
                                cb = 4 + (ctr["ps"] % 4)
                                ctr["ps"] += 1
                                Z = PS(zb)[0:nk, qc0:qc0 + nq]
                                C = PS(cb)[0:nk, qc0:qc0 + nq]
                                S.add("tensor", MM(Z, KVK[pa, sl, kc], QT(p)[pa, qc0:qc0 + nq], True, True),
                                      reads=[("KVK", sl)] + qkeys, writes=[("PS", zb)])
                                fe = nfs()
                                E = FS[0:nk, fe, 0:nq]
                                S.add("scalar", ACTF(E, Z, AF.Exp), reads=[("PS", zb)], writes=[("FS", fe)])
                                lb = nbs()
                                Lt = BS[0:nk, lb, 0:nq]
                                S.add("scalar", ACTF(Lt, E, AF.Ln, bias=ONEC[0:nk, 0:1]), reads=[("FS", fe), ("SMK",)], writes=[("BS", lb)])
                                if mask is not None:
                                    S.add("gpsimd", TT(Lt, Lt, mask[0:nk, qc0:qc0 + nq], ALU.mult),
                                          reads=[("BS", lb), ("MASK",)], writes=[("BS", lb)])
                                S.add("tensor", MM(C, TRI[0:nk, 0:nk], Lt, True, False),
                                      reads=[("BS", lb), ("CM",)], writes=[("PS", cb)])
                                if not first_tile:
                                    S.add("tensor", MM(C, ONES[:, 0:nk], ACC[:, a, 0:nq], False, False),
                                          reads=[("ACC", a), ("CM",)], writes=[("PS", cb)], partial=True)
                                S.add("tensor", MM(C, KVK[pa, sl, kc], QN(p)[pa, qc0:qc0 + nq], False, True),
                                      reads=[("KVK", sl)] + qkeys, writes=[("PS", cb)], partial=True)
                                wb = nbs()
                                Wt = BS[0:nk, wb, 0:nq]
                                S.add("scalar", ACTF(Wt, C, AF.Exp, scale=-1.0), reads=[("PS", cb)], writes=[("BS", wb)])
                                if mask is not None:
                                    S.add("gpsimd", TT(Wt, Wt, mask[0:nk, qc0:qc0 + nq], ALU.mult),
                                          reads=[("BS", wb), ("MASK",)], writes=[("BS", wb)])
                                S.add("tensor", MM(PS(0)[:, qc0:qc0 + nq], KVV1[0:nk, sl, t, a, :], Wt,
                                                   first_tile and a == 0, last_tile and a == 1),
                                      reads=[("KVV", sl), ("BS", wb)], writes=[("PS", 0)], partial=True)
                                if not last_tile:
                                    if first_tile:
                                        if nk < 128:
                                            S.add("gpsimd", MSET(ACC[:, a, :], 0.0), writes=[("ACC", a)])
                                        S.add("gpsimd", CP(ACC[0:nk, a, 0:nq], Lt), reads=[("BS", lb)], writes=[("ACC", a)])
                                    else:
                                        S.add("gpsimd", TT(ACC[0:nk, a, 0:nq], ACC[0:nk, a, 0:nq], Lt, ALU.add),
                                              reads=[("BS", lb), ("ACC", a)], writes=[("ACC", a)])
                        first_tile = False
            if kind == 0:
                osb = []
                for a in range(2):
                    r = nfs()
                    S.add("vector", RECIP(FS[:, r, 0:TB], PS(2 + a)[:, 0:TB]), reads=[("PS", 2 + a)], writes=[("FS", r)])
                    o = nfs()
                    S.add("vector", TT(FS[:, o, 0:TB], PS(a)[:, 0:TB], FS[:, r, 0:TB], ALU.mult),
                          reads=[("PS", a), ("FS", r)], writes=[("FS", o)])
                    osb.append(o)
                oc = nfs()
                S.add("vector", STT(FS[:, oc, 0:TB], FS[:, osb[1], 0:TB], NLAM[:, 0:1], FS[:, osb[0], 0:TB], ALU.mult, ALU.add),
                      reads=[("FS", osb[0]), ("FS", osb[1]), ("LAM",)], writes=[("FS", oc)])
                sq = nbs()
                S.add("gpsimd", TT(BS[:, sq, 0:TB], FS[:, oc, 0:TB], FS[:, oc, 0:TB], ALU.mult),
                      reads=[("FS", oc)], writes=[("BS", sq)])
                S.add("tensor", MM(PS(4)[:, 0:TB], ONES, BS[:, sq, 0:TB], True, True),
                      reads=[("BS", sq), ("CM",)], writes=[("PS", 4)])
                ln = nfs()
                S.add("scalar", ACTF(FS[:, ln, 0:TB], PS(4)[:, 0:TB], AF.Ln, scale=1.0 / 128, bias=SEPSC[:, 0:1]),
                      reads=[("PS", 4), ("SMK",)], writes=[("FS", ln)])
                rs = nfs()
                S.add("scalar", ACTF(FS[:, rs, 0:TB], FS[:, ln, 0:TB], AF.Exp, scale=-0.5),
                      reads=[("FS", ln)], writes=[("FS", rs)])
                S.add("vector", STT(ATT[:, p, 0:TB], FS[:, oc, 0:TB], GSUB[:, 0:1], FS[:, rs, 0:TB], ALU.mult, ALU.mult),
                      reads=[("FS", oc), ("FS", rs), ("GSUB",)], writes=[("ATT", p)])
            else:
                S.add("vector", CP(ATT[:, p, 0:TB], PS(0)[:, 0:TB]), reads=[("PS", 0)], writes=[("ATT", p)])

    def mem_attention(ntt, l, seqs):
        TB = ntt * 128
        rms_to_HT(ntt)

        def evq(pi, j, b):
            eng = evac_eng()
            S.add(eng, copy_op(eng, R32[:, 4 * pi + j, 0:TB], PS(b)[:, 0:TB], scale=1.0 / 16),
                  reads=[("PS", b)], writes=[Rk(4 * pi + j)])
        linear_fm(ntt, 2, evq)
        for h in range(4):
            pts = []
            for kt in range(2):
                for (qc0, nq, ms) in seqs:
                    for dc in range(2):
                        S.add("tensor", MM(PS(kt)[:, qc0:qc0 + nq], MEMK[:, ms, 2 * h + dc, kt * 128:(kt + 1) * 128],
                                           R32[:, 2 * h + dc, qc0:qc0 + nq], dc == 0, dc == 1),
                              reads=[("MEMK", ms), Rk(2 * h + dc)], writes=[("PS", kt)], partial=True)
                pb = nbs()
                S.add("scalar", ACTF(BS[:, pb, 0:TB], PS(kt)[:, 0:TB], AF.Exp), reads=[("PS", kt)], writes=[("BS", pb)])
                pts.append(pb)
            for (qc0, nq, ms) in seqs:
                for dvc in range(2):
                    for kt in range(2):
                        S.add("tensor", MM(PS(2 + dvc)[:, qc0:qc0 + nq],
                                           MEMV[:, ms, kt, h * 256 + dvc * 128:h * 256 + (dvc + 1) * 128],
                                           BS[:, pts[kt], qc0:qc0 + nq], kt == 0, kt == 1),
                              reads=[("MEMV", ms), ("BS", pts[kt])], writes=[("PS", 2 + dvc)], partial=True)
                for kt in range(2):
                    S.add("tensor", MM(PS(4)[:, qc0:qc0 + nq], ONES, BS[:, pts[kt], qc0:qc0 + nq], kt == 0, kt == 1),
                          reads=[("BS", pts[kt]), ("CM",)], writes=[("PS", 4)], partial=True)
            r = nfs()
            S.add("vector", RECIP(FS[:, r, 0:TB], PS(4)[:, 0:TB]), reads=[("PS", 4)], writes=[("FS", r)])
            for dvc in range(2):
                S.add("vector", TT(ATT[:, 2 * h + dvc, 0:TB], PS(2 + dvc)[:, 0:TB], FS[:, r, 0:TB], ALU.mult),
                      reads=[("PS", 2 + dvc), ("FS", r)], writes=[("ATT", 2 * h + dvc)])
        linear_tm(ntt, ATT_lhs, ATT_keys, 2, resid_add_evac(ntt))

    def ffn(ntt):
        TB = ntt * 128
        rms_to_HT(ntt)

        def evu(pi, j, b):
            f = nfs()
            S.add("scalar", ACTF(FS[:, f, 0:TB], PS(b)[:, 0:TB], AF.Relu), reads=[("PS", b)], writes=[("FS", f)])
            S.add("vector", TT(R32[:, 4 * pi + j, 0:TB], FS[:, f, 0:TB], FS[:, f, 0:TB], ALU.mult),
                  reads=[("FS", f)], writes=[Rk(4 * pi + j)])
        linear_fm(ntt, 8, evu)
        for half in range(2):
            for kq in range(4):
                s = next_panel()
                for tt in range(ntt):
                    for c in range(8):
                        S.add("tensor", MM(PS(tt), R32[:, kq * 8 + c, tt * 128:(tt + 1) * 128], WS[:, s, c, :],
                                           kq == 0 and c == 0, kq == 3 and c == 7),
                              reads=[("WS", s), Rk(kq * 8 + c)], writes=[("PS", tt)], partial=True)
            for tt in range(ntt):
                S.add("vector", TT(XR[:, tt, half * 512:(half + 1) * 512], PS(tt), XR[:, tt, half * 512:(half + 1) * 512], ALU.add),
                      reads=[("PS", tt), ("XR", tt)], writes=[("XR", tt)])

    EPSC = SM[:, 32:33]; SEPSC = SM[:, 33:34]; ONEC = SM[:, 34:35]; NLAM = SM[:, 35:36]; GSUB = SM[:, 36:37]
    S.add("gpsimd", MSET(SM[:, 32:33], EPS), writes=[("SMK",)])
    S.add("gpsimd", MSET(SM[:, 33:34], SUBLN_EPS), writes=[("SMK",)], partial=True)
    S.add("gpsimd", MSET(SM[:, 34:35], 1.0), writes=[("SMK",)], partial=True)
    S.add("gpsimd", MSET(KVV1[:, :, :, :, :], 0.0), writes=[("KVV", i) for i in range(NKV)])
    S.add("gpsimd", MSET(CU[:, :, :], 0.0), writes=[("CU", j) for j in range(4)])
    S.add("sync", DMA(FS[:, 0, :], c_mats.rearrange("p a b -> p (a b)")), writes=[("FS", 0)], dsem="c0")
    S.add("vector", CP(CM[:, :, :].rearrange("p a b -> p (a b)"), FS[:, 0, :]), reads=[("FS", 0)], writes=[("CM",)])
    S.add("vector", CP(IDF[:, :], FS[:, 0, 0:128]), reads=[("FS", 0)], writes=[("IDF",)])
    for m in range(9):
        f = nfs()
        S.add("sync", DMA(FS[:, f, :], c_mask[:, m, :]), writes=[("FS", f)], dsem=("FS", f))
        S.add("vector", CP(MASK[:, m, :], FS[:, f, :]), reads=[("FS", f)], writes=[("MASK",)], partial=True)
    S.add("sync", DMA(GF[:, :], nfin.partition_broadcast(128)), writes=[("GF",)], dsem="c1")
    S.add("sync", DMA(GSUB, subln), writes=[("GSUB0",)], dsem="c1")
    S.add("sync", DMA(LAMT[:, :, :].rearrange("p a b -> p (a b)"), lam4.rearrange("a b -> (a b)").partition_broadcast(128)),
          writes=[("LAMT",)], dsem="c1")
    gl = [nmix[0:1, :], nmix[1:2, :], ncross[0:1, :], ncross[1:2, :], nffn[0:1, :], nffn[1:2, :], nmem[0:1, :], nmem[1:2, :]]
    for gi, g in enumerate(gl):
        S.add("sync", DMA(GN[:, gi, :], g.rearrange("o (c p) -> p (o c)", p=128), slow=True),
              writes=[("GN",)], dsem="c1", partial=True)
    for j in range(3):
        S.add("sync", DMA(CW[:, :, j], convw[j:j + 1, :].rearrange("o (c p) -> p (o c)", p=128), slow=True),
              writes=[("CW",)], dsem="c1", partial=True)

    S.add("vector", TS(GSUB, GSUB, 1.0 - LAM_INIT0, ALU.mult), reads=[("GSUB0",)], writes=[("GSUB",)])
    S.add("vector", TT(LAMT[:, 0, :], LAMT[:, 0, :], LAMT[:, 1, :], ALU.mult), reads=[("LAMT",)], writes=[("LAMa",)])
    S.add("vector", TT(LAMT[:, 2, :], LAMT[:, 2, :], LAMT[:, 3, :], ALU.mult), reads=[("LAMT",)], writes=[("LAMb",)])
    S.add("vector", lambda e: e.reduce_sum(out=SM[:, 40:41], in_=LAMT[:, 0, :], axis=mybir.AxisListType.X),
          reads=[("LAMa",)], writes=[("LAMc",)])
    S.add("vector", lambda e: e.reduce_sum(out=SM[:, 41:42], in_=LAMT[:, 2, :], axis=mybir.AxisListType.X),
          reads=[("LAMb",)], writes=[("LAMd",)])
    S.add("scalar", ACTF(SM[:, 42:44], SM[:, 40:42], AF.Exp), reads=[("LAMc",), ("LAMd",)], writes=[("LAMe",)])
    S.add("vector", TT(SM[:, 44:45], SM[:, 43:44], SM[:, 42:43], ALU.subtract), reads=[("LAMe",)], writes=[("LAMf",)])
    S.add("vector", TS(NLAM, SM[:, 44:45], -LAM_INIT0, ALU.add), reads=[("LAMf",)], writes=[("LAM",)])
    if _dm:
        for dummy in dummies:
            S.add("sync", DMA(dummy[0:128, :], MASK[:, 0, :]), reads=[("MASK",)], writes=[("DUMMY",)], dsem="c0", partial=True)
            S.add("sync", DMA(dummy[_dm * 1024 - 128:_dm * 1024, :], MASK[:, 0, :]), reads=[("MASK",)], writes=[("DUMMY",)], dsem="c0", partial=True)
    ck(10)
    ctasks = []

    def cast_weight(src, dst, K, N, gi):
        for c in range(K // 128):
            for n0 in range(0, N, 2048):
                ctasks.append((src, dst, c, n0, min(2048, N - n0), gi))

    for l in range(2):
        cast_weight(wk[l], wb_k[l], D, D, 6 + l)
        cast_weight(wv[l], wb_v[l], D, D, 6 + l)
    cast_weight(w_in0, wb_in[0], D, 3072, 0)
    cast_weight(w_out0, wb_out[0], D, D, None)
    cast_weight(w_in1, wb_in[1], D, 3072, 1)
    cast_weight(w_out1, wb_out[1], D, D, None)
    for l in range(2):
        cast_weight(wq[l], wb_q[l], D, D, 2 + l)
        cast_weight(wo[l], wb_o[l], D, D, None)
        cast_weight(wup[l], wb_up[l], D, 4096, 4 + l)
        cast_weight(wdn[l], wb_dn[l], 4096, D, None)

    def c_fin(i):
        s = i % 2
        return R32[:, 8 * s:8 * s + 8, :].rearrange("p a b -> p (a b)").bitcast(F32)

    def c_fout(i):
        s = i % 2
        return R32[:, 16 + 4 * s:20 + 4 * s, :].rearrange("p a b -> p (a b)")

    def c_load(i):
        src, dst, c, n0, w, gi = ctasks[i]
        S.add("sync", DMA(c_fin(i)[:, 0:w], src[c * 128:(c + 1) * 128, n0:n0 + w]), writes=[("CI", i % 2)], dsem=("CI", i % 2))

    c_load(0)
    for i in range(len(ctasks)):
        src, dst, c, n0, w, gi = ctasks[i]
        s = i % 2
        if i + 1 < len(ctasks):
            c_load(i + 1)
        eng = ["vector", "gpsimd", "scalar"][i % 3]
        fin, fout = c_fin(i), c_fout(i)
        if gi is None:
            S.add(eng, copy_op(eng, fout[:, 0:w], fin[:, 0:w]), reads=[("CI", s)], writes=[("CO", s)])
        else:
            gcol = GN[:, gi, (c % 8):(c % 8) + 1]
            if eng == "scalar":
                S.add(eng, ACTF(fout[:, 0:w], fin[:, 0:w], AF.Copy, scale=gcol), reads=[("CI", s), ("GN",)], writes=[("CO", s)])
            else:
                S.add(eng, TS(fout[:, 0:w], fin[:, 0:w], gcol, ALU.mult), reads=[("CI", s), ("GN",)], writes=[("CO", s)])
        S.add("sync", DMA(dst[c * 128:(c + 1) * 128, n0:n0 + w], fout[:, 0:w]), reads=[("CO", s)], writes=[("WSCR",)],
              dsem=("CO", s), partial=True)

    ck(20)
    for l in range(2):
        for w in (wb_k[l], wb_v[l]):
            for c0 in (0, 512):
                panels.append(panel_ap(w, 0, c0))
    bp = block_panels()
    for _ in range(NB + 1):
        panels.extend(bp)

    S.add("sync", DMA(XR[:, 0:2, :], memp.rearrange("(t p) d -> p t d", p=128)), writes=[("XR", 0), ("XR", 1)], dsem="XR")
    ck(21)
    rms_to_HT(2)
    ck(22)
    for l in range(2):
        def evk(pi, tt, b, l=l):
            s = nstg()
            S.add("scalar", ACTF(STG[:, s, 0:512], PS(b), AF.Copy), reads=[("PS", b)], writes=[("STG", s)])
            store_rows(pmk[l, tt * 128:(tt + 1) * 128, pi * 512:(pi + 1) * 512], s, 512)
            kb = nbs()
            S.add("vector", CP(BS[:, kb, :], PS(b)), reads=[("PS", b)], writes=[("BS", kb)])
            transposes_to(lambda j: BS[:, kb, j * 128:(j + 1) * 128], [("BS", kb)], 4,
                          MEMK[:, l, 4 * pi:4 * pi + 4, tt * 128:(tt + 1) * 128], [("MEMK", l)])
        linear_tm(2, HT_lhs, HT_keys, 2, evk)
        ck(23 + 2 * l)

        def evv(pi, tt, b, l=l):
            s = nstg()
            S.add("scalar", ACTF(STG[:, s, 0:512], PS(b), AF.Copy), reads=[("PS", b)], writes=[("STG", s)])
            store_rows(pmv[l, tt * 128:(tt + 1) * 128, pi * 512:(pi + 1) * 512], s, 512)
            S.add("vector", CP(MEMV[:, l, tt, pi * 512:(pi + 1) * 512], PS(b)), reads=[("PS", b)], writes=[("MEMV", l)])
        linear_tm(2, HT_lhs, HT_keys, 2, evv)

    ck(30)
    def prep_cache(l, ck, cv):
        npair = NP[l]
        dv = DV[l]
        for j in range(NSEQ):
            for k0 in range(0, PAST, 512):
                for t in range(4):
                    s = nstg()
                    S.add("sync", DMA(STG[:, s, 0:dv], ck[j, k0 + t * 128:k0 + (t + 1) * 128, :]), writes=[("STG", s)], dsem=("STG", s))
                    for g in range(npair // 4):
                        b = nps()
                        for q in range(4):
                            pq = g * 4 + q
                            S.add("tensor", TR(PS(b)[:, q * 128:(q + 1) * 128], STG[:, s, pq * 128:(pq + 1) * 128], IDF[:, :]),
                                  reads=[("STG", s), ("IDF",)], writes=[("PS", b)], partial=True)
                        eng = evac_eng()
                        S.add(eng, copy_op(eng, R32[:, 16 + g * 4:16 + g * 4 + 4, t * 128:(t + 1) * 128],
                                           PS(b).rearrange("p (c t) -> p c t", t=128)),
                              reads=[("PS", b)], writes=[Rk(16 + g * 4 + q) for q in range(4)], partial=True)
                    s2 = nstg()
                    S.add("sync", DMA(STG[:, s2, 0:dv], cv[j, k0 + t * 128:k0 + (t + 1) * 128, :]), writes=[("STG", s2)], dsem=("STG", s2))
                    vo = R32[:, 24 + 2 * t:26 + 2 * t, :].rearrange("p a b -> p (a b)")
                    eng = evac_eng()
                    S.add(eng, copy_op(eng, vo[:, 0:dv], STG[:, s2, 0:dv]), reads=[("STG", s2)], writes=[Rk(24 + 2 * t), Rk(25 + 2 * t)])
                S.add("sync", DMA(KTs[l][j][:, :, k0:k0 + 512], R32[:, 16:16 + npair, :]),
                      reads=[Rk(16 + q) for q in range(npair)], writes=[("KVSCR",)], dsem="KTO", partial=True)
                S.add("sync", DMA(Vs[l][j][k0:k0 + 512, :].rearrange("(t p) d -> p t d", p=128),
                                  R32[:, 24:32, :].rearrange("p (t a) b -> p t (a b)", a=2)[:, :, 0:dv]),
                      reads=[Rk(24 + q) for q in range(8)], writes=[("KVSCR",)], dsem="VO", partial=True)

    prep_cache(0, cdk, cdv)
    ck(33)
    prep_cache(1, csk, csv)
    ck(35)
    for l in range(2):
        for j in range(NSEQ):
            for t in range(2):
                s = nstg()
                S.add("sync", DMA(STG[:, s, :], cmk[l, j, t * 128:(t + 1) * 128, :]), writes=[("STG", s)], dsem=("STG", s))
                for g in range(2):
                    b = nps()
                    for q in range(4):
                        pq = g * 4 + q
                        S.add("tensor", TR(PS(b)[:, q * 128:(q + 1) * 128], STG[:, s, pq * 128:(pq + 1) * 128], IDF[:, :]),
                              reads=[("STG", s), ("IDF",)], writes=[("PS", b)], partial=True)
                    eng = evac_eng()
                    S.add(eng, copy_op(eng, R32[:, 16 + g * 4:16 + g * 4 + 4, t * 128:(t + 1) * 128],
                                       PS(b).rearrange("p (c t) -> p c t", t=128)),
                          reads=[("PS", b)], writes=[Rk(16 + g * 4 + q) for q in range(4)], partial=True)
                s2 = nstg()
                S.add("sync", DMA(STG[:, s2, :], cmv[l, j, t * 128:(t + 1) * 128, :]), writes=[("STG", s2)], dsem=("STG", s2))
                vo = R32[:, 24 + 2 * t:26 + 2 * t, :].rearrange("p a b -> p (a b)")
                eng = evac_eng()
                S.add(eng, copy_op(eng, vo, STG[:, s2, :]), reads=[("STG", s2)], writes=[Rk(24 + 2 * t), Rk(25 + 2 * t)])
            S.add("sync", DMA(MKs[l][j][:, :, :], R32[:, 16:24, 0:256]),
                  reads=[Rk(16 + q) for q in range(8)], writes=[("MSCR",)], dsem="KTO", partial=True)
            S.add("sync", DMA(MVs[l][j][:, :].rearrange("(t p) d -> p t d", p=128),
                              R32[:, 24:28, :].rearrange("p (t a) b -> p t (a b)", a=2)),
                  reads=[Rk(24 + q) for q in range(4)], writes=[("MSCR",)], dsem="VO", partial=True)

    ck(38)
    def run_block(bi):
        sample = bi == NB
        ntt = 1 if sample else 4
        TB = ntt * 128
        nseq, L = (NSEQ, LS) if sample else (1, 512)
        t0 = bi * 512
        xsrc = xs if sample else xp[t0:t0 + TB, :]
        S.add("sync", DMA(XR[:, 0:ntt, :], xsrc.rearrange("(t p) d -> p t d", p=128)),
              writes=[("XR", t) for t in range(ntt)], dsem="XR")
        rrow = T if sample else t0
        S.add("sync", DMA(ROPE[:, 0:ntt, :], c_rope[rrow:rrow + TB, :].rearrange("(t p) d -> p t d", p=128)),
              writes=[("ROPE",)], dsem="ROPE")
        if sample:
            S.add("sync", DMA(FS[0:8, 0, :], sconv), writes=[("FS", 0)], dsem=("FS", 0))
            b = nps()
            for j in range(4):
                S.add("tensor", TR(PS(b)[:, j * 8:(j + 1) * 8], FS[0:8, 0, j * 128:(j + 1) * 128], IDF[0:8, 0:8]),
                      reads=[("FS", 0), ("IDF",)], writes=[("PS", b)], partial=True)
            S.add("vector", CP(CU[:, :, 0:NSEQ * 34].rearrange("p j (s x) -> p j s x", x=34)[:, :, :, 0:2],
                               PS(b)[:, 0:32].rearrange("p (j s x) -> p j s x", j=4, x=2)),
                  reads=[("PS", b)], writes=[("CU", j) for j in range(4)])
        W34 = L + 2
        cu3 = lambda j: CU[:, j, 0:nseq * W34].rearrange("p (s x) -> p s x", x=W34)
        v3 = lambda ap: ap.rearrange("p (s x) -> p s x", x=L)

        rms_to_HT(ntt)
        rq = lambda tt, a, b_: ROPE[:, tt, a:b_]

        def rope(tt, b, base, out_ap, okeys):
            x3 = PS(b).rearrange("p (h d) -> p h d", d=64)
            fa = nfs(); fb = nfs()
            A3 = FS[:, fa, :].rearrange("p (h d) -> p h d", d=64)
            B3 = FS[:, fb, :].rearrange("p (h d) -> p h d", d=64)
            cosb = rq(tt, base, base + 64).unsqueeze(1).to_broadcast([128, 8, 64])
            s1 = rq(tt, base + 64, base + 96).unsqueeze(1).to_broadcast([128, 8, 32])
            s2 = rq(tt, base + 96, base + 128).unsqueeze(1).to_broadcast([128, 8, 32])
            S.add("vector", TT(A3, x3, cosb, ALU.mult), reads=[("PS", b), ("ROPE",)], writes=[("FS", fa)])
            S.add("vector", TT(B3[:, :, 0:32], x3[:, :, 32:64], s1, ALU.mult), reads=[("PS", b), ("ROPE",)], writes=[("FS", fb)])
            S.add("vector", TT(B3[:, :, 32:64], x3[:, :, 0:32], s2, ALU.mult), reads=[("PS", b), ("ROPE",)], writes=[("FS", fb)], partial=True)
            S.add("gpsimd", TT(out_ap, FS[:, fa, :], FS[:, fb, :], ALU.add), reads=[("FS", fa), ("FS", fb)], writes=okeys)

        def ev_q0(pi, tt, b):
            qb = nbs()
            rope(tt, b, 0, BS[:, qb, :], [("BS", qb)])
            transposes_to(lambda j: BS[:, qb, j * 128:(j + 1) * 128], [("BS", qb)], 4,
                          R32[:, 0:4, tt * 128:(tt + 1) * 128], [Rk(q) for q in range(4)])
        linear_tm(ntt, HT_lhs, HT_keys, 1, ev_q0)

        kdst = sdk if sample else pdk[t0:t0 + TB, :]
        vdst = sdv if sample else pdv[t0:t0 + TB, :]

        def ev_k0(pi, tt, b):
            s = nstg()
            rope(tt, b, 128, STG[:, s, 0:512], [("STG", s)])
            store_rows(kdst[tt * 128:(tt + 1) * 128, :], s, 512)
            kb = nbs()
            S.add("scalar", ACTF(BS[:, kb, :], STG[:, s, 0:512], AF.Copy), reads=[("STG", s)], writes=[("BS", kb)])
            transposes_to(lambda j: BS[:, kb, j * 128:(j + 1) * 128], [("BS", kb)], 4,
                          R32[:, 16:20, tt * 128:(tt + 1) * 128], [Rk(16 + q) for q in range(4)])
        linear_tm(ntt, HT_lhs, HT_keys, 1, ev_k0)

        def VO(tt):
            return R32[:, 24 + 2 * tt:26 + 2 * tt, :].rearrange("p a b -> p (a b)")

        def ev_v0(pi, tt, b):
            s = nstg()
            S.add("scalar", ACTF(STG[:, s, 0:512], PS(b), AF.Copy), reads=[("PS", b)], writes=[("STG", s)])
            store_rows(vdst[tt * 128:(tt + 1) * 128, :], s, 512)
            S.add("vector", CP(VO(tt)[:, 0:512], PS(b)), reads=[("PS", b)], writes=[Rk(24 + 2 * tt), Rk(25 + 2 * tt)])
        linear_tm(ntt, HT_lhs, HT_keys, 1, ev_v0)

        def store_kv(l):
            npair, dv = NP[l], DV[l]
            if not sample:
                S.add("sync", DMA(KTp[l][:, :, t0:t0 + TB], R32[:, 16:16 + npair, 0:TB]),
                      reads=[Rk(16 + q) for q in range(npair)], writes=[("KVSCR",)], dsem="KTO", partial=True)
                S.add("sync", DMA(Vp[l][t0:t0 + TB, :].rearrange("(t p) d -> p t d", p=128),
                                  R32[:, 24:32, :].rearrange("p (t a) b -> p t (a b)", a=2)[:, :, 0:dv]),
                      reads=[Rk(24 + q) for q in range(8)], writes=[("KVSCR",)], dsem="VO", partial=True)
            else:
                for j in range(NSEQ):
                    S.add("sync", DMA(KTs[l][j][:, :, PAST:PAST + LS], R32[:, 16:16 + npair, j * LS:(j + 1) * LS]),
                          reads=[Rk(16 + q) for q in range(npair)], writes=[("KVSCR",)], dsem="KTO", partial=True)
                    S.add("sync", DMA(Vs[l][j][PAST:PAST + LS, :], VO(0)[j * LS:(j + 1) * LS, 0:dv]),
                          reads=[Rk(24), Rk(25)], writes=[("KVSCR",)], dsem="VO", partial=True)
        store_kv(0)

        def ev_gc(pi, j, b):
            S.add("scalar", ACTF(FS[:, j, 0:TB], PS(b)[:, 0:TB], AF.Copy), reads=[("PS", b)], writes=[("FS", j)])
        ctr["fs"] = 3
        linear_fm(ntt, 1, ev_gc)

        def ev_u(pi, j, b):
            S.add("vector", TT(cu3(j)[:, :, 2:2 + L], v3(PS(b)[:, 0:TB]), v3(FS[:, j, 0:TB]), ALU.mult),
                  reads=[("PS", b), ("FS", j)], writes=[("CU", j)])
            S.add("gpsimd", TS(v3(FS[:, j, 0:TB]), cu3(j)[:, :, 2:2 + L], CW[:, j, 2:3], ALU.mult),
                  reads=[("CU", j), ("CW",)], writes=[("FS", j)])
            S.add("vector", STT(v3(FS[:, j, 0:TB]), cu3(j)[:, :, 1:1 + L], CW[:, j, 1:2], v3(FS[:, j, 0:TB]), ALU.mult, ALU.add),
                  reads=[("CU", j), ("CW",), ("FS", j)], writes=[("FS", j)])
            S.add("vector", STT(v3(FS[:, j, 0:TB]), cu3(j)[:, :, 0:L], CW[:, j, 0:1], v3(FS[:, j, 0:TB]), ALU.mult, ALU.add),
                  reads=[("CU", j), ("CW",), ("FS", j)], writes=[("FS", j)])
        ctr["fs"] = 3
        linear_fm(ntt, 1, ev_u)

        def ev_gb(pi, j, b):
            S.add("vector", TT(ATT[:, 4 + j, 0:TB], PS(b)[:, 0:TB], FS[:, j, 0:TB], ALU.mult),
                  reads=[("PS", b), ("FS", j)], writes=[("ATT", 4 + j)])
        linear_fm(ntt, 1, ev_gb)
        if sample or bi == NB - 1:
            b = nps()
            nr = 2 * nseq
            f = 4
            for j in range(4):
                S.add("gpsimd", CP(FS[:, f, j * 8:j * 8 + nr].rearrange("p (s x) -> p s x", x=2), cu3(j)[:, :, L:L + 2]),
                      reads=[("CU", j)], writes=[("FS", f)], partial=True)
            for j in range(4):
                S.add("tensor", TR(PS(b)[0:nr, j * 128:(j + 1) * 128], FS[:, f, j * 8:j * 8 + nr], IDF[:, :]),
                      reads=[("FS", f), ("IDF",)], writes=[("PS", b)], partial=True)
            s = nstg()
            S.add("scalar", ACTF(STG[0:nr, s, 0:512], PS(b)[0:nr, :], AF.Copy), reads=[("PS", b)], writes=[("STG", s)])
            S.add("sync", DMA((sconvo if sample else pconv)[:, :], STG[0:nr, s, 0:512]), reads=[("STG", s)], writes=[("OUT",)],
                  dsem=("STG", s), partial=True)
        if not sample:
            for j in range(4):
                S.add("gpsimd", CP(CU[:, j, 0:2], CU[:, j, 512:514]), reads=[("CU", j)], writes=[("CU", j)])

        def streams_for(l):
            if not sample:
                chunks = [(kc * 512, 512) for kc in range(bi, -1, -1)]
                mbase = 4 if l == 0 else 0

                def maskfn(k0t, mbase=mbase):
                    if k0t >= t0:
                        return MASK[:, mbase + (k0t - t0) // 128, :]
                    return None
                return [(0, 512, KTp[l], Vp[l], chunks, maskfn)]
            st = []
            chunks = [(PAST, LS)] + [(k0, 512) for k0 in range(PAST - 512, -1, -512)]
            for j in range(NSEQ):
                if l == 0:
                    mf = lambda k0t: None
                else:
                    def mf(k0t, j=j):
                        if k0t >= PAST:
                            return MASK[:, 8, :]
                        return None
                st.append((j * LS, LS, KTs[l][j], Vs[l][j], chunks, mf))
            return st

        if sample:
            pass
        attention(0, ntt, streams_for(0))
        linear_tm(ntt, ATT_lhs, ATT_keys, 2, resid_add_evac(ntt))

        def mem_seqs(l):
            if not sample:
                return [(0, 512, l)]
            return None

        def do_mem(l):
            if not sample:
                mem_attention(ntt, l, [(0, 512, l)])
            else:
                seqs = []
                for j in range(NSEQ):
                    seqs.append((j * LS, LS, j % 2))
                mem_attention_sample(l, seqs)
        def mem_attention_sample(l, seqs):
            TBs = 128
            rms_to_HT(1)

            def evq(pi, j, b):
                eng = evac_eng()
                S.add(eng, copy_op(eng, R32[:, 4 * pi + j, 0:TBs], PS(b)[:, 0:TBs], scale=1.0 / 16),
                      reads=[("PS", b)], writes=[Rk(4 * pi + j)])
            linear_fm(1, 2, evq)
            for (qc0, nq, ms) in seqs:
                j = qc0 // LS
                S.add("sync", DMA(MEMK[:, ms, :, :], MKs[l][j][:, :, :]), reads=[("MSCR",)], writes=[("MEMK", ms)], dsem=("MEMK", ms))
                S.add("sync", DMA(MEMV[:, ms, :, :], MVs[l][j][:, :].rearrange("(t p) d -> p t d", p=128)),
                      reads=[("MSCR",)], writes=[("MEMV", ms)], dsem=("MEMV", ms))
                for h in range(4):
                    pts = []
                    for kt in range(2):
                        for dc in range(2):
                            S.add("tensor", MM(PS(kt)[:, 0:nq], MEMK[:, ms, 2 * h + dc, kt * 128:(kt + 1) * 128],
                                               R32[:, 2 * h + dc, qc0:qc0 + nq], dc == 0, dc == 1),
                                  reads=[("MEMK", ms), Rk(2 * h + dc)], writes=[("PS", kt)], partial=True)
                        pb = nbs()
                        S.add("scalar", ACTF(BS[:, pb, 0:nq], PS(kt)[:, 0:nq], AF.Exp), reads=[("PS", kt)], writes=[("BS", pb)])
                        pts.append(pb)
                    for dvc in range(2):
                        for kt in range(2):
                            S.add("tensor", MM(PS(2 + dvc)[:, 0:nq],
                                               MEMV[:, ms, kt, h * 256 + dvc * 128:h * 256 + (dvc + 1) * 128],
                                               BS[:, pts[kt], 0:nq], kt == 0, kt == 1),
                                  reads=[("MEMV", ms), ("BS", pts[kt])], writes=[("PS", 2 + dvc)], partial=True)
                    for kt in range(2):
                        S.add("tensor", MM(PS(4)[:, 0:nq], ONES, BS[:, pts[kt], 0:nq], kt == 0, kt == 1),
                              reads=[("BS", pts[kt]), ("CM",)], writes=[("PS", 4)], partial=True)
                    r = nfs()
                    S.add("vector", RECIP(FS[:, r, 0:nq], PS(4)[:, 0:nq]), reads=[("PS", 4)], writes=[("FS", r)])
                    for dvc in range(2):
                        S.add("vector", TT(ATT[:, 2 * h + dvc, qc0:qc0 + nq], PS(2 + dvc)[:, 0:nq], FS[:, r, 0:nq], ALU.mult),
                              reads=[("PS", 2 + dvc), ("FS", r)], writes=[("ATT", 2 * h + dvc)], partial=True)
            linear_tm(1, ATT_lhs, ATT_keys, 2, resid_add_evac(1))

        do_mem(0)
        ffn(ntt)

        rms_to_HT(ntt)

        def ev_q1(pi, tt, b):
            qb = nbs()
            S.add("scalar", ACTF(BS[:, qb, :], PS(b), AF.Copy, scale=0.125), reads=[("PS", b)], writes=[("BS", qb)])
            transposes_to(lambda j: BS[:, qb, j * 128:(j + 1) * 128], [("BS", qb)], 4,
                          R32[:, 4 * pi:4 * pi + 4, tt * 128:(tt + 1) * 128], [Rk(4 * pi + q) for q in range(4)],
                          scale_neg_dst=R32[:, 8 + 4 * pi:12 + 4 * pi, tt * 128:(tt + 1) * 128],
                          neg_keys=[Rk(8 + 4 * pi + q) for q in range(4)])
        linear_tm(ntt, HT_lhs, HT_keys, 2, ev_q1)
        kdst1 = ssk if sample else psk[t0:t0 + TB, :]
        vdst1 = ssv if sample else psv[t0:t0 + TB, :]

        def ev_k1(pi, tt, b):
            s = nstg()
            S.add("scalar", ACTF(STG[:, s, 0:512], PS(b), AF.Copy), reads=[("PS", b)], writes=[("STG", s)])
            store_rows(kdst1[tt * 128:(tt + 1) * 128, pi * 512:(pi + 1) * 512], s, 512)
            kb = nbs()
            S.add("vector", CP(BS[:, kb, :], PS(b)), reads=[("PS", b)], writes=[("BS", kb)])
            transposes_to(lambda j: BS[:, kb, j * 128:(j + 1) * 128], [("BS", kb)], 4,
                          R32[:, 16 + 4 * pi:20 + 4 * pi, tt * 128:(tt + 1) * 128], [Rk(16 + 4 * pi + q) for q in range(4)])
        linear_tm(ntt, HT_lhs, HT_keys, 2, ev_k1)

        def ev_v1(pi, tt, b):
            s = nstg()
            S.add("scalar", ACTF(STG[:, s, 0:512], PS(b), AF.Copy), reads=[("PS", b)], writes=[("STG", s)])
            store_rows(vdst1[tt * 128:(tt + 1) * 128, pi * 512:(pi + 1) * 512], s, 512)
            S.add("vector", CP(VO(tt)[:, pi * 512:(pi + 1) * 512], PS(b)), reads=[("PS", b)], writes=[Rk(24 + 2 * tt), Rk(25 + 2 * tt)],
                  partial=True)
        linear_tm(ntt, HT_lhs, HT_keys, 2, ev_v1)
        store_kv(1)
        attention(1, ntt, streams_for(1))
        linear_tm(ntt, ATT_lhs, ATT_keys, 2, resid_add_evac(ntt))
        do_mem(1)
        ffn(ntt)

        ydst = y_s if sample else y_p[t0:t0 + TB, :]
        for tt in range(ntt):
            s = ctr["hn"] = (ctr["hn"] + 1) % 2
            S.add("scalar", ACTF(HN[:, s, :], XR[:, tt, :], AF.Square, accum=SM[:, tt:tt + 1]),
                  reads=[("XR", tt)], writes=[("HN", s), ("SMa", tt)])
            S.add("scalar", ACTF(SM[:, 4 + tt:5 + tt], SM[:, tt:tt + 1], AF.Ln, scale=1.0 / D, bias=EPSC[:, 0:1]),
                  reads=[("SMa", tt), ("SMK",)], writes=[("SMb", tt)])
            S.add("scalar", ACTF(SM[:, 8 + tt:9 + tt], SM[:, 4 + tt:5 + tt], AF.Exp, scale=-0.5),
                  reads=[("SMb", tt)], writes=[("SMc", tt)])
            sg = nstg()
            S.add("vector", STT(STG[:, sg, :], XR[:, tt, :], SM[:, 8 + tt:9 + tt], GF[:, :], ALU.mult, ALU.mult),
                  reads=[("XR", tt), ("SMc", tt), ("GF",)], writes=[("STG", sg)])
            store_rows(ydst[tt * 128:(tt + 1) * 128, :], sg, D)

    try:
        ck(40)
        for bi in range(NB + 1):
            run_block(bi)
            ck(50 + bi)
    except _Stop:
        pass

    print("NOPS", len(S.ops), flush=True)
    if os.environ.get("KDUMP"):
        for i, op in enumerate(S.ops[-int(os.environ["KDUMP"]):]):
            print(len(S.ops) - int(os.environ["KDUMP"]) + i, op["eng"], op["dsem"], flush=True)
    S.emit(nc, es)
    es.close()
    return nc


def _consts(T, PAST):
    mats = np.zeros((128, 4, 128), np.float32)
    mats[:, 0, :] = np.eye(128)
    k = np.arange(128)[:, None]; kp = np.arange(128)[None, :]
    mats[:, 1, :] = (k >= kp)
    mats[:, 2, :] = 1.0
    mask = np.zeros((128, 9, 512), np.float32)
    q = np.arange(512)[None, :]
    for j in range(4):
        kk = j * 128 + np.arange(128)[:, None]
        mask[:, j, :] = (kk < q)
        mask[:, 4 + j, :] = ((kk // 64) <= (q // 64))
    kk = np.arange(128)[:, None]
    mask[:, 8, :] = (kk < (q % 32))
    half = 32
    inv = np.power(np.float32(10000.0), -np.arange(half, dtype=np.float32) * np.float32(2.0 / 64)).astype(np.float32)
    pos = np.concatenate([np.arange(T), PAST + (np.arange(128) % 32)]).astype(np.float32)
    ang = (pos[:, None] * inv[None, :]).astype(np.float32)
    cos = np.cos(ang).astype(np.float32); sin = np.sin(ang).astype(np.float32)
    cos2 = np.concatenate([cos, cos], 1); sinS = np.concatenate([-sin, sin], 1)
    rope = np.concatenate([cos2 * 0.125, sinS * 0.125, cos2, sinS], 1).astype(np.float32)
    return mats, mask, rope


_CACHE = {}


def _run(inputs, T, PAST, ncores):
    key = (T, PAST)
    if key not in _CACHE:
        _CACHE[key] = build(T, PAST)
    nc = _CACHE[key]
    mats, mask, rope = _consts(T, PAST)
    f = lambda a: np.ascontiguousarray(np.asarray(a, dtype=np.float32))
    I = {k: np.asarray(v) for k, v in inputs.items()}
    shared = dict(
        w_in0=f(I["w_in_even"][0]), w_out0=f(I["w_out_even"][0]),
        lam4=f(np.stack([I["lambda_q1"][0], I["lambda_k1"][0], I["lambda_q2"][0], I["lambda_k2"][0]])),
        subln=f(I["subln_gain"][0].reshape(128, 1)), convw=f(I["conv_w"][0]),
        w_in1=f(I["w_in_odd"][0]), w_out1=f(I["w_out_odd"][0]),
        nmix=f(I["norm_mix"]), nmem=f(I["norm_mem"]), ncross=f(I["norm_cross"]), nffn=f(I["norm_ffn"]),
        wq=f(I["w_q_mem"]), wk=f(I["w_k_mem"]), wv=f(I["w_v_mem"]), wo=f(I["w_o_mem"]),
        wup=f(I["w_ffn_up"]), wdn=f(I["w_ffn_down"]), nfin=f(I["norm_final"].reshape(1, D)),
        c_mats=mats, c_mask=mask, c_rope=rope,
    )
    def tagged(a, axis, c):
        shp = list(a.shape); shp[axis] = 1
        return np.ascontiguousarray(np.concatenate([a, np.full(shp, float(c), np.float32)], axis=axis))

    TAG = dict(w_in0=0, w_out0=0, w_in1=0, w_out1=0, wq=1, wk=1, wv=1, wo=1, wup=1, wdn=1, c_mask=1, c_rope=0)
    in_maps = []
    for c in range(ncores):
        sl = slice(NSEQ * c, NSEQ * (c + 1))
        m = dict(shared)
        for k, ax in TAG.items():
            m[k] = tagged(shared[k], ax, c)
        m.update(
            xp=f(I["x_prompt"][c]), xs=f(I["x_sample"][sl].reshape(128, D)),
            cdk=f(I["cache_diff_k"][0, sl].reshape(NSEQ, PAST, 512)), cdv=f(I["cache_diff_v"][0, sl].reshape(NSEQ, PAST, 512)),
            sconv=f(I["state_conv"][0, sl].reshape(NSEQ * 2, 512)),
            csk=f(I["cache_sb_k"][0, sl].reshape(NSEQ, PAST, 1024)), csv=f(I["cache_sb_v"][0, sl].reshape(NSEQ, PAST, 1024)),
            cmk=f(I["cache_mem_k"][:, sl].reshape(2, NSEQ, 256, 1024)), cmv=f(I["cache_mem_v"][:, sl].reshape(2, NSEQ, 256, 1024)),
            memp=f(I["mem_prompt"][c]),
        )
        in_maps.append(m)
    res = run_bass_kernel_spmd(nc, in_maps, core_ids=list(range(ncores)))
    R = res.results
    B = ncores
    g = lambda name: np.stack([np.asarray(R[c][name], dtype=np.float32) for c in range(B)])
    y_prompt = g("y_p")
    y_sample = g("y_s").reshape(B * NSEQ, LS, D)
    p_diff_k = g("pdk").reshape(1, B, T, 8, 64)
    p_diff_v = g("pdv").reshape(1, B, T, 4, 128)
    p_conv = g("pconv").reshape(1, B, 2, 512)
    p_sb_k = g("psk").reshape(1, B, T, 16, 64)
    p_sb_v = g("psv").reshape(1, B, T, 16, 64)
    p_mem_k = np.transpose(g("pmk"), (1, 0, 2, 3)).reshape(2, B, 256, 4, 256)
    p_mem_v = np.transpose(g("pmv"), (1, 0, 2, 3)).reshape(2, B, 256, 4, 256)
    s_diff_k = g("sdk").reshape(1, B * NSEQ, LS, 8, 64)
    s_diff_v = g("sdv").reshape(1, B * NSEQ, LS, 4, 128)
    s_conv = g("sconvo").reshape(1, B * NSEQ, 2, 512)
    s_sb_k = g("ssk").reshape(1, B * NSEQ, LS, 16, 64)
    s_sb_v = g("ssv").reshape(1, B * NSEQ, LS, 16, 64)
    return (y_prompt, y_sample, p_diff_k, p_diff_v, p_conv, p_sb_k, p_sb_v, p_mem_k, p_mem_v,
            s_diff_k, s_diff_v, s_conv, s_sb_k, s_sb_v)


def kernel(**inputs):
    T = int(np.asarray(inputs["x_prompt"]).shape[1])
    PAST = int(np.asarray(inputs["cache_diff_k"]).shape[2])
    ncores = int(np.asarray(inputs["x_prompt"]).shape[0])
    return _run(inputs, T, PAST, ncores)
```

```python
import contextlib
import numpy as np
import concourse.bass as bass
import concourse.mybir as mybir
from concourse.bass_utils import run_bass_kernel_spmd

F32 = mybir.dt.float32
BF16 = mybir.dt.bfloat16
AF = mybir.ActivationFunctionType
ALU = mybir.AluOpType

D = 1024
NSEQ = 4
LS = 32
EPS = 1e-6
SUBLN_EPS = 1e-5
LAM_INIT0 = 0.2
NWS = 3
NKV = 3
NSTG = 3
NFS = 8
NBS = 10


class Sched:
    ENGS = ["sync", "scalar", "vector", "gpsimd", "tensor"]

    def __init__(self):
        self.ops = []
        self.state = {}
        self.dma_count = {}
        self.frozen = False
        import os
        self.maxops = int(os.environ.get("KOPS", "100000000"))

    def add(self, eng, fn, reads=(), writes=(), dsem=None, partial=False):
        if self.frozen or len(self.ops) >= self.maxops:
            return -1
        idx = len(self.ops)
        deps = set()
        for k in reads:
            st = self.state.setdefault(k, [[], [], []])
            deps.update(st[0])
            if k[0] == "PS":
                deps.update(r for r in st[1] if self.ops[r]["eng"] != eng)
        for k in writes:
            st = self.state.setdefault(k, [[], [], []])
            if st[1] or not partial:
                deps.update(st[1])
                deps.update(st[0])
                st[2] = list(st[1]) + list(st[0])
                st[0] = [idx]
                st[1] = []
            else:
                deps.update(st[2])
                st[0].append(idx)
        for k in reads:
            self.state[k][1].append(idx)
        waits = []
        for d in deps:
            od = self.ops[d]
            if od["dsem"] is not None:
                waits.append(("dma", od["dsem"], 16 * self.dma_count[od["dsem"]]))
            else:
                if od["eng"] == eng and eng == "tensor":
                    continue
                od["signaled"] = True
                waits.append(("eng", d))
        op = dict(eng=eng, fn=fn, dsem=dsem, signaled=False, waits=waits)
        if dsem is not None:
            self.dma_count[dsem] = self.dma_count.get(dsem, 0) + 1
        self.ops.append(op)
        return idx

    def emit(self, nc, es):
        seenidx = {e: {} for e in self.ENGS}
        for op in self.ops:
            op["signaled"] = False
        for op in self.ops:
            need = {}
            for w in op["waits"]:
                if w[0] == "eng":
                    E = self.ops[w[1]]["eng"]
                    need[E] = max(need.get(E, -1), w[1])
            ew = []
            for E, d in need.items():
                if seenidx[op["eng"]].get(E, -1) >= d:
                    continue
                seenidx[op["eng"]][E] = d
                ew.append(("eng", d))
                self.ops[d]["signaled"] = True
            op["waits"] = [w for w in op["waits"] if w[0] == "dma"] + ew
        cnt = {e: 0 for e in self.ENGS}
        for op in self.ops:
            if op["dsem"] is None and op["signaled"]:
                cnt[op["eng"]] += 1
                op["sig"] = cnt[op["eng"]]
        esem = {e: es.enter_context(nc.semaphore("se_" + e)) for e in self.ENGS}
        dsem = {k: es.enter_context(nc.semaphore("sd_%d" % i)) for i, k in enumerate(sorted(self.dma_count, key=str))}
        block = es.enter_context(nc.Block())
        ops = self.ops

        def run(engname):
            def body(e):
                seen = {}
                for op in ops:
                    if op["eng"] != engname:
                        continue
                    for w in op["waits"]:
                        if w[0] == "dma":
                            s, v = dsem[w[1]], w[2]
                            key = ("d", w[1])
                        else:
                            od = ops[w[1]]
                            s, v = esem[od["eng"]], od["sig"]
                            key = ("e", od["eng"])
                        if seen.get(key, 0) >= v:
                            continue
                        seen[key] = v
                        e.wait_ge(s, v)
                    ins = op["fn"](e)
                    if op["dsem"] is not None:
                        ins.then_inc(dsem[op["dsem"]], 16)
                    elif op["signaled"]:
                        ins.then_inc(esem[engname], 1)
                if engname == "sync":
                    for k, c in self.dma_count.items():
                        e.wait_ge(dsem[k], 16 * c)
            return body

        block.sync(run("sync"))
        block.scalar(run("scalar"))
        block.vector(run("vector"))
        block.gpsimd(run("gpsimd"))
        block.tensor(run("tensor"))


class _Stop(Exception):
    pass


def build(T, PAST):
    import os
    STOP = int(os.environ.get("KSTOP", "9999"))

    def ck(n):
        if n >= STOP:
            S.frozen = True
    NB = T // 512
    NKS = PAST + LS
    nc = bass.Bass("TRN2", target_bir_lowering=False)
    es = contextlib.ExitStack()
    S = Sched()

    def din(name, shape, dt=F32):
        return nc.dram_tensor(name, list(shape), dt, kind="ExternalInput").ap()

    def dout(name, shape):
        return nc.dram_tensor(name, list(shape), F32, kind="ExternalOutput").ap()

    def dscr(name, shape, dt=BF16):
        return nc.dram_tensor(name, list(shape), dt).ap()

    xp = din("xp", [T, D]); xs = din("xs", [128, D])
    cdk = din("cdk", [NSEQ, PAST, 512]); cdv = din("cdv", [NSEQ, PAST, 512])
    sconv = din("sconv", [NSEQ * 2, 512])
    csk = din("csk", [NSEQ, PAST, 1024]); csv = din("csv", [NSEQ, PAST, 1024])
    cmk = din("cmk", [2, NSEQ, 256, 1024]); cmv = din("cmv", [2, NSEQ, 256, 1024])
    memp = din("memp", [256, D])
    w_in0 = din("w_in0", [D + 1, 3072]); w_out0 = din("w_out0", [D + 1, D])
    lam4 = din("lam4", [4, 64]); subln = din("subln", [128, 1]); convw = din("convw", [3, 512])
    w_in1 = din("w_in1", [D + 1, 3072]); w_out1 = din("w_out1", [D + 1, D])
    nmix = din("nmix", [2, D]); nmem = din("nmem", [2, D]); ncross = din("ncross", [2, D]); nffn = din("nffn", [2, D])
    wq = din("wq", [2, D + 1, D]); wk = din("wk", [2, D + 1, D]); wv = din("wv", [2, D + 1, D]); wo = din("wo", [2, D + 1, D])
    wup = din("wup", [2, D + 1, 4096]); wdn = din("wdn", [2, 4097, D]); nfin = din("nfin", [1, D])
    c_mats = din("c_mats", [128, 4, 128])
    c_mask = din("c_mask", [128, 10, 512])
    c_rope = din("c_rope", [T + 129, 256])

    y_p = dout("y_p", [T, D]); y_s = dout("y_s", [128, D])
    pdk = dout("pdk", [T, 512]); pdv = dout("pdv", [T, 512]); pconv = dout("pconv", [2, 512])
    psk = dout("psk", [T, 1024]); psv = dout("psv", [T, 1024])
    pmk = dout("pmk", [2, 256, 1024]); pmv = dout("pmv", [2, 256, 1024])
    sdk = dout("sdk", [128, 512]); sdv = dout("sdv", [128, 512]); sconvo = dout("sconvo", [NSEQ * 2, 512])
    ssk = dout("ssk", [128, 1024]); ssv = dout("ssv", [128, 1024])

    wb_in = [dscr("wb_in0", [D, 3072]), dscr("wb_in1", [D, 3072])]
    wb_out = [dscr("wb_out0", [D, D]), dscr("wb_out1", [D, D])]
    wb_q = [dscr("wb_q%d" % l, [D, D]) for l in range(2)]
    wb_k = [dscr("wb_k%d" % l, [D, D]) for l in range(2)]
    wb_v = [dscr("wb_v%d" % l, [D, D]) for l in range(2)]
    wb_o = [dscr("wb_o%d" % l, [D, D]) for l in range(2)]
    wb_up = [dscr("wb_up%d" % l, [D, 4096]) for l in range(2)]
    wb_dn = [dscr("wb_dn%d" % l, [4096, D]) for l in range(2)]
    NP = [4, 8]
    DV = [512, 1024]
    KTp = [dscr("KTp%d" % l, [128, NP[l], T]) for l in range(2)]
    Vp = [dscr("Vp%d" % l, [T, DV[l]]) for l in range(2)]
    KTs = [[dscr("KTs%d_%d" % (l, j), [128, NP[l], NKS]) for j in range(NSEQ)] for l in range(2)]
    Vs = [[dscr("Vs%d_%d" % (l, j), [NKS, DV[l]]) for j in range(NSEQ)] for l in range(2)]
    MKs = [[dscr("MKs%d_%d" % (l, j), [128, 8, 256]) for j in range(NSEQ)] for l in range(2)]
    MVs = [[dscr("MVs%d_%d" % (l, j), [256, 1024]) for j in range(NSEQ)] for l in range(2)]

    _dm = int(os.environ.get("KDUMMY", "0"))
    if _dm:
        dummies = [dscr("dummy_scr%d" % i, [_dm * 1024, 512]) for i in range(int(os.environ.get("KDUMMYN", "1")))]
    def sb(name, shape, dt):
        return es.enter_context(nc.sbuf_tensor(name, list(shape), dt))

    XR = sb("XR", [128, 4, D], F32)
    HN = sb("HN", [128, 2, D], BF16)
    HT = sb("HT", [128, 8, 512], BF16)
    WS = sb("WS", [128, NWS, 8, 512], BF16)
    R32 = sb("R32", [128, 32, 512], BF16)
    ATT = sb("ATT", [128, 8, 512], BF16)
    MEMK = sb("MEMK", [128, 2, 8, 256], BF16)
    MEMV = sb("MEMV", [128, 2, 2, 1024], BF16)
    STG = sb("STG", [128, NSTG, D], F32)
    FS = sb("FS", [128, NFS, 512], F32)
    BS = sb("BS", [128, NBS, 512], BF16)
    ACC = sb("ACC", [128, 2, 3, 512], BF16)
    KVK = sb("KVK", [128, NKV, 512], BF16)
    KVV0 = sb("KVV0", [128, NKV, 4, 128], BF16)
    KVV1 = sb("KVV1", [128, NKV, 4, 2, 128], BF16)
    MASK = sb("MASK", [128, 9, 512], BF16)
    CM = sb("CM", [128, 4, 128], BF16)
    IDF = sb("IDF", [128, 128], F32)
    ROPE = sb("ROPE", [128, 4, 256], F32)
    GF = sb("GF", [128, D], F32)
    SM = sb("SM", [128, 64], F32)
    CW = sb("CW", [128, 4, 3], F32)
    GN = sb("GN", [128, 9, 8], F32)
    CU = sb("CU", [128, 4, 520], F32)
    LAMT = sb("LAMT", [128, 4, 64], F32)
    PSB = [es.enter_context(nc.psum_tensor("PS%d" % b, [128, 512], F32)) for b in range(8)]

    IDB = CM[:, 0, :]
    TRI = CM[:, 1, :]
    ONES = CM[:, 2, :]

    def PS(b):
        return PSB[b][:, :]

    def PSbf(b):
        return PSB[b][:, :].bitcast(BF16)

    def MM(out, lhsT, rhs, start, stop):
        return lambda e: e.matmul(out, lhsT=lhsT, rhs=rhs, start=start, stop=stop, skip_group_check=True)

    def TR(out, in_, ident):
        return lambda e: e.transpose(out, in_, ident)

    def ACTF(out, in_, func, scale=1.0, bias=None, accum=None):
        def f(e):
            kw = {}
            if bias is not None:
                kw["bias"] = bias
            if accum is not None:
                kw["accum_out"] = accum
            return e.activation(out=out, in_=in_, func=func, scale=scale, **kw)
        return f

    def TT(out, a, b, op):
        return lambda e: e.tensor_tensor(out=out, in0=a, in1=b, op=op)

    def TS(out, a, s1, op0, s2=None, op1=None):
        if op1 is None:
            return lambda e: e.tensor_scalar(out=out, in0=a, scalar1=s1, scalar2=None, op0=op0)
        return lambda e: e.tensor_scalar(out=out, in0=a, scalar1=s1, scalar2=s2, op0=op0, op1=op1)

    def STT(out, a, scalar, b, op0, op1):
        return lambda e: e.scalar_tensor_tensor(out=out, in0=a, scalar=scalar, in1=b, op0=op0, op1=op1)

    def CP(out, in_):
        return lambda e: e.tensor_copy(out=out, in_=in_)

    def RECIP(out, in_):
        return lambda e: e.reciprocal(out=out, in_=in_)

    def MSET(ap, v):
        return lambda e: e.memset(ap, v)

    def DMA(out, in_, slow=False):
        if slow:
            return lambda e: e.dma_start(out=out, in_=in_, allow_slow_non_contiguous=True)
        return lambda e: e.dma_start(out=out, in_=in_)

    ctr = dict(fs=0, bs=0, stg=0, ps=0, tp=0, hn=0, evac=0)

    def nfs():
        ctr["fs"] = (ctr["fs"] + 1) % NFS
        return ctr["fs"]

    def nbs():
        ctr["bs"] = (ctr["bs"] + 1) % NBS
        return ctr["bs"]

    def nstg():
        ctr["stg"] = (ctr["stg"] + 1) % NSTG
        return ctr["stg"]

    def nps():
        ctr["ps"] = (ctr["ps"] + 1) % 6
        return ctr["ps"]

    def ntp():
        ctr["tp"] = (ctr["tp"] + 1) % 2
        return 6 + ctr["tp"]

    def evac_eng():
        ctr["evac"] += 1
        return "vector" if ctr["evac"] % 2 else "scalar"

    def copy_op(eng, out, in_, scale=None):
        if eng == "scalar":
            return ACTF(out, in_, AF.Copy, scale=1.0 if scale is None else scale)
        if scale is None:
            return CP(out, in_)
        return TS(out, in_, float(scale), ALU.mult)

    Rk = lambda i: ("R", i)

    panels = []
    wstate = dict(issued=0, cur=-1)

    def panel_ap(w, r0, c0):
        return w[r0:r0 + 1024, c0:c0 + 512].rearrange("(c p) n -> p c n", p=128)

    def block_panels():
        pl = []
        for c0 in (0, 512, 1024, 2048, 2560, 1536):
            pl.append(panel_ap(wb_in[0], 0, c0))
        for l in range(2):
            if l == 1:
                for c0 in range(0, 3072, 512):
                    pl.append(panel_ap(wb_in[1], 0, c0))
            for c0 in (0, 512):
                pl.append(panel_ap(wb_out[l], 0, c0))
            for c0 in (0, 512):
                pl.append(panel_ap(wb_q[l], 0, c0))
            for c0 in (0, 512):
                pl.append(panel_ap(wb_o[l], 0, c0))
            for c0 in range(0, 4096, 512):
                pl.append(panel_ap(wb_up[l], 0, c0))
            for c0 in (0, 512):
                for r0 in range(0, 4096, 1024):
                    pl.append(panel_ap(wb_dn[l], r0, c0))
        return pl

    def issue_panels(upto):
        while wstate["issued"] <= min(upto, len(panels) - 1):
            i = wstate["issued"]
            s = i % NWS
            S.add("sync", DMA(WS[:, s, :, :], panels[i]), reads=[("WSCR",)], writes=[("WS", s)], dsem=("WS", s))
            wstate["issued"] += 1

    def next_panel():
        wstate["cur"] += 1
        i = wstate["cur"]
        issue_panels(i + NWS - 1)
        return i % NWS

    def rms_to_HT(ntt):
        for tt in range(ntt):
            s = ctr["hn"] = (ctr["hn"] + 1) % 2
            S.add("scalar", ACTF(HN[:, s, :], XR[:, tt, :], AF.Square, accum=SM[:, tt:tt + 1]),
                  reads=[("XR", tt)], writes=[("HN", s), ("SMa", tt)])
            S.add("scalar", ACTF(SM[:, 4 + tt:5 + tt], SM[:, tt:tt + 1], AF.Ln, scale=1.0 / D, bias=EPSC[:, 0:1]),
                  reads=[("SMa", tt), ("SMK",)], writes=[("SMb", tt)])
            S.add("scalar", ACTF(SM[:, 8 + tt:9 + tt], SM[:, 4 + tt:5 + tt], AF.Exp, scale=-0.5),
                  reads=[("SMb", tt)], writes=[("SMc", tt)])
            S.add("vector", TS(HN[:, s, :], XR[:, tt, :], SM[:, 8 + tt:9 + tt], ALU.mult),
                  reads=[("XR", tt), ("SMc", tt)], writes=[("HN", s)])
            b = ntp()
            for c in range(8):
                S.add("tensor", TR(PSbf(b)[:, c * 128:(c + 1) * 128], HN[:, s, c * 128:(c + 1) * 128], IDB),
                      reads=[("HN", s)], writes=[("PS", b)], partial=True)
            eng = evac_eng()
            S.add(eng, copy_op(eng, HT[:, :, tt * 128:(tt + 1) * 128],
                               PSbf(b).rearrange("p (c t) -> p c t", t=128)),
                  reads=[("PS", b)], writes=[("HT", tt)])

    def linear_tm(ntt, lhs_fn, lhs_keys_fn, npanels, evac):
        for pi in range(npanels):
            s = next_panel()
            for tt in range(ntt):
                b = nps()
                for c in range(8):
                    S.add("tensor", MM(PS(b), lhs_fn(c, tt), WS[:, s, c, :], c == 0, c == 7),
                          reads=[("WS", s)] + lhs_keys_fn(c, tt), writes=[("PS", b)], partial=True)
                evac(pi, tt, b)

    def linear_fm(ntt, npanels, evac):
        TB = ntt * 128
        for pi in range(npanels):
            s = next_panel()
            for j in range(4):
                b = nps()
                for c in range(8):
                    S.add("tensor", MM(PS(b)[:, 0:TB], WS[:, s, c, j * 128:(j + 1) * 128], HT[:, c, 0:TB], c == 0, c == 7),
                          reads=[("WS", s)] + [("HT", t) for t in range(ntt)], writes=[("PS", b)], partial=True)
                evac(pi, j, b)

    HT_lhs = lambda c, tt: HT[:, c, tt * 128:(tt + 1) * 128]
    HT_keys = lambda c, tt: [("HT", tt)]
    ATT_lhs = lambda c, tt: ATT[:, c, tt * 128:(tt + 1) * 128]
    ATT_keys = lambda c, tt: [("ATT", c)]

    def resid_add_evac(ntt):
        def ev(pi, tt, b):
            S.add("vector", TT(XR[:, tt, pi * 512:(pi + 1) * 512], PS(b), XR[:, tt, pi * 512:(pi + 1) * 512], ALU.add),
                  reads=[("PS", b), ("XR", tt)], writes=[("XR", tt)])
        return ev

    def store_rows(dst_ap, stg_s, ncols):
        S.add("sync", DMA(dst_ap, STG[:, stg_s, 0:ncols]), reads=[("STG", stg_s)], writes=[("OUT",)],
              dsem=("STG", stg_s), partial=True)

    def transposes_to(src_ap_fn, src_keys, n, dst_ap, dst_keys, scale_neg_dst=None, neg_keys=None):
        b = ntp()
        for j in range(n):
            S.add("tensor", TR(PSbf(b)[:, j * 128:(j + 1) * 128], src_ap_fn(j), IDB),
                  reads=src_keys, writes=[("PS", b)], partial=True)
        src = PSbf(b)[:, 0:n * 128].rearrange("p (c t) -> p c t", t=128)
        S.add("vector", CP(dst_ap, src), reads=[("PS", b)], writes=dst_keys)
        if scale_neg_dst is not None:
            S.add("scalar", ACTF(scale_neg_dst, src, AF.Copy, scale=-1.0), reads=[("PS", b)], writes=neg_keys)

    kvstate = dict(n=0)
    RACC = 3

    def attention(kind, ntt, streams):
        TB = ntt * 128
        npairs = NP[kind]
        D1 = 2
        D2 = 3 if kind == 1 else 2
        QT = lambda p: R32[:, p, :]
        QN = lambda p: R32[:, 8 + p, :]
        for p in range(npairs):
            qkeys = [Rk(p)] + ([Rk(8 + p)] if kind == 1 else [])
            chunks_all = []
            units = []
            for si, (qc0, nq, KTd, Vd, chunks, maskfn) in enumerate(streams):
                ntiles_total = sum((nkc + 127) // 128 for (_, nkc) in chunks)
                tcount = 0
                for (k0, nkc) in chunks:
                    ci = len(chunks_all)
                    chunks_all.append((si, k0, nkc))
                    nt = (nkc + 127) // 128
                    for t in range(nt - 1, -1, -1):
                        nk = min(128, nkc - t * 128)
                        tcount += 1
                        for a in range(2):
                            units.append(dict(si=si, ci=ci, t=t, nk=nk, a=a, qc0=qc0, nq=nq,
                                              mask=maskfn(k0 + t * 128), first=(tcount == 1), last=(tcount == ntiles_total)))
            cslot = {}

            def load_chunk(ci):
                if ci in cslot or ci >= len(chunks_all):
                    return
                si, k0, nkc = chunks_all[ci]
                KTd, Vd = streams[si][2], streams[si][3]
                kvstate["n"] += 1
                sl = kvstate["n"] % NKV
                cslot[ci] = sl
                S.add("sync", DMA(KVK[:, sl, 0:nkc], KTd[:, p, k0:k0 + nkc]),
                      reads=[("KVSCR",)], writes=[("KVK", sl)], dsem=("KV", sl), partial=True)
                nt = (nkc + 127) // 128
                if kind == 0:
                    if nkc >= 128:
                        S.add("sync", DMA(KVV0[:, sl, 0:nt, :],
                                          Vd[k0:k0 + nkc, p * 128:(p + 1) * 128].rearrange("(t q) d -> q t d", q=128)),
                              reads=[("KVSCR",)], writes=[("KVV", sl)], dsem=("KV", sl), partial=True)
                    else:
                        S.add("sync", DMA(KVV0[0:nkc, sl, 0, :], Vd[k0:k0 + nkc, p * 128:(p + 1) * 128]),
                              reads=[("KVSCR",)], writes=[("KVV", sl)], dsem=("KV", sl), partial=True)
                else:
                    for a in range(2):
                        c0 = p * 128 + a * 64
                        if nkc >= 128:
                            S.add("sync", DMA(KVV1[:, sl, 0:nt, a, a * 64:(a + 1) * 64],
                                              Vd[k0:k0 + nkc, c0:c0 + 64].rearrange("(t q) d -> q t d", q=128)),
                                  reads=[("KVSCR",)], writes=[("KVV", sl)], dsem=("KV", sl), partial=True)
                        else:
                            S.add("sync", DMA(KVV1[0:nkc, sl, 0, a, a * 64:(a + 1) * 64], Vd[k0:k0 + nkc, c0:c0 + 64]),
                                  reads=[("KVSCR",)], writes=[("KVV", sl)], dsem=("KV", sl), partial=True)

            accr = [0, 0]
            NU = len(units)
            for s_ in range(NU + D2):
                if s_ < NU:
                    u = units[s_]
                    load_chunk(u["ci"])
                    if s_ == 0 or units[s_ - 1]["ci"] != u["ci"]:
                        load_chunk(u["ci"] + 1)
                    sl = u["sl"] = cslot[u["ci"]]
                    a, nk, nq, qc0, t = u["a"], u["nk"], u["nq"], u["qc0"], u["t"]
                    pa = slice(64 * a, 64 * a + 64)
                    kc = slice(t * 128, t * 128 + nk)
                    mask = u["mask"]
                    if kind == 0:
                        zb = u["bank"] = 4 + (s_ % 4)
                        Z = PS(zb)[0:nk, qc0:qc0 + nq]
                        S.add("tensor", MM(Z, KVK[pa, sl, kc], QT(p)[pa, qc0:qc0 + nq], True, True),
                              reads=[("KVK", sl)] + qkeys, writes=[("PS", zb)])
                        pb = u["pb"] = nbs()
                        Pt = BS[0:nk, pb, 0:nq]
                        S.add("scalar", ACTF(Pt, Z, AF.Exp), reads=[("PS", zb)], writes=[("BS", pb)])
                        if mask is not None:
                            S.add("gpsimd", TT(Pt, Pt, mask[0:nk, qc0:qc0 + nq], ALU.mult),
                                  reads=[("BS", pb), ("MASK",)], writes=[("BS", pb)])
                    else:
                        zb = u["bank"] = 1 + (s_ % 6)
                        Z = PS(zb)[0:nk, qc0:qc0 + nq]
                        S.add("tensor", MM(Z, KVK[pa, sl, kc], QT(p)[pa, qc0:qc0 + nq], True, True),
                              reads=[("KVK", sl)] + qkeys, writes=[("PS", zb)])
                        fe = nfs()
                        E = FS[0:nk, fe, 0:nq]
                        S.add("scalar", ACTF(E, Z, AF.Exp), reads=[("PS", zb)], writes=[("FS", fe)])
                        lb = u["lb"] = nbs()
                        Lt = BS[0:nk, lb, 0:nq]
                        S.add("scalar", ACTF(Lt, E, AF.Ln, bias=ONEC[0:nk, 0:1]), reads=[("FS", fe), ("SMK",)], writes=[("BS", lb)])
                        if mask is not None:
                            S.add("gpsimd", TT(Lt, Lt, mask[0:nk, qc0:qc0 + nq], ALU.mult),
                                  reads=[("BS", lb), ("MASK",)], writes=[("BS", lb)])
                        u["acc_in"] = accr[a]
                        if not u["last"]:
                            rn = (accr[a] + 1) % RACC
                            if u["first"]:
                                if nk < 128:
                                    S.add("gpsimd", MSET(ACC[:, a, rn, :], 0.0), writes=[("ACC", a, rn)])
                                S.add("gpsimd", CP(ACC[0:nk, a, rn, 0:nq], Lt), reads=[("BS", lb)], writes=[("ACC", a, rn)])
                            else:
                                S.add("gpsimd", TT(ACC[:, a, rn, 0:nq], ACC[:, a, accr[a], 0:nq], Lt, ALU.add),
                                      reads=[("BS", lb), ("ACC", a, accr[a])], writes=[("ACC", a, rn)])
                            accr[a] = rn
                if kind == 1 and 0 <= s_ - D1 < NU:
                    u = units[s_ - D1]
                    sl, a, nk, nq, qc0, t = u["sl"], u["a"], u["nk"], u["nq"], u["qc0"], u["t"]
                    pa = slice(64 * a, 64 * a + 64)
                    kc = slice(t * 128, t * 128 + nk)
                    cb = u["bank"]
                    C = PS(cb)[0:nk, qc0:qc0 + nq]
                    Lt = BS[0:nk, u["lb"], 0:nq]
                    S.add("tensor", MM(C, TRI[0:nk, 0:nk], Lt, True, False),
                          reads=[("BS", u["lb"]), ("CM",)], writes=[("PS", cb)])
                    if not u["first"]:
                        S.add("tensor", MM(C, ONES[:, 0:nk], ACC[:, a, u["acc_in"], 0:nq], False, False),
                              reads=[("ACC", a, u["acc_in"]), ("CM",)], writes=[("PS", cb)], partial=True)
                    S.add("tensor", MM(C, KVK[pa, sl, kc], QN(p)[pa, qc0:qc0 + nq], False, True),
                          reads=[("KVK", sl)] + qkeys, writes=[("PS", cb)], partial=True)
                    wb = u["wb"] = nbs()
                    Wt = BS[0:nk, wb, 0:nq]
                    S.add("scalar", ACTF(Wt, C, AF.Exp, scale=-1.0), reads=[("PS", cb)], writes=[("BS", wb)])
                    if u["mask"] is not None:
                        S.add("gpsimd", TT(Wt, Wt, u["mask"][0:nk, qc0:qc0 + nq], ALU.mult),
                              reads=[("BS", wb), ("MASK",)], writes=[("BS", wb)])
                if 0 <= s_ - D2 < NU:
                    u = units[s_ - D2]
                    sl, a, nk, nq, qc0, t = u["sl"], u["a"], u["nk"], u["nq"], u["qc0"], u["t"]
                    if kind == 0:
                        Pt = BS[0:nk, u["pb"], 0:nq]
                        S.add("tensor", MM(PS(a)[:, qc0:qc0 + nq], KVV0[0:nk, sl, t, :], Pt, u["first"], u["last"]),
                              reads=[("KVV", sl), ("BS", u["pb"])], writes=[("PS", a)], partial=True)
                        S.add("tensor", MM(PS(2 + a)[:, qc0:qc0 + nq], ONES[0:nk, :], Pt, u["first"], u["last"]),
                              reads=[("BS", u["pb"]), ("CM",)], writes=[("PS", 2 + a)], partial=True)
                    else:
                        Wt = BS[0:nk, u["wb"], 0:nq]
                        S.add("tensor", MM(PS(0)[:, qc0:qc0 + nq], KVV1[0:nk, sl, t, a, :], Wt,
                                           u["first"] and a == 0, u["last"] and a == 1),
                              reads=[("KVV", sl), ("BS", u["wb"])], writes=[("PS", 0)], partial=True)
            if kind == 0:
                osb = []
                for a in range(2):
                    r = nfs()
                    S.add("vector", RECIP(FS[:, r, 0:TB], PS(2 + a)[:, 0:TB]), reads=[("PS", 2 + a)], writes=[("FS", r)])
                    o = nfs()
                    S.add("vector", TT(FS[:, o, 0:TB], PS(a)[:, 0:TB], FS[:, r, 0:TB], ALU.mult),
                          reads=[("PS", a), ("FS", r)], writes=[("FS", o)])
                    osb.append(o)
                oc = nfs()
                S.add("vector", STT(FS[:, oc, 0:TB], FS[:, osb[1], 0:TB], NLAM[:, 0:1], FS[:, osb[0], 0:TB], ALU.mult, ALU.add),
                      reads=[("FS", osb[0]), ("FS", osb[1]), ("LAM",)], writes=[("FS", oc)])
                sq = nbs()
                S.add("gpsimd", TT(BS[:, sq, 0:TB], FS[:, oc, 0:TB], FS[:, oc, 0:TB], ALU.mult),
                      reads=[("FS", oc)], writes=[("BS", sq)])
                S.add("tensor", MM(PS(4)[:, 0:TB], ONES, BS[:, sq, 0:TB], True, True),
                      reads=[("BS", sq), ("CM",)], writes=[("PS", 4)])
                ln = nfs()
                S.add("scalar", ACTF(FS[:, ln, 0:TB], PS(4)[:, 0:TB], AF.Ln, scale=1.0 / 128, bias=SEPSC[:, 0:1]),
                      reads=[("PS", 4), ("SMK",)], writes=[("FS", ln)])
                rs = nfs()
                S.add("scalar", ACTF(FS[:, rs, 0:TB], FS[:, ln, 0:TB], AF.Exp, scale=-0.5),
                      reads=[("FS", ln)], writes=[("FS", rs)])
                S.add("vector", STT(ATT[:, p, 0:TB], FS[:, oc, 0:TB], GSUB[:, 0:1], FS[:, rs, 0:TB], ALU.mult, ALU.mult),
                      reads=[("FS", oc), ("FS", rs), ("GSUB",)], writes=[("ATT", p)])
            else:
                S.add("vector", CP(ATT[:, p, 0:TB], PS(0)[:, 0:TB]), reads=[("PS", 0)], writes=[("ATT", p)])

    def mem_attention(ntt, l, seqs):
        TB = ntt * 128
        rms_to_HT(ntt)

        def evq(pi, j, b):
            eng = evac_eng()
            S.add(eng, copy_op(eng, R32[:, 4 * pi + j, 0:TB], PS(b)[:, 0:TB], scale=1.0 / 16),
                  reads=[("PS", b)], writes=[Rk(4 * pi + j)])
        linear_fm(ntt, 2, evq)
        for h in range(4):
            pts = []
            for kt in range(2):
                for (qc0, nq, ms) in seqs:
                    for dc in range(2):
                        S.add("tensor", MM(PS(kt)[:, qc0:qc0 + nq], MEMK[:, ms, 2 * h + dc, kt * 128:(kt + 1) * 128],
                                           R32[:, 2 * h + dc, qc0:qc0 + nq], dc == 0, dc == 1),
                              reads=[("MEMK", ms), Rk(2 * h + dc)], writes=[("PS", kt)], partial=True)
                pb = nbs()
                S.add("scalar", ACTF(BS[:, pb, 0:TB], PS(kt)[:, 0:TB], AF.Exp), reads=[("PS", kt)], writes=[("BS", pb)])
                pts.append(pb)
            for (qc0, nq, ms) in seqs:
                for dvc in range(2):
                    for kt in range(2):
                        S.add("tensor", MM(PS(2 + dvc)[:, qc0:qc0 + nq],
                                           MEMV[:, ms, kt, h * 256 + dvc * 128:h * 256 + (dvc + 1) * 128],
                                           BS[:, pts[kt], qc0:qc0 + nq], kt == 0, kt == 1),
                              reads=[("MEMV", ms), ("BS", pts[kt])], writes=[("PS", 2 + dvc)], partial=True)
                for kt in range(2):
                    S.add("tensor", MM(PS(4)[:, qc0:qc0 + nq], ONES, BS[:, pts[kt], qc0:qc0 + nq], kt == 0, kt == 1),
                          reads=[("BS", pts[kt]), ("CM",)], writes=[("PS", 4)], partial=True)
            r = nfs()
            S.add("vector", RECIP(FS[:, r, 0:TB], PS(4)[:, 0:TB]), reads=[("PS", 4)], writes=[("FS", r)])
            for dvc in range(2):
                S.add("vector", TT(ATT[:, 2 * h + dvc, 0:TB], PS(2 + dvc)[:, 0:TB], FS[:, r, 0:TB], ALU.mult),
                      reads=[("PS", 2 + dvc), ("FS", r)], writes=[("ATT", 2 * h + dvc)])
        linear_tm(ntt, ATT_lhs, ATT_keys, 2, resid_add_evac(ntt))

    def ffn(ntt):
        TB = ntt * 128
        rms_to_HT(ntt)

        def evu(pi, j, b):
            f = nfs()
            S.add("scalar", ACTF(FS[:, f, 0:TB], PS(b)[:, 0:TB], AF.Relu), reads=[("PS", b)], writes=[("FS", f)])
            S.add("vector", TT(R32[:, 4 * pi + j, 0:TB], FS[:, f, 0:TB], FS[:, f, 0:TB], ALU.mult),
                  reads=[("FS", f)], writes=[Rk(4 * pi + j)])
        linear_fm(ntt, 8, evu)
        for half in range(2):
            for kq in range(4):
                s = next_panel()
                for tt in range(ntt):
                    for c in range(8):
                        S.add("tensor", MM(PS(tt), R32[:, kq * 8 + c, tt * 128:(tt + 1) * 128], WS[:, s, c, :],
                                           kq == 0 and c == 0, kq == 3 and c == 7),
                              reads=[("WS", s), Rk(kq * 8 + c)], writes=[("PS", tt)], partial=True)
            for tt in range(ntt):
                S.add("vector", TT(XR[:, tt, half * 512:(half + 1) * 512], PS(tt), XR[:, tt, half * 512:(half + 1) * 512], ALU.add),
                      reads=[("PS", tt), ("XR", tt)], writes=[("XR", tt)])

    EPSC = SM[:, 32:33]; SEPSC = SM[:, 33:34]; ONEC = SM[:, 34:35]; NLAM = SM[:, 35:36]; GSUB = SM[:, 36:37]
    S.add("gpsimd", MSET(SM[:, 32:33], EPS), writes=[("SMK",)])
    S.add("gpsimd", MSET(SM[:, 33:34], SUBLN_EPS), writes=[("SMK",)], partial=True)
    S.add("gpsimd", MSET(SM[:, 34:35], 1.0), writes=[("SMK",)], partial=True)
    S.add("gpsimd", MSET(KVV1[:, :, :, :, :], 0.0), writes=[("KVV", i) for i in range(NKV)])
    S.add("gpsimd", MSET(CU[:, :, :], 0.0), writes=[("CU", j) for j in range(4)])
    S.add("sync", DMA(FS[:, 0, :], c_mats.rearrange("p a b -> p (a b)")), writes=[("FS", 0)], dsem="c0")
    S.add("vector", CP(CM[:, :, :].rearrange("p a b -> p (a b)"), FS[:, 0, :]), reads=[("FS", 0)], writes=[("CM",)])
    S.add("vector", CP(IDF[:, :], FS[:, 0, 0:128]), reads=[("FS", 0)], writes=[("IDF",)])
    for m in range(9):
        f = nfs()
        S.add("sync", DMA(FS[:, f, :], c_mask[:, m, :]), writes=[("FS", f)], dsem=("FS", f))
        S.add("vector", CP(MASK[:, m, :], FS[:, f, :]), reads=[("FS", f)], writes=[("MASK",)], partial=True)
    S.add("sync", DMA(GF[:, :], nfin.partition_broadcast(128)), writes=[("GF",)], dsem="c1")
    S.add("sync", DMA(GSUB, subln), writes=[("GSUB0",)], dsem="c1")
    S.add("sync", DMA(LAMT[:, :, :].rearrange("p a b -> p (a b)"), lam4.rearrange("a b -> (a b)").partition_broadcast(128)),
          writes=[("LAMT",)], dsem="c1")
    gl = [nmix[0:1, :], nmix[1:2, :], ncross[0:1, :], ncross[1:2, :], nffn[0:1, :], nffn[1:2, :], nmem[0:1, :], nmem[1:2, :]]
    for gi, g in enumerate(gl):
        S.add("sync", DMA(GN[:, gi, :], g.rearrange("o (c p) -> p (o c)", p=128), slow=True),
              writes=[("GN",)], dsem="c1", partial=True)
    for j in range(3):
        S.add("sync", DMA(CW[:, :, j], convw[j:j + 1, :].rearrange("o (c p) -> p (o c)", p=128), slow=True),
              writes=[("CW",)], dsem="c1", partial=True)

    S.add("vector", TS(GSUB, GSUB, 1.0 - LAM_INIT0, ALU.mult), reads=[("GSUB0",)], writes=[("GSUB",)])
    S.add("vector", TT(LAMT[:, 0, :], LAMT[:, 0, :], LAMT[:, 1, :], ALU.mult), reads=[("LAMT",)], writes=[("LAMa",)])
    S.add("vector", TT(LAMT[:, 2, :], LAMT[:, 2, :], LAMT[:, 3, :], ALU.mult), reads=[("LAMT",)], writes=[("LAMb",)])
    S.add("vector", lambda e: e.reduce_sum(out=SM[:, 40:41], in_=LAMT[:, 0, :], axis=mybir.AxisListType.X),
          reads=[("LAMa",)], writes=[("LAMc",)])
    S.add("vector", lambda e: e.reduce_sum(out=SM[:, 41:42], in_=LAMT[:, 2, :], axis=mybir.AxisListType.X),
          reads=[("LAMb",)], writes=[("LAMd",)])
    S.add("scalar", ACTF(SM[:, 42:44], SM[:, 40:42], AF.Exp), reads=[("LAMc",), ("LAMd",)], writes=[("LAMe",)])
    S.add("vector", TT(SM[:, 44:45], SM[:, 43:44], SM[:, 42:43], ALU.subtract), reads=[("LAMe",)], writes=[("LAMf",)])
    S.add("vector", TS(NLAM, SM[:, 44:45], -LAM_INIT0, ALU.add), reads=[("LAMf",)], writes=[("LAM",)])
    if _dm:
        for dummy in dummies:
            S.add("sync", DMA(dummy[0:128, :], MASK[:, 0, :]), reads=[("MASK",)], writes=[("DUMMY",)], dsem="c0", partial=True)
            S.add("sync", DMA(dummy[_dm * 1024 - 128:_dm * 1024, :], MASK[:, 0, :]), reads=[("MASK",)], writes=[("DUMMY",)], dsem="c0", partial=True)
    ck(10)
    ctasks = []

    def cast_weight(src, dst, K, N, gi):
        for c in range(K // 128):
            for n0 in range(0, N, 2048):
                ctasks.append((src, dst, c, n0, min(2048, N - n0), gi))

    for l in range(2):
        cast_weight(wk[l], wb_k[l], D, D, 6 + l)
        cast_weight(wv[l], wb_v[l], D, D, 6 + l)
    cast_weight(w_in0, wb_in[0], D, 3072, 0)
    cast_weight(w_out0, wb_out[0], D, D, None)
    cast_weight(w_in1, wb_in[1], D, 3072, 1)
    cast_weight(w_out1, wb_out[1], D, D, None)
    for l in range(2):
        cast_weight(wq[l], wb_q[l], D, D, 2 + l)
        cast_weight(wo[l], wb_o[l], D, D, None)
        cast_weight(wup[l], wb_up[l], D, 4096, 4 + l)
        cast_weight(wdn[l], wb_dn[l], 4096, D, None)

    def c_fin(i):
        s = i % 2
        return R32[:, 8 * s:8 * s + 8, :].rearrange("p a b -> p (a b)").bitcast(F32)

    def c_fout(i):
        s = i % 2
        return R32[:, 16 + 4 * s:20 + 4 * s, :].rearrange("p a b -> p (a b)")

    def c_load(i):
        src, dst, c, n0, w, gi = ctasks[i]
        S.add("sync", DMA(c_fin(i)[:, 0:w], src[c * 128:(c + 1) * 128, n0:n0 + w]), writes=[("CI", i % 2)], dsem=("CI", i % 2))

    c_load(0)
    for i in range(len(ctasks)):
        src, dst, c, n0, w, gi = ctasks[i]
        s = i % 2
        if i + 1 < len(ctasks):
            c_load(i + 1)
        eng = ["vector", "gpsimd", "scalar"][i % 3]
        fin, fout = c_fin(i), c_fout(i)
        if gi is None:
            S.add(eng, copy_op(eng, fout[:, 0:w], fin[:, 0:w]), reads=[("CI", s)], writes=[("CO", s)])
        else:
            gcol = GN[:, gi, (c % 8):(c % 8) + 1]
            if eng == "scalar":
                S.add(eng, ACTF(fout[:, 0:w], fin[:, 0:w], AF.Copy, scale=gcol), reads=[("CI", s), ("GN",)], writes=[("CO", s)])
            else:
                S.add(eng, TS(fout[:, 0:w], fin[:, 0:w], gcol, ALU.mult), reads=[("CI", s), ("GN",)], writes=[("CO", s)])
        S.add("sync", DMA(dst[c * 128:(c + 1) * 128, n0:n0 + w], fout[:, 0:w]), reads=[("CO", s)], writes=[("WSCR",)],
              dsem=("CO", s), partial=True)

    ck(20)
    for l in range(2):
        for w in (wb_k[l], wb_v[l]):
            for c0 in (0, 512):
                panels.append(panel_ap(w, 0, c0))
    bp = block_panels()
    for _ in range(NB + 1):
        panels.extend(bp)

    S.add("sync", DMA(XR[:, 0:2, :], memp.rearrange("(t p) d -> p t d", p=128)), writes=[("XR", 0), ("XR", 1)], dsem="XR")
    ck(21)
    rms_to_HT(2)
    ck(22)
    for l in range(2):
        def evk(pi, tt, b, l=l):
            s = nstg()
            S.add("scalar", ACTF(STG[:, s, 0:512], PS(b), AF.Copy), reads=[("PS", b)], writes=[("STG", s)])
            store_rows(pmk[l, tt * 128:(tt + 1) * 128, pi * 512:(pi + 1) * 512], s, 512)
            kb = nbs()
            S.add("vector", CP(BS[:, kb, :], PS(b)), reads=[("PS", b)], writes=[("BS", kb)])
            transposes_to(lambda j: BS[:, kb, j * 128:(j + 1) * 128], [("BS", kb)], 4,
                          MEMK[:, l, 4 * pi:4 * pi + 4, tt * 128:(tt + 1) * 128], [("MEMK", l)])
        linear_tm(2, HT_lhs, HT_keys, 2, evk)
        ck(23 + 2 * l)

        def evv(pi, tt, b, l=l):
            s = nstg()
            S.add("scalar", ACTF(STG[:, s, 0:512], PS(b), AF.Copy), reads=[("PS", b)], writes=[("STG", s)])
            store_rows(pmv[l, tt * 128:(tt + 1) * 128, pi * 512:(pi + 1) * 512], s, 512)
            S.add("vector", CP(MEMV[:, l, tt, pi * 512:(pi + 1) * 512], PS(b)), reads=[("PS", b)], writes=[("MEMV", l)])
        linear_tm(2, HT_lhs, HT_keys, 2, evv)

    ck(30)
    def prep_cache(l, ck, cv):
        npair = NP[l]
        dv = DV[l]
        for j in range(NSEQ):
            for k0 in range(0, PAST, 512):
                for t in range(4):
                    s = nstg()
                    S.add("sync", DMA(STG[:, s, 0:dv], ck[j, k0 + t * 128:k0 + (t + 1) * 128, :]), writes=[("STG", s)], dsem=("STG", s))
                    for g in range(npair // 4):
                        b = nps()
                        for q in range(4):
                            pq = g * 4 + q
                            S.add("tensor", TR(PS(b)[:, q * 128:(q + 1) * 128], STG[:, s, pq * 128:(pq + 1) * 128], IDF[:, :]),
                                  reads=[("STG", s), ("IDF",)], writes=[("PS", b)], partial=True)
                        eng = evac_eng()
                        S.add(eng, copy_op(eng, R32[:, 16 + g * 4:16 + g * 4 + 4, t * 128:(t + 1) * 128],
                                           PS(b).rearrange("p (c t) -> p c t", t=128)),
                              reads=[("PS", b)], writes=[Rk(16 + g * 4 + q) for q in range(4)], partial=True)
                    s2 = nstg()
                    S.add("sync", DMA(STG[:, s2, 0:dv], cv[j, k0 + t * 128:k0 + (t + 1) * 128, :]), writes=[("STG", s2)], dsem=("STG", s2))
                    vo = R32[:, 24 + 2 * t:26 + 2 * t, :].rearrange("p a b -> p (a b)")
                    eng = evac_eng()
                    S.add(eng, copy_op(eng, vo[:, 0:dv], STG[:, s2, 0:dv]), reads=[("STG", s2)], writes=[Rk(24 + 2 * t), Rk(25 + 2 * t)])
                S.add("sync", DMA(KTs[l][j][:, :, k0:k0 + 512], R32[:, 16:16 + npair, :]),
                      reads=[Rk(16 + q) for q in range(npair)], writes=[("KVSCR",)], dsem="KTO", partial=True)
                S.add("sync", DMA(Vs[l][j][k0:k0 + 512, :].rearrange("(t p) d -> p t d", p=128),
                                  R32[:, 24:32, :].rearrange("p (t a) b -> p t (a b)", a=2)[:, :, 0:dv]),
                      reads=[Rk(24 + q) for q in range(8)], writes=[("KVSCR",)], dsem="VO", partial=True)

    prep_cache(0, cdk, cdv)
    ck(33)
    prep_cache(1, csk, csv)
    ck(35)
    for l in range(2):
        for j in range(NSEQ):
            for t in range(2):
                s = nstg()
                S.add("sync", DMA(STG[:, s, :], cmk[l, j, t * 128:(t + 1) * 128, :]), writes=[("STG", s)], dsem=("STG", s))
                for g in range(2):
                    b = nps()
                    for q in range(4):
                        pq = g * 4 + q
                        S.add("tensor", TR(PS(b)[:, q * 128:(q + 1) * 128], STG[:, s, pq * 128:(pq + 1) * 128], IDF[:, :]),
                              reads=[("STG", s), ("IDF",)], writes=[("PS", b)], partial=True)
                    eng = evac_eng()
                    S.add(eng, copy_op(eng, R32[:, 16 + g * 4:16 + g * 4 + 4, t * 128:(t + 1) * 128],
                                       PS(b).rearrange("p (c t) -> p c t", t=128)),
                          reads=[("PS", b)], writes=[Rk(16 + g * 4 + q) for q in range(4)], partial=True)
                s2 = nstg()
                S.add("sync", DMA(STG[:, s2, :], cmv[l, j, t * 128:(t + 1) * 128, :]), writes=[("STG", s2)], dsem=("STG", s2))
                vo = R32[:, 24 + 2 * t:26 + 2 * t, :].rearrange("p a b -> p (a b)")
                eng = evac_eng()
                S.add(eng, copy_op(eng, vo, STG[:, s2, :]), reads=[("STG", s2)], writes=[Rk(24 + 2 * t), Rk(25 + 2 * t)])
            S.add("sync", DMA(MKs[l][j][:, :, :], R32[:, 16:24, 0:256]),
                  reads=[Rk(16 + q) for q in range(8)], writes=[("MSCR",)], dsem="KTO", partial=True)
            S.add("sync", DMA(MVs[l][j][:, :].rearrange("(t p) d -> p t d", p=128),
                              R32[:, 24:28, :].rearrange("p (t a) b -> p t (a b)", a=2)),
                  reads=[Rk(24 + q) for q in range(4)], writes=[("MSCR",)], dsem="VO", partial=True)

    ck(38)
    def run_block(bi):
        sample = bi == NB
        ntt = 1 if sample else 4
        TB = ntt * 128
        nseq, L = (NSEQ, LS) if sample else (1, 512)
        t0 = bi * 512
        xsrc = xs if sample else xp[t0:t0 + TB, :]
        S.add("sync", DMA(XR[:, 0:ntt, :], xsrc.rearrange("(t p) d -> p t d", p=128)),
              writes=[("XR", t) for t in range(ntt)], dsem="XR")
        rrow = T if sample else t0
        S.add("sync", DMA(ROPE[:, 0:ntt, :], c_rope[rrow:rrow + TB, :].rearrange("(t p) d -> p t d", p=128)),
              writes=[("ROPE",)], dsem="ROPE")
        if sample:
            S.add("sync", DMA(FS[0:8, 0, :], sconv), writes=[("FS", 0)], dsem=("FS", 0))
            b = nps()
            for j in range(4):
                S.add("tensor", TR(PS(b)[:, j * 8:(j + 1) * 8], FS[0:8, 0, j * 128:(j + 1) * 128], IDF[0:8, 0:8]),
                      reads=[("FS", 0), ("IDF",)], writes=[("PS", b)], partial=True)
            S.add("vector", CP(CU[:, :, 0:NSEQ * 34].rearrange("p j (s x) -> p j s x", x=34)[:, :, :, 0:2],
                               PS(b)[:, 0:32].rearrange("p (j s x) -> p j s x", j=4, x=2)),
                  reads=[("PS", b)], writes=[("CU", j) for j in range(4)])
        W34 = L + 2
        cu3 = lambda j: CU[:, j, 0:nseq * W34].rearrange("p (s x) -> p s x", x=W34)
        v3 = lambda ap: ap.rearrange("p (s x) -> p s x", x=L)

        rms_to_HT(ntt)
        rq = lambda tt, a, b_: ROPE[:, tt, a:b_]

        def rope(tt, b, base, out_ap, okeys):
            x3 = PS(b).rearrange("p (h d) -> p h d", d=64)
            fa = nfs(); fb = nfs()
            A3 = FS[:, fa, :].rearrange("p (h d) -> p h d", d=64)
            B3 = FS[:, fb, :].rearrange("p (h d) -> p h d", d=64)
            cosb = rq(tt, base, base + 64).unsqueeze(1).to_broadcast([128, 8, 64])
            s1 = rq(tt, base + 64, base + 96).unsqueeze(1).to_broadcast([128, 8, 32])
            s2 = rq(tt, base + 96, base + 128).unsqueeze(1).to_broadcast([128, 8, 32])
            S.add("vector", TT(A3, x3, cosb, ALU.mult), reads=[("PS", b), ("ROPE",)], writes=[("FS", fa)])
            S.add("vector", TT(B3[:, :, 0:32], x3[:, :, 32:64], s1, ALU.mult), reads=[("PS", b), ("ROPE",)], writes=[("FS", fb)])
            S.add("vector", TT(B3[:, :, 32:64], x3[:, :, 0:32], s2, ALU.mult), reads=[("PS", b), ("ROPE",)], writes=[("FS", fb)], partial=True)
            S.add("gpsimd", TT(out_ap, FS[:, fa, :], FS[:, fb, :], ALU.add), reads=[("FS", fa), ("FS", fb)], writes=okeys)

        def ev_q0(pi, tt, b):
            qb = nbs()
            rope(tt, b, 0, BS[:, qb, :], [("BS", qb)])
            transposes_to(lambda j: BS[:, qb, j * 128:(j + 1) * 128], [("BS", qb)], 4,
                          R32[:, 0:4, tt * 128:(tt + 1) * 128], [Rk(q) for q in range(4)])
        linear_tm(ntt, HT_lhs, HT_keys, 1, ev_q0)

        kdst = sdk if sample else pdk[t0:t0 + TB, :]
        vdst = sdv if sample else pdv[t0:t0 + TB, :]

        def ev_k0(pi, tt, b):
            s = nstg()
            rope(tt, b, 128, STG[:, s, 0:512], [("STG", s)])
            store_rows(kdst[tt * 128:(tt + 1) * 128, :], s, 512)
            kb = nbs()
            S.add("scalar", ACTF(BS[:, kb, :], STG[:, s, 0:512], AF.Copy), reads=[("STG", s)], writes=[("BS", kb)])
            transposes_to(lambda j: BS[:, kb, j * 128:(j + 1) * 128], [("BS", kb)], 4,
                          R32[:, 16:20, tt * 128:(tt + 1) * 128], [Rk(16 + q) for q in range(4)])
        linear_tm(ntt, HT_lhs, HT_keys, 1, ev_k0)

        def VO(tt):
            return R32[:, 24 + 2 * tt:26 + 2 * tt, :].rearrange("p a b -> p (a b)")

        def ev_v0(pi, tt, b):
            s = nstg()
            S.add("scalar", ACTF(STG[:, s, 0:512], PS(b), AF.Copy), reads=[("PS", b)], writes=[("STG", s)])
            store_rows(vdst[tt * 128:(tt + 1) * 128, :], s, 512)
            S.add("vector", CP(VO(tt)[:, 0:512], PS(b)), reads=[("PS", b)], writes=[Rk(24 + 2 * tt), Rk(25 + 2 * tt)])
        linear_tm(ntt, HT_lhs, HT_keys, 1, ev_v0)

        def store_kv(l):
            npair, dv = NP[l], DV[l]
            if not sample:
                S.add("sync", DMA(KTp[l][:, :, t0:t0 + TB], R32[:, 16:16 + npair, 0:TB]),
                      reads=[Rk(16 + q) for q in range(npair)], writes=[("KVSCR",)], dsem="KTO", partial=True)
                S.add("sync", DMA(Vp[l][t0:t0 + TB, :].rearrange("(t p) d -> p t d", p=128),
                                  R32[:, 24:32, :].rearrange("p (t a) b -> p t (a b)", a=2)[:, :, 0:dv]),
                      reads=[Rk(24 + q) for q in range(8)], writes=[("KVSCR",)], dsem="VO", partial=True)
            else:
                for j in range(NSEQ):
                    S.add("sync", DMA(KTs[l][j][:, :, PAST:PAST + LS], R32[:, 16:16 + npair, j * LS:(j + 1) * LS]),
                          reads=[Rk(16 + q) for q in range(npair)], writes=[("KVSCR",)], dsem="KTO", partial=True)
                    S.add("sync", DMA(Vs[l][j][PAST:PAST + LS, :], VO(0)[j * LS:(j + 1) * LS, 0:dv]),
                          reads=[Rk(24), Rk(25)], writes=[("KVSCR",)], dsem="VO", partial=True)
        store_kv(0)

        def ev_gc(pi, j, b):
            S.add("scalar", ACTF(FS[:, j, 0:TB], PS(b)[:, 0:TB], AF.Copy), reads=[("PS", b)], writes=[("FS", j)])
        ctr["fs"] = 3
        linear_fm(ntt, 1, ev_gc)

        def ev_u(pi, j, b):
            S.add("vector", TT(cu3(j)[:, :, 2:2 + L], v3(PS(b)[:, 0:TB]), v3(FS[:, j, 0:TB]), ALU.mult),
                  reads=[("PS", b), ("FS", j)], writes=[("CU", j)])
            S.add("gpsimd", TS(v3(FS[:, j, 0:TB]), cu3(j)[:, :, 2:2 + L], CW[:, j, 2:3], ALU.mult),
                  reads=[("CU", j), ("CW",)], writes=[("FS", j)])
            S.add("vector", STT(v3(FS[:, j, 0:TB]), cu3(j)[:, :, 1:1 + L], CW[:, j, 1:2], v3(FS[:, j, 0:TB]), ALU.mult, ALU.add),
                  reads=[("CU", j), ("CW",), ("FS", j)], writes=[("FS", j)])
            S.add("vector", STT(v3(FS[:, j, 0:TB]), cu3(j)[:, :, 0:L], CW[:, j, 0:1], v3(FS[:, j, 0:TB]), ALU.mult, ALU.add),
                  reads=[("CU", j), ("CW",), ("FS", j)], writes=[("FS", j)])
        ctr["fs"] = 3
        linear_fm(ntt, 1, ev_u)

        def ev_gb(pi, j, b):
            S.add("vector", TT(ATT[:, 4 + j, 0:TB], PS(b)[:, 0:TB], FS[:, j, 0:TB], ALU.mult),
                  reads=[("PS", b), ("FS", j)], writes=[("ATT", 4 + j)])
        linear_fm(ntt, 1, ev_gb)
        if sample or bi == NB - 1:
            b = nps()
            nr = 2 * nseq
            f = 4
            for j in range(4):
                S.add("gpsimd", CP(FS[:, f, j * 8:j * 8 + nr].rearrange("p (s x) -> p s x", x=2), cu3(j)[:, :, L:L + 2]),
                      reads=[("CU", j)], writes=[("FS", f)], partial=True)
            for j in range(4):
                S.add("tensor", TR(PS(b)[0:nr, j * 128:(j + 1) * 128], FS[:, f, j * 8:j * 8 + nr], IDF[:, :]),
                      reads=[("FS", f), ("IDF",)], writes=[("PS", b)], partial=True)
            s = nstg()
            S.add("scalar", ACTF(STG[0:nr, s, 0:512], PS(b)[0:nr, :], AF.Copy), reads=[("PS", b)], writes=[("STG", s)])
            S.add("sync", DMA((sconvo if sample else pconv)[:, :], STG[0:nr, s, 0:512]), reads=[("STG", s)], writes=[("OUT",)],
                  dsem=("STG", s), partial=True)
        if not sample:
            for j in range(4):
                S.add("gpsimd", CP(CU[:, j, 0:2], CU[:, j, 512:514]), reads=[("CU", j)], writes=[("CU", j)])

        def streams_for(l):
            if not sample:
                chunks = [(kc * 512, 512) for kc in range(bi, -1, -1)]
                mbase = 4 if l == 0 else 0

                def maskfn(k0t, mbase=mbase):
                    if k0t >= t0:
                        return MASK[:, mbase + (k0t - t0) // 128, :]
                    return None
                return [(0, 512, KTp[l], Vp[l], chunks, maskfn)]
            st = []
            chunks = [(PAST, LS)] + [(k0, 512) for k0 in range(PAST - 512, -1, -512)]
            for j in range(NSEQ):
                if l == 0:
                    mf = lambda k0t: None
                else:
                    def mf(k0t, j=j):
                        if k0t >= PAST:
                            return MASK[:, 8, :]
                        return None
                st.append((j * LS, LS, KTs[l][j], Vs[l][j], chunks, mf))
            return st

        if sample:
            pass
        attention(0, ntt, streams_for(0))
        linear_tm(ntt, ATT_lhs, ATT_keys, 2, resid_add_evac(ntt))

        def mem_seqs(l):
            if not sample:
                return [(0, 512, l)]
            return None

        def do_mem(l):
            if not sample:
                mem_attention(ntt, l, [(0, 512, l)])
            else:
                seqs = []
                for j in range(NSEQ):
                    seqs.append((j * LS, LS, j % 2))
                mem_attention_sample(l, seqs)
        def mem_attention_sample(l, seqs):
            TBs = 128
            rms_to_HT(1)

            def evq(pi, j, b):
                eng = evac_eng()
                S.add(eng, copy_op(eng, R32[:, 4 * pi + j, 0:TBs], PS(b)[:, 0:TBs], scale=1.0 / 16),
                      reads=[("PS", b)], writes=[Rk(4 * pi + j)])
            linear_fm(1, 2, evq)
            for (qc0, nq, ms) in seqs:
                j = qc0 // LS
                S.add("sync", DMA(MEMK[:, ms, :, :], MKs[l][j][:, :, :]), reads=[("MSCR",)], writes=[("MEMK", ms)], dsem=("MEMK", ms))
                S.add("sync", DMA(MEMV[:, ms, :, :], MVs[l][j][:, :].rearrange("(t p) d -> p t d", p=128)),
                      reads=[("MSCR",)], writes=[("MEMV", ms)], dsem=("MEMV", ms))
                for h in range(4):
                    pts = []
                    for kt in range(2):
                        for dc in range(2):
                            S.add("tensor", MM(PS(kt)[:, 0:nq], MEMK[:, ms, 2 * h + dc, kt * 128:(kt + 1) * 128],
                                               R32[:, 2 * h + dc, qc0:qc0 + nq], dc == 0, dc == 1),
                                  reads=[("MEMK", ms), Rk(2 * h + dc)], writes=[("PS", kt)], partial=True)
                        pb = nbs()
                        S.add("scalar", ACTF(BS[:, pb, 0:nq], PS(kt)[:, 0:nq], AF.Exp), reads=[("PS", kt)], writes=[("BS", pb)])
                        pts.append(pb)
                    for dvc in range(2):
                        for kt in range(2):
                            S.add("tensor", MM(PS(2 + dvc)[:, 0:nq],
                                               MEMV[:, ms, kt, h * 256 + dvc * 128:h * 256 + (dvc + 1) * 128],
                                               BS[:, pts[kt], 0:nq], kt == 0, kt == 1),
                                  reads=[("MEMV", ms), ("BS", pts[kt])], writes=[("PS", 2 + dvc)], partial=True)
                    for kt in range(2):
                        S.add("tensor", MM(PS(4)[:, 0:nq], ONES, BS[:, pts[kt], 0:nq], kt == 0, kt == 1),
                              reads=[("BS", pts[kt]), ("CM",)], writes=[("PS", 4)], partial=True)
                    r = nfs()
                    S.add("vector", RECIP(FS[:, r, 0:nq], PS(4)[:, 0:nq]), reads=[("PS", 4)], writes=[("FS", r)])
                    for dvc in range(2):
                        S.add("vector", TT(ATT[:, 2 * h + dvc, qc0:qc0 + nq], PS(2 + dvc)[:, 0:nq], FS[:, r, 0:nq], ALU.mult),
                              reads=[("PS", 2 + dvc), ("FS", r)], writes=[("ATT", 2 * h + dvc)], partial=True)
            linear_tm(1, ATT_lhs, ATT_keys, 2, resid_add_evac(1))

        do_mem(0)
        ffn(ntt)

        rms_to_HT(ntt)

        def ev_q1(pi, tt, b):
            qb = nbs()
            S.add("scalar", ACTF(BS[:, qb, :], PS(b), AF.Copy, scale=0.125), reads=[("PS", b)], writes=[("BS", qb)])
            transposes_to(lambda j: BS[:, qb, j * 128:(j + 1) * 128], [("BS", qb)], 4,
                          R32[:, 4 * pi:4 * pi + 4, tt * 128:(tt + 1) * 128], [Rk(4 * pi + q) for q in range(4)],
                          scale_neg_dst=R32[:, 8 + 4 * pi:12 + 4 * pi, tt * 128:(tt + 1) * 128],
                          neg_keys=[Rk(8 + 4 * pi + q) for q in range(4)])
        linear_tm(ntt, HT_lhs, HT_keys, 2, ev_q1)
        kdst1 = ssk if sample else psk[t0:t0 + TB, :]
        vdst1 = ssv if sample else psv[t0:t0 + TB, :]

        def ev_k1(pi, tt, b):
            s = nstg()
            S.add("scalar", ACTF(STG[:, s, 0:512], PS(b), AF.Copy), reads=[("PS", b)], writes=[("STG", s)])
            store_rows(kdst1[tt * 128:(tt + 1) * 128, pi * 512:(pi + 1) * 512], s, 512)
            kb = nbs()
            S.add("vector", CP(BS[:, kb, :], PS(b)), reads=[("PS", b)], writes=[("BS", kb)])
            transposes_to(lambda j: BS[:, kb, j * 128:(j + 1) * 128], [("BS", kb)], 4,
                          R32[:, 16 + 4 * pi:20 + 4 * pi, tt * 128:(tt + 1) * 128], [Rk(16 + 4 * pi + q) for q in range(4)])
        linear_tm(ntt, HT_lhs, HT_keys, 2, ev_k1)

        def ev_v1(pi, tt, b):
            s = nstg()
            S.add("scalar", ACTF(STG[:, s, 0:512], PS(b), AF.Copy), reads=[("PS", b)], writes=[("STG", s)])
            store_rows(vdst1[tt * 128:(tt + 1) * 128, pi * 512:(pi + 1) * 512], s, 512)
            S.add("vector", CP(VO(tt)[:, pi * 512:(pi + 1) * 512], PS(b)), reads=[("PS", b)], writes=[Rk(24 + 2 * tt), Rk(25 + 2 * tt)],
                  partial=True)
        linear_tm(ntt, HT_lhs, HT_keys, 2, ev_v1)
        store_kv(1)
        attention(1, ntt, streams_for(1))
        linear_tm(ntt, ATT_lhs, ATT_keys, 2, resid_add_evac(ntt))
        do_mem(1)
        ffn(ntt)

        ydst = y_s if sample else y_p[t0:t0 + TB, :]
        for tt in range(ntt):
            s = ctr["hn"] = (ctr["hn"] + 1) % 2
            S.add("scalar", ACTF(HN[:, s, :], XR[:, tt, :], AF.Square, accum=SM[:, tt:tt + 1]),
                  reads=[("XR", tt)], writes=[("HN", s), ("SMa", tt)])
            S.add("scalar", ACTF(SM[:, 4 + tt:5 + tt], SM[:, tt:tt + 1], AF.Ln, scale=1.0 / D, bias=EPSC[:, 0:1]),
                  reads=[("SMa", tt), ("SMK",)], writes=[("SMb", tt)])
            S.add("scalar", ACTF(SM[:, 8 + tt:9 + tt], SM[:, 4 + tt:5 + tt], AF.Exp, scale=-0.5),
                  reads=[("SMb", tt)], writes=[("SMc", tt)])
            sg = nstg()
            S.add("vector", STT(STG[:, sg, :], XR[:, tt, :], SM[:, 8 + tt:9 + tt], GF[:, :], ALU.mult, ALU.mult),
                  reads=[("XR", tt), ("SMc", tt), ("GF",)], writes=[("STG", sg)])
            store_rows(ydst[tt * 128:(tt + 1) * 128, :], sg, D)

    try:
        ck(40)
        for bi in range(NB + 1):
            run_block(bi)
            ck(50 + bi)
    except _Stop:
        pass

    print("NOPS", len(S.ops), flush=True)
    if os.environ.get("KDUMP"):
        for i, op in enumerate(S.ops[-int(os.environ["KDUMP"]):]):
            print(len(S.ops) - int(os.environ["KDUMP"]) + i, op["eng"], op["dsem"], flush=True)
    S.emit(nc, es)
    es.close()
    return nc


def _consts(T, PAST):
    mats = np.zeros((128, 4, 128), np.float32)
    mats[:, 0, :] = np.eye(128)
    k = np.arange(128)[:, None]; kp = np.arange(128)[None, :]
    mats[:, 1, :] = (k >= kp)
    mats[:, 2, :] = 1.0
    mask = np.zeros((128, 9, 512), np.float32)
    q = np.arange(512)[None, :]
    for j in range(4):
        kk = j * 128 + np.arange(128)[:, None]
        mask[:, j, :] = (kk < q)
        mask[:, 4 + j, :] = ((kk // 64) <= (q // 64))
    kk = np.arange(128)[:, None]
    mask[:, 8, :] = (kk < (q % 32))
    half = 32
    inv = np.power(np.float32(10000.0), -np.arange(half, dtype=np.float32) * np.float32(2.0 / 64)).astype(np.float32)
    pos = np.concatenate([np.arange(T), PAST + (np.arange(128) % 32)]).astype(np.float32)
    ang = (pos[:, None] * inv[None, :]).astype(np.float32)
    cos = np.cos(ang).astype(np.float32); sin = np.sin(ang).astype(np.float32)
    cos2 = np.concatenate([cos, cos], 1); sinS = np.concatenate([-sin, sin], 1)
    rope = np.concatenate([cos2 * 0.125, sinS * 0.125, cos2, sinS], 1).astype(np.float32)
    return mats, mask, rope


_CACHE = {}


def _run(inputs, T, PAST, ncores):
    key = (T, PAST)
    if key not in _CACHE:
        _CACHE[key] = build(T, PAST)
    nc = _CACHE[key]
    mats, mask, rope = _consts(T, PAST)
    f = lambda a: np.ascontiguousarray(np.asarray(a, dtype=np.float32))
    I = {k: np.asarray(v) for k, v in inputs.items()}
    shared = dict(
        w_in0=f(I["w_in_even"][0]), w_out0=f(I["w_out_even"][0]),
        lam4=f(np.stack([I["lambda_q1"][0], I["lambda_k1"][0], I["lambda_q2"][0], I["lambda_k2"][0]])),
        subln=f(I["subln_gain"][0].reshape(128, 1)), convw=f(I["conv_w"][0]),
        w_in1=f(I["w_in_odd"][0]), w_out1=f(I["w_out_odd"][0]),
        nmix=f(I["norm_mix"]), nmem=f(I["norm_mem"]), ncross=f(I["norm_cross"]), nffn=f(I["norm_ffn"]),
        wq=f(I["w_q_mem"]), wk=f(I["w_k_mem"]), wv=f(I["w_v_mem"]), wo=f(I["w_o_mem"]),
        wup=f(I["w_ffn_up"]), wdn=f(I["w_ffn_down"]), nfin=f(I["norm_final"].reshape(1, D)),
        c_mats=mats, c_mask=mask, c_rope=rope,
    )
    def tagged(a, axis, c):
        shp = list(a.shape); shp[axis] = 1
        return np.ascontiguousarray(np.concatenate([a, np.full(shp, float(c), np.float32)], axis=axis))

    TAG = dict(w_in0=0, w_out0=0, w_in1=0, w_out1=0, wq=1, wk=1, wv=1, wo=1, wup=1, wdn=1, c_mask=1, c_rope=0)
    in_maps = []
    for c in range(ncores):
        sl = slice(NSEQ * c, NSEQ * (c + 1))
        m = dict(shared)
        for k, ax in TAG.items():
            m[k] = tagged(shared[k], ax, c)
        m.update(
            xp=f(I["x_prompt"][c]), xs=f(I["x_sample"][sl].reshape(128, D)),
            cdk=f(I["cache_diff_k"][0, sl].reshape(NSEQ, PAST, 512)), cdv=f(I["cache_diff_v"][0, sl].reshape(NSEQ, PAST, 512)),
            sconv=f(I["state_conv"][0, sl].reshape(NSEQ * 2, 512)),
            csk=f(I["cache_sb_k"][0, sl].reshape(NSEQ, PAST, 1024)), csv=f(I["cache_sb_v"][0, sl].reshape(NSEQ, PAST, 1024)),
            cmk=f(I["cache_mem_k"][:, sl].reshape(2, NSEQ, 256, 1024)), cmv=f(I["cache_mem_v"][:, sl].reshape(2, NSEQ, 256, 1024)),
            memp=f(I["mem_prompt"][c]),
        )
        in_maps.append(m)
    res = run_bass_kernel_spmd(nc, in_maps, core_ids=list(range(ncores)))
    R = res.results
    B = ncores
    g = lambda name: np.stack([np.asarray(R[c][name], dtype=np.float32) for c in range(B)])
    y_prompt = g("y_p")
    y_sample = g("y_s").reshape(B * NSEQ, LS, D)
    p_diff_k = g("pdk").reshape(1, B, T, 8, 64)
    p_diff_v = g("pdv").reshape(1, B, T, 4, 128)
    p_conv = g("pconv").reshape(1, B, 2, 512)
    p_sb_k = g("psk").reshape(1, B, T, 16, 64)
    p_sb_v = g("psv").reshape(1, B, T, 16, 64)
    p_mem_k = np.transpose(g("pmk"), (1, 0, 2, 3)).reshape(2, B, 256, 4, 256)
    p_mem_v = np.transpose(g("pmv"), (1, 0, 2, 3)).reshape(2, B, 256, 4, 256)
    s_diff_k = g("sdk").reshape(1, B * NSEQ, LS, 8, 64)
    s_diff_v = g("sdv").reshape(1, B * NSEQ, LS, 4, 128)
    s_conv = g("sconvo").reshape(1, B * NSEQ, 2, 512)
    s_sb_k = g("ssk").reshape(1, B * NSEQ, LS, 16, 64)
    s_sb_v = g("ssv").reshape(1, B * NSEQ, LS, 16, 64)
    return (y_prompt, y_sample, p_diff_k, p_diff_v, p_conv, p_sb_k, p_sb_v, p_mem_k, p_mem_v,
            s_diff_k, s_diff_v, s_conv, s_sb_k, s_sb_v)


def kernel(**inputs):
    T = int(np.asarray(inputs["x_prompt"]).shape[1])
    PAST = int(np.asarray(inputs["cache_diff_k"]).shape[2])
    ncores = int(np.asarray(inputs["x_prompt"]).shape[0])
    return _run(inputs, T, PAST, ncores)
```

```python
import contextlib
import numpy as np
import concourse.bass as bass
import concourse.mybir as mybir
from concourse.bass_utils import run_bass_kernel_spmd

F32 = mybir.dt.float32
BF16 = mybir.dt.bfloat16
AF = mybir.ActivationFunctionType
ALU = mybir.AluOpType

D = 1024
NSEQ = 4
LS = 32
EPS = 1e-6
SUBLN_EPS = 1e-5
LAM_INIT0 = 0.2
NWS = 3
NKV = 3
NSTG = 3
NFS = 8
NBS = 10


class Sched:
    ENGS = ["sync", "scalar", "vector", "gpsimd", "tensor"]

    def __init__(self):
        self.ops = []
        self.state = {}
        self.dma_count = {}
        self.frozen = False
        import os
        self.maxops = int(os.environ.get("KOPS", "100000000"))

    def add(self, eng, fn, reads=(), writes=(), dsem=None, partial=False):
        if self.frozen or len(self.ops) >= self.maxops:
            return -1
        idx = len(self.ops)
        deps = set()
        for k in reads:
            st = self.state.setdefault(k, [[], [], []])
            deps.update(st[0])
            if k[0] == "PS":
                deps.update(r for r in st[1] if self.ops[r]["eng"] != eng)
        for k in writes:
            st = self.state.setdefault(k, [[], [], []])
            if st[1] or not partial:
                deps.update(st[1])
                deps.update(st[0])
                st[2] = list(st[1]) + list(st[0])
                st[0] = [idx]
                st[1] = []
            else:
                deps.update(st[2])
                st[0].append(idx)
        for k in reads:
            self.state[k][1].append(idx)
        waits = []
        for d in deps:
            od = self.ops[d]
            if od["dsem"] is not None:
                waits.append(("dma", od["dsem"], 16 * self.dma_count[od["dsem"]]))
            else:
                if od["eng"] == eng and eng == "tensor":
                    continue
                od["signaled"] = True
                waits.append(("eng", d))
        op = dict(eng=eng, fn=fn, dsem=dsem, signaled=False, waits=waits)
        if dsem is not None:
            self.dma_count[dsem] = self.dma_count.get(dsem, 0) + 1
        self.ops.append(op)
        return idx

    def emit(self, nc, es):
        seenidx = {e: {} for e in self.ENGS}
        for op in self.ops:
            op["signaled"] = False
        for op in self.ops:
            need = {}
            for w in op["waits"]:
                if w[0] == "eng":
                    E = self.ops[w[1]]["eng"]
                    need[E] = max(need.get(E, -1), w[1])
            ew = []
            for E, d in need.items():
                if seenidx[op["eng"]].get(E, -1) >= d:
                    continue
                seenidx[op["eng"]][E] = d
                ew.append(("eng", d))
                self.ops[d]["signaled"] = True
            op["waits"] = [w for w in op["waits"] if w[0] == "dma"] + ew
        cnt = {e: 0 for e in self.ENGS}
        for op in self.ops:
            if op["dsem"] is None and op["signaled"]:
                cnt[op["eng"]] += 1
                op["sig"] = cnt[op["eng"]]
        esem = {e: es.enter_context(nc.semaphore("se_" + e)) for e in self.ENGS}
        dsem = {k: es.enter_context(nc.semaphore("sd_%d" % i)) for i, k in enumerate(sorted(self.dma_count, key=str))}
        block = es.enter_context(nc.Block())
        ops = self.ops

        def run(engname):
            def body(e):
                seen = {}
                for op in ops:
                    if op["eng"] != engname:
                        continue
                    for w in op["waits"]:
                        if w[0] == "dma":
                            s, v = dsem[w[1]], w[2]
                            key = ("d", w[1])
                        else:
                            od = ops[w[1]]
                            s, v = esem[od["eng"]], od["sig"]
                            key = ("e", od["eng"])
                        if seen.get(key, 0) >= v:
                            continue
                        seen[key] = v
                        e.wait_ge(s, v)
                    ins = op["fn"](e)
                    if op["dsem"] is not None:
                        ins.then_inc(dsem[op["dsem"]], 16)
                    elif op["signaled"]:
                        ins.then_inc(esem[engname], 1)
                if engname == "sync":
                    for k, c in self.dma_count.items():
                        e.wait_ge(dsem[k], 16 * c)
            return body

        block.sync(run("sync"))
        block.scalar(run("scalar"))
        block.vector(run("vector"))
        block.gpsimd(run("gpsimd"))
        block.tensor(run("tensor"))


class _Stop(Exception):
    pass


def build(T, PAST):
    import os
    STOP = int(os.environ.get("KSTOP", "9999"))

    def ck(n):
        if n >= STOP:
            S.frozen = True
    NB = T // 512
    NKS = PAST + LS
    nc = bass.Bass("TRN2", target_bir_lowering=False)
    es = contextlib.ExitStack()
    S = Sched()

    def din(name, shape, dt=F32):
        return nc.dram_tensor(name, list(shape), dt, kind="ExternalInput").ap()

    def dout(name, shape):
        return nc.dram_tensor(name, list(shape), F32, kind="ExternalOutput").ap()

    def dscr(name, shape, dt=BF16):
        return nc.dram_tensor(name, list(shape), dt).ap()

    xp = din("xp", [T, D]); xs = din("xs", [128, D])
    cdk = din("cdk", [NSEQ, PAST, 512]); cdv = din("cdv", [NSEQ, PAST, 512])
    sconv = din("sconv", [NSEQ * 2, 512])
    csk = din("csk", [NSEQ, PAST, 1024]); csv = din("csv", [NSEQ, PAST, 1024])
    cmk = din("cmk", [2, NSEQ, 256, 1024]); cmv = din("cmv", [2, NSEQ, 256, 1024])
    memp = din("memp", [256, D])
    w_in0 = din("w_in0", [D + 1, 3072]); w_out0 = din("w_out0", [D + 1, D])
    lam4 = din("lam4", [4, 64]); subln = din("subln", [128, 1]); convw = din("convw", [3, 512])
    w_in1 = din("w_in1", [D + 1, 3072]); w_out1 = din("w_out1", [D + 1, D])
    nmix = din("nmix", [2, D]); nmem = din("nmem", [2, D]); ncross = din("ncross", [2, D]); nffn = din("nffn", [2, D])
    wq = din("wq", [2, D + 1, D]); wk = din("wk", [2, D + 1, D]); wv = din("wv", [2, D + 1, D]); wo = din("wo", [2, D + 1, D])
    wup = din("wup", [2, D + 1, 4096]); wdn = din("wdn", [2, 4097, D]); nfin = din("nfin", [1, D])
    c_mats = din("c_mats", [128, 4, 128])
    c_mask = din("c_mask", [128, 10, 512])
    c_rope = din("c_rope", [T + 129, 256])

    y_p = dout("y_p", [T, D]); y_s = dout("y_s", [128, D])
    pdk = dout("pdk", [T, 512]); pdv = dout("pdv", [T, 512]); pconv = dout("pconv", [2, 512])
    psk = dout("psk", [T, 1024]); psv = dout("psv", [T, 1024])
    pmk = dout("pmk", [2, 256, 1024]); pmv = dout("pmv", [2, 256, 1024])
    sdk = dout("sdk", [128, 512]); sdv = dout("sdv", [128, 512]); sconvo = dout("sconvo", [NSEQ * 2, 512])
    ssk = dout("ssk", [128, 1024]); ssv = dout("ssv", [128, 1024])

    wb_in = [dscr("wb_in0", [D, 3072]), dscr("wb_in1", [D, 3072])]
    wb_out = [dscr("wb_out0", [D, D]), dscr("wb_out1", [D, D])]
    wb_q = [dscr("wb_q%d" % l, [D, D]) for l in range(2)]
    wb_k = [dscr("wb_k%d" % l, [D, D]) for l in range(2)]
    wb_v = [dscr("wb_v%d" % l, [D, D]) for l in range(2)]
    wb_o = [dscr("wb_o%d" % l, [D, D]) for l in range(2)]
    wb_up = [dscr("wb_up%d" % l, [D, 4096]) for l in range(2)]
    wb_dn = [dscr("wb_dn%d" % l, [4096, D]) for l in range(2)]
    NP = [4, 8]
    DV = [512, 1024]
    KTp = [dscr("KTp%d" % l, [128, NP[l], T]) for l in range(2)]
    Vp = [dscr("Vp%d" % l, [T, DV[l]]) for l in range(2)]
    KTs = [[dscr("KTs%d_%d" % (l, j), [128, NP[l], NKS]) for j in range(NSEQ)] for l in range(2)]
    Vs = [[dscr("Vs%d_%d" % (l, j), [NKS, DV[l]]) for j in range(NSEQ)] for l in range(2)]
    MKs = [[dscr("MKs%d_%d" % (l, j), [128, 8, 256]) for j in range(NSEQ)] for l in range(2)]
    MVs = [[dscr("MVs%d_%d" % (l, j), [256, 1024]) for j in range(NSEQ)] for l in range(2)]

    _dm = int(os.environ.get("KDUMMY", "0"))
    if _dm:
        dummies = [dscr("dummy_scr%d" % i, [_dm * 1024, 512]) for i in range(int(os.environ.get("KDUMMYN", "1")))]
    def sb(name, shape, dt):
        return es.enter_context(nc.sbuf_tensor(name, list(shape), dt))

    XR = sb("XR", [128, 4, D], F32)
    HN = sb("HN", [128, 2, D], BF16)
    HT = sb("HT", [128, 8, 512], BF16)
    WS = sb("WS", [128, NWS, 8, 512], BF16)
    R32 = sb("R32", [128, 32, 512], BF16)
    ATT = sb("ATT", [128, 8, 512], BF16)
    MEMK = sb("MEMK", [128, 2, 8, 256], BF16)
    MEMV = sb("MEMV", [128, 2, 2, 1024], BF16)
    STG = sb("STG", [128, NSTG, D], F32)
    FS = sb("FS", [128, NFS, 512], F32)
    BS = sb("BS", [128, NBS, 512], BF16)
    ACC = sb("ACC", [128, 2, 3, 512], BF16)
    KVK = sb("KVK", [128, NKV, 512], BF16)
    KVV0 = sb("KVV0", [128, NKV, 4, 128], BF16)
    KVV1 = sb("KVV1", [128, NKV, 4, 2, 128], BF16)
    MASK = sb("MASK", [128, 9, 512], BF16)
    CM = sb("CM", [128, 4, 128], BF16)
    IDF = sb("IDF", [128, 128], F32)
    ROPE = sb("ROPE", [128, 4, 256], F32)
    GF = sb("GF", [128, D], F32)
    SM = sb("SM", [128, 64], F32)
    CW = sb("CW", [128, 4, 3], F32)
    GN = sb("GN", [128, 9, 8], F32)
    CU = sb("CU", [128, 4, 520], F32)
    LAMT = sb("LAMT", [128, 4, 64], F32)
    PSB = [es.enter_context(nc.psum_tensor("PS%d" % b, [128, 512], F32)) for b in range(8)]

    IDB = CM[:, 0, :]
    TRI = CM[:, 1, :]
    ONES = CM[:, 2, :]

    def PS(b):
        return PSB[b][:, :]

    def PSbf(b):
        return PSB[b][:, :].bitcast(BF16)

    def MM(out, lhsT, rhs, start, stop):
        return lambda e: e.matmul(out, lhsT=lhsT, rhs=rhs, start=start, stop=stop, skip_group_check=True)

    def TR(out, in_, ident):
        return lambda e: e.transpose(out, in_, ident)

    def ACTF(out, in_, func, scale=1.0, bias=None, accum=None):
        def f(e):
            kw = {}
            if bias is not None:
                kw["bias"] = bias
            if accum is not None:
                kw["accum_out"] = accum
            return e.activation(out=out, in_=in_, func=func, scale=scale, **kw)
        return f

    def TT(out, a, b, op):
        return lambda e: e.tensor_tensor(out=out, in0=a, in1=b, op=op)

    def TS(out, a, s1, op0, s2=None, op1=None):
        if op1 is None:
            return lambda e: e.tensor_scalar(out=out, in0=a, scalar1=s1, scalar2=None, op0=op0)
        return lambda e: e.tensor_scalar(out=out, in0=a, scalar1=s1, scalar2=s2, op0=op0, op1=op1)

    def STT(out, a, scalar, b, op0, op1):
        return lambda e: e.scalar_tensor_tensor(out=out, in0=a, scalar=scalar, in1=b, op0=op0, op1=op1)

    def CP(out, in_):
        return lambda e: e.tensor_copy(out=out, in_=in_)

    def RECIP(out, in_):
        return lambda e: e.reciprocal(out=out, in_=in_)

    def MSET(ap, v):
        return lambda e: e.memset(ap, v)

    def DMA(out, in_, slow=False):
        if slow:
            return lambda e: e.dma_start(out=out, in_=in_, allow_slow_non_contiguous=True)
        return lambda e: e.dma_start(out=out, in_=in_)

    ctr = dict(fs=0, bs=0, stg=0, ps=0, tp=0, hn=0, evac=0)

    def nfs():
        ctr["fs"] = (ctr["fs"] + 1) % NFS
        return ctr["fs"]

    def nbs():
        ctr["bs"] = (ctr["bs"] + 1) % NBS
        return ctr["bs"]

    def nstg():
        ctr["stg"] = (ctr["stg"] + 1) % NSTG
        return ctr["stg"]

    def nps():
        ctr["ps"] = (ctr["ps"] + 1) % 6
        return ctr["ps"]

    def ntp():
        ctr["tp"] = (ctr["tp"] + 1) % 2
        return 6 + ctr["tp"]

    def evac_eng():
        ctr["evac"] += 1
        return "vector" if ctr["evac"] % 2 else "scalar"

    def copy_op(eng, out, in_, scale=None):
        if eng == "scalar":
            return ACTF(out, in_, AF.Copy, scale=1.0 if scale is None else scale)
        if scale is None:
            return CP(out, in_)
        return TS(out, in_, float(scale), ALU.mult)

    Rk = lambda i: ("R", i)

    panels = []
    wstate = dict(issued=0, cur=-1)

    def panel_ap(w, r0, c0):
        return w[r0:r0 + 1024, c0:c0 + 512].rearrange("(c p) n -> p c n", p=128)

    def block_panels():
        pl = []
        for c0 in (0, 512, 1024, 2048, 2560, 1536):
            pl.append(panel_ap(wb_in[0], 0, c0))
        for l in range(2):
            if l == 1:
                for c0 in range(0, 3072, 512):
                    pl.append(panel_ap(wb_in[1], 0, c0))
            for c0 in (0, 512):
                pl.append(panel_ap(wb_out[l], 0, c0))
            for c0 in (0, 512):
                pl.append(panel_ap(wb_q[l], 0, c0))
            for c0 in (0, 512):
                pl.append(panel_ap(wb_o[l], 0, c0))
            for c0 in range(0, 4096, 512):
                pl.append(panel_ap(wb_up[l], 0, c0))
            for c0 in (0, 512):
                for r0 in range(0, 4096, 1024):
                    pl.append(panel_ap(wb_dn[l], r0, c0))
        return pl

    def issue_panels(upto):
        while wstate["issued"] <= min(upto, len(panels) - 1):
            i = wstate["issued"]
            s = i % NWS
            S.add("sync", DMA(WS[:, s, :, :], panels[i]), reads=[("WSCR",)], writes=[("WS", s)], dsem=("WS", s))
            wstate["issued"] += 1

    def next_panel():
        wstate["cur"] += 1
        i = wstate["cur"]
        issue_panels(i + NWS - 1)
        return i % NWS

    def rms_to_HT(ntt):
        for tt in range(ntt):
            s = ctr["hn"] = (ctr["hn"] + 1) % 2
            S.add("scalar", ACTF(HN[:, s, :], XR[:, tt, :], AF.Square, accum=SM[:, tt:tt + 1]),
                  reads=[("XR", tt)], writes=[("HN", s), ("SMa", tt)])
            S.add("scalar", ACTF(SM[:, 4 + tt:5 + tt], SM[:, tt:tt + 1], AF.Ln, scale=1.0 / D, bias=EPSC[:, 0:1]),
                  reads=[("SMa", tt), ("SMK",)], writes=[("SMb", tt)])
            S.add("scalar", ACTF(SM[:, 8 + tt:9 + tt], SM[:, 4 + tt:5 + tt], AF.Exp, scale=-0.5),
                  reads=[("SMb", tt)], writes=[("SMc", tt)])
            S.add("vector", TS(HN[:, s, :], XR[:, tt, :], SM[:, 8 + tt:9 + tt], ALU.mult),
                  reads=[("XR", tt), ("SMc", tt)], writes=[("HN", s)])
            b = ntp()
            for c in range(8):
                S.add("tensor", TR(PSbf(b)[:, c * 128:(c + 1) * 128], HN[:, s, c * 128:(c + 1) * 128], IDB),
                      reads=[("HN", s)], writes=[("PS", b)], partial=True)
            eng = evac_eng()
            S.add(eng, copy_op(eng, HT[:, :, tt * 128:(tt + 1) * 128],
                               PSbf(b).rearrange("p (c t) -> p c t", t=128)),
                  reads=[("PS", b)], writes=[("HT", tt)])

    def linear_tm(ntt, lhs_fn, lhs_keys_fn, npanels, evac):
        for pi in range(npanels):
            s = next_panel()
            for tt in range(ntt):
                b = nps()
                for c in range(8):
                    S.add("tensor", MM(PS(b), lhs_fn(c, tt), WS[:, s, c, :], c == 0, c == 7),
                          reads=[("WS", s)] + lhs_keys_fn(c, tt), writes=[("PS", b)], partial=True)
                evac(pi, tt, b)

    def linear_fm(ntt, npanels, evac):
        TB = ntt * 128
        for pi in range(npanels):
            s = next_panel()
            for j in range(4):
                b = nps()
                for c in range(8):
                    S.add("tensor", MM(PS(b)[:, 0:TB], WS[:, s, c, j * 128:(j + 1) * 128], HT[:, c, 0:TB], c == 0, c == 7),
                          reads=[("WS", s)] + [("HT", t) for t in range(ntt)], writes=[("PS", b)], partial=True)
                evac(pi, j, b)

    HT_lhs = lambda c, tt: HT[:, c, tt * 128:(tt + 1) * 128]
    HT_keys = lambda c, tt: [("HT", tt)]
    ATT_lhs = lambda c, tt: ATT[:, c, tt * 128:(tt + 1) * 128]
    ATT_keys = lambda c, tt: [("ATT", c)]

    def resid_add_evac(ntt):
        def ev(pi, tt, b):
            S.add("vector", TT(XR[:, tt, pi * 512:(pi + 1) * 512], PS(b), XR[:, tt, pi * 512:(pi + 1) * 512], ALU.add),
                  reads=[("PS", b), ("XR", tt)], writes=[("XR", tt)])
        return ev

    def store_rows(dst_ap, stg_s, ncols):
        S.add("sync", DMA(dst_ap, STG[:, stg_s, 0:ncols]), reads=[("STG", stg_s)], writes=[("OUT",)],
              dsem=("STG", stg_s), partial=True)

    def transposes_to(src_ap_fn, src_keys, n, dst_ap, dst_keys, scale_neg_dst=None, neg_keys=None):
        b = ntp()
        for j in range(n):
            S.add("tensor", TR(PSbf(b)[:, j * 128:(j + 1) * 128], src_ap_fn(j), IDB),
                  reads=src_keys, writes=[("PS", b)], partial=True)
        src = PSbf(b)[:, 0:n * 128].rearrange("p (c t) -> p c t", t=128)
        S.add("vector", CP(dst_ap, src), reads=[("PS", b)], writes=dst_keys)
        if scale_neg_dst is not None:
            S.add("scalar", ACTF(scale_neg_dst, src, AF.Copy, scale=-1.0), reads=[("PS", b)], writes=neg_keys)

    kvstate = dict(n=0)
    RACC = 3

    def attention(kind, ntt, streams):
        TB = ntt * 128
        npairs = NP[kind]
        D1 = 2
        D2 = 3 if kind == 1 else 2
        QT = lambda p: R32[:, p, :]
        QN = lambda p: R32[:, 8 + p, :]
        for p in range(npairs):
            qkeys = [Rk(p)] + ([Rk(8 + p)] if kind == 1 else [])
            chunks_all = []
            units = []
            for si, (qc0, nq, KTd, Vd, chunks, maskfn) in enumerate(streams):
                ntiles_total = sum((nkc + 127) // 128 for (_, nkc) in chunks)
                tcount = 0
                for (k0, nkc) in chunks:
                    ci = len(chunks_all)
                    chunks_all.append((si, k0, nkc))
                    nt = (nkc + 127) // 128
                    for t in range(nt - 1, -1, -1):
                        nk = min(128, nkc - t * 128)
                        tcount += 1
                        for a in range(2):
                            units.append(dict(si=si, ci=ci, t=t, nk=nk, a=a, qc0=qc0, nq=nq,
                                              mask=maskfn(k0 + t * 128), first=(tcount == 1), last=(tcount == ntiles_total)))
            cslot = {}

            def load_chunk(ci):
                if ci in cslot or ci >= len(chunks_all):
                    return
                si, k0, nkc = chunks_all[ci]
                KTd, Vd = streams[si][2], streams[si][3]
                kvstate["n"] += 1
                sl = kvstate["n"] % NKV
                cslot[ci] = sl
                S.add("sync", DMA(KVK[:, sl, 0:nkc], KTd[:, p, k0:k0 + nkc]),
                      reads=[("KVSCR",)], writes=[("KVK", sl)], dsem=("KV", sl), partial=True)
                nt = (nkc + 127) // 128
                if kind == 0:
                    if nkc >= 128:
                        S.add("sync", DMA(KVV0[:, sl, 0:nt, :],
                                          Vd[k0:k0 + nkc, p * 128:(p + 1) * 128].rearrange("(t q) d -> q t d", q=128)),
                              reads=[("KVSCR",)], writes=[("KVV", sl)], dsem=("KV", sl), partial=True)
                    else:
                        S.add("sync", DMA(KVV0[0:nkc, sl, 0, :], Vd[k0:k0 + nkc, p * 128:(p + 1) * 128]),
                              reads=[("KVSCR",)], writes=[("KVV", sl)], dsem=("KV", sl), partial=True)
                else:
                    for a in range(2):
                        c0 = p * 128 + a * 64
                        if nkc >= 128:
                            S.add("sync", DMA(KVV1[:, sl, 0:nt, a, a * 64:(a + 1) * 64],
                                              Vd[k0:k0 + nkc, c0:c0 + 64].rearrange("(t q) d -> q t d", q=128)),
                                  reads=[("KVSCR",)], writes=[("KVV", sl)], dsem=("KV", sl), partial=True)
                        else:
                            S.add("sync", DMA(KVV1[0:nkc, sl, 0, a, a * 64:(a + 1) * 64], Vd[k0:k0 + nkc, c0:c0 + 64]),
                                  reads=[("KVSCR",)], writes=[("KVV", sl)], dsem=("KV", sl), partial=True)

            accr = [0, 0]
            NU = len(units)
            for s_ in range(NU + D2):
                if s_ < NU:
                    u = units[s_]
                    load_chunk(u["ci"])
                    if s_ == 0 or units[s_ - 1]["ci"] != u["ci"]:
                        load_chunk(u["ci"] + 1)
                    sl = u["sl"] = cslot[u["ci"]]
                    a, nk, nq, qc0, t = u["a"], u["nk"], u["nq"], u["qc0"], u["t"]
                    pa = slice(64 * a, 64 * a + 64)
                    kc = slice(t * 128, t * 128 + nk)
                    mask = u["mask"]
                    if kind == 0:
                        zb = u["bank"] = 4 + (s_ % 4)
                        Z = PS(zb)[0:nk, qc0:qc0 + nq]
                        S.add("tensor", MM(Z, KVK[pa, sl, kc], QT(p)[pa, qc0:qc0 + nq], True, True),
                              reads=[("KVK", sl)] + qkeys, writes=[("PS", zb)])
                        pb = u["pb"] = nbs()
                        Pt = BS[0:nk, pb, 0:nq]
                        S.add("scalar", ACTF(Pt, Z, AF.Exp), reads=[("PS", zb)], writes=[("BS", pb)])
                        if mask is not None:
                            S.add("gpsimd", TT(Pt, Pt, mask[0:nk, qc0:qc0 + nq], ALU.mult),
                                  reads=[("BS", pb), ("MASK",)], writes=[("BS", pb)])
                    else:
                        zb = u["bank"] = 1 + (s_ % 6)
                        Z = PS(zb)[0:nk, qc0:qc0 + nq]
                        S.add("tensor", MM(Z, KVK[pa, sl, kc], QT(p)[pa, qc0:qc0 + nq], True, True),
                              reads=[("KVK", sl)] + qkeys, writes=[("PS", zb)])
                        fe = u["fe"] = nfs()
                        E = FS[0:nk, fe, 0:nq]
                        S.add("scalar", ACTF(E, Z, AF.Exp), reads=[("PS", zb)], writes=[("FS", fe)])
                if kind == 1 and 0 <= s_ - D1 < NU:
                    u = units[s_ - D1]
                    sl, a, nk, nq, qc0, t = u["sl"], u["a"], u["nk"], u["nq"], u["qc0"], u["t"]
                    pa = slice(64 * a, 64 * a + 64)
                    kc = slice(t * 128, t * 128 + nk)
                    cb = u["bank"]
                    C = PS(cb)[0:nk, qc0:qc0 + nq]
                    Lt = BS[0:nk, u["lb"], 0:nq]
                    S.add("tensor", MM(C, TRI[0:nk, 0:nk], Lt, True, False),
                          reads=[("BS", u["lb"]), ("CM",)], writes=[("PS", cb)])
                    if not u["first"]:
                        S.add("tensor", MM(C, ONES[:, 0:nk], ACC[:, a, u["acc_in"], 0:nq], False, False),
                              reads=[("ACC", a, u["acc_in"]), ("CM",)], writes=[("PS", cb)], partial=True)
                    S.add("tensor", MM(C, KVK[pa, sl, kc], QN(p)[pa, qc0:qc0 + nq], False, True),
                          reads=[("KVK", sl)] + qkeys, writes=[("PS", cb)], partial=True)
                    wb = u["wb"] = nbs()
                    Wt = BS[0:nk, wb, 0:nq]
                    S.add("scalar", ACTF(Wt, C, AF.Exp, scale=-1.0), reads=[("PS", cb)], writes=[("BS", wb)])
                    if u["mask"] is not None:
                        S.add("gpsimd", TT(Wt, Wt, u["mask"][0:nk, qc0:qc0 + nq], ALU.mult),
                              reads=[("BS", wb), ("MASK",)], writes=[("BS", wb)])
                if kind == 1 and s_ < NU:
                    u = units[s_]
                    a, nk, nq, qc0, mask = u["a"], u["nk"], u["nq"], u["qc0"], u["mask"]
                    fe = u["fe"]
                    E = FS[0:nk, fe, 0:nq]
                    lb = u["lb"] = nbs()
                    Lt = BS[0:nk, lb, 0:nq]
                    S.add("scalar", ACTF(Lt, E, AF.Ln, bias=ONEC[0:nk, 0:1]), reads=[("FS", fe), ("SMK",)], writes=[("BS", lb)])
                    if mask is not None:
                        S.add("gpsimd", TT(Lt, Lt, mask[0:nk, qc0:qc0 + nq], ALU.mult),
                              reads=[("BS", lb), ("MASK",)], writes=[("BS", lb)])
                    u["acc_in"] = accr[a]
                    if not u["last"]:
                        rn = (accr[a] + 1) % RACC
                        if u["first"]:
                            if nk < 128:
                                S.add("gpsimd", MSET(ACC[:, a, rn, :], 0.0), writes=[("ACC", a, rn)])
                            S.add("gpsimd", CP(ACC[0:nk, a, rn, 0:nq], Lt), reads=[("BS", lb)], writes=[("ACC", a, rn)])
                        else:
                            S.add("gpsimd", TT(ACC[:, a, rn, 0:nq], ACC[:, a, accr[a], 0:nq], Lt, ALU.add),
                                  reads=[("BS", lb), ("ACC", a, accr[a])], writes=[("ACC", a, rn)])
                        accr[a] = rn
                if 0 <= s_ - D2 < NU:
                    u = units[s_ - D2]
                    sl, a, nk, nq, qc0, t = u["sl"], u["a"], u["nk"], u["nq"], u["qc0"], u["t"]
                    if kind == 0:
                        Pt = BS[0:nk, u["pb"], 0:nq]
                        S.add("tensor", MM(PS(a)[:, qc0:qc0 + nq], KVV0[0:nk, sl, t, :], Pt, u["first"], u["last"]),
                              reads=[("KVV", sl), ("BS", u["pb"])], writes=[("PS", a)], partial=True)
                        S.add("tensor", MM(PS(2 + a)[:, qc0:qc0 + nq], ONES[0:nk, :], Pt, u["first"], u["last"]),
                              reads=[("BS", u["pb"]), ("CM",)], writes=[("PS", 2 + a)], partial=True)
                    else:
                        Wt = BS[0:nk, u["wb"], 0:nq]
                        S.add("tensor", MM(PS(0)[:, qc0:qc0 + nq], KVV1[0:nk, sl, t, a, :], Wt,
                                           u["first"] and a == 0, u["last"] and a == 1),
                              reads=[("KVV", sl), ("BS", u["wb"])], writes=[("PS", 0)], partial=True)
            if kind == 0:
                osb = []
                for a in range(2):
                    r = nfs()
                    S.add("vector", RECIP(FS[:, r, 0:TB], PS(2 + a)[:, 0:TB]), reads=[("PS", 2 + a)], writes=[("FS", r)])
                    o = nfs()
                    S.add("vector", TT(FS[:, o, 0:TB], PS(a)[:, 0:TB], FS[:, r, 0:TB], ALU.mult),
                          reads=[("PS", a), ("FS", r)], writes=[("FS", o)])
                    osb.append(o)
                oc = nfs()
                S.add("vector", STT(FS[:, oc, 0:TB], FS[:, osb[1], 0:TB], NLAM[:, 0:1], FS[:, osb[0], 0:TB], ALU.mult, ALU.add),
                      reads=[("FS", osb[0]), ("FS", osb[1]), ("LAM",)], writes=[("FS", oc)])
                sq = nbs()
                S.add("gpsimd", TT(BS[:, sq, 0:TB], FS[:, oc, 0:TB], FS[:, oc, 0:TB], ALU.mult),
                      reads=[("FS", oc)], writes=[("BS", sq)])
                S.add("tensor", MM(PS(4)[:, 0:TB], ONES, BS[:, sq, 0:TB], True, True),
                      reads=[("BS", sq), ("CM",)], writes=[("PS", 4)])
                ln = nfs()
                S.add("scalar", ACTF(FS[:, ln, 0:TB], PS(4)[:, 0:TB], AF.Ln, scale=1.0 / 128, bias=SEPSC[:, 0:1]),
                      reads=[("PS", 4), ("SMK",)], writes=[("FS", ln)])
                rs = nfs()
                S.add("scalar", ACTF(FS[:, rs, 0:TB], FS[:, ln, 0:TB], AF.Exp, scale=-0.5),
                      reads=[("FS", ln)], writes=[("FS", rs)])
                S.add("vector", STT(ATT[:, p, 0:TB], FS[:, oc, 0:TB], GSUB[:, 0:1], FS[:, rs, 0:TB], ALU.mult, ALU.mult),
                      reads=[("FS", oc), ("FS", rs), ("GSUB",)], writes=[("ATT", p)])
            else:
                S.add("vector", CP(ATT[:, p, 0:TB], PS(0)[:, 0:TB]), reads=[("PS", 0)], writes=[("ATT", p)])

    def mem_attention(ntt, l, seqs):
        TB = ntt * 128
        rms_to_HT(ntt)

        def evq(pi, j, b):
            eng = evac_eng()
            S.add(eng, copy_op(eng, R32[:, 4 * pi + j, 0:TB], PS(b)[:, 0:TB], scale=1.0 / 16),
                  reads=[("PS", b)], writes=[Rk(4 * pi + j)])
        linear_fm(ntt, 2, evq)
        for h in range(4):
            pts = []
            for kt in range(2):
                for (qc0, nq, ms) in seqs:
                    for dc in range(2):
                        S.add("tensor", MM(PS(kt)[:, qc0:qc0 + nq], MEMK[:, ms, 2 * h + dc, kt * 128:(kt + 1) * 128],
                                           R32[:, 2 * h + dc, qc0:qc0 + nq], dc == 0, dc == 1),
                              reads=[("MEMK", ms), Rk(2 * h + dc)], writes=[("PS", kt)], partial=True)
                pb = nbs()
                S.add("scalar", ACTF(BS[:, pb, 0:TB], PS(kt)[:, 0:TB], AF.Exp), reads=[("PS", kt)], writes=[("BS", pb)])
                pts.append(pb)
            for (qc0, nq, ms) in seqs:
                for dvc in range(2):
                    for kt in range(2):
                        S.add("tensor", MM(PS(2 + dvc)[:, qc0:qc0 + nq],
                                           MEMV[:, ms, kt, h * 256 + dvc * 128:h * 256 + (dvc + 1) * 128],
                                           BS[:, pts[kt], qc0:qc0 + nq], kt == 0, kt == 1),
                              reads=[("MEMV", ms), ("BS", pts[kt])], writes=[("PS", 2 + dvc)], partial=True)
                for kt in range(2):
                    S.add("tensor", MM(PS(4)[:, qc0:qc0 + nq], ONES, BS[:, pts[kt], qc0:qc0 + nq], kt == 0, kt == 1),
                          reads=[("BS", pts[kt]), ("CM",)], writes=[("PS", 4)], partial=True)
            r = nfs()
            S.add("vector", RECIP(FS[:, r, 0:TB], PS(4)[:, 0:TB]), reads=[("PS", 4)], writes=[("FS", r)])
            for dvc in range(2):
                S.add("vector", TT(ATT[:, 2 * h + dvc, 0:TB], PS(2 + dvc)[:, 0:TB], FS[:, r, 0:TB], ALU.mult),
                      reads=[("PS", 2 + dvc), ("FS", r)], writes=[("ATT", 2 * h + dvc)])
        linear_tm(ntt, ATT_lhs, ATT_keys, 2, resid_add_evac(ntt))

    def ffn(ntt):
        TB = ntt * 128
        rms_to_HT(ntt)

        def evu(pi, j, b):
            f = nfs()
            S.add("scalar", ACTF(FS[:, f, 0:TB], PS(b)[:, 0:TB], AF.Relu), reads=[("PS", b)], writes=[("FS", f)])
            S.add("vector", TT(R32[:, 4 * pi + j, 0:TB], FS[:, f, 0:TB], FS[:, f, 0:TB], ALU.mult),
                  reads=[("FS", f)], writes=[Rk(4 * pi + j)])
        linear_fm(ntt, 8, evu)
        for half in range(2):
            for kq in range(4):
                s = next_panel()
                for tt in range(ntt):
                    for c in range(8):
                        S.add("tensor", MM(PS(tt), R32[:, kq * 8 + c, tt * 128:(tt + 1) * 128], WS[:, s, c, :],
                                           kq == 0 and c == 0, kq == 3 and c == 7),
                              reads=[("WS", s), Rk(kq * 8 + c)], writes=[("PS", tt)], partial=True)
            for tt in range(ntt):
                S.add("vector", TT(XR[:, tt, half * 512:(half + 1) * 512], PS(tt), XR[:, tt, half * 512:(half + 1) * 512], ALU.add),
                      reads=[("PS", tt), ("XR", tt)], writes=[("XR", tt)])

    EPSC = SM[:, 32:33]; SEPSC = SM[:, 33:34]; ONEC = SM[:, 34:35]; NLAM = SM[:, 35:36]; GSUB = SM[:, 36:37]
    S.add("gpsimd", MSET(SM[:, 32:33], EPS), writes=[("SMK",)])
    S.add("gpsimd", MSET(SM[:, 33:34], SUBLN_EPS), writes=[("SMK",)], partial=True)
    S.add("gpsimd", MSET(SM[:, 34:35], 1.0), writes=[("SMK",)], partial=True)
    S.add("gpsimd", MSET(KVV1[:, :, :, :, :], 0.0), writes=[("KVV", i) for i in range(NKV)])
    S.add("gpsimd", MSET(CU[:, :, :], 0.0), writes=[("CU", j) for j in range(4)])
    S.add("sync", DMA(FS[:, 0, :], c_mats.rearrange("p a b -> p (a b)")), writes=[("FS", 0)], dsem="c0")
    S.add("vector", CP(CM[:, :, :].rearrange("p a b -> p (a b)"), FS[:, 0, :]), reads=[("FS", 0)], writes=[("CM",)])
    S.add("vector", CP(IDF[:, :], FS[:, 0, 0:128]), reads=[("FS", 0)], writes=[("IDF",)])
    for m in range(9):
        f = nfs()
        S.add("sync", DMA(FS[:, f, :], c_mask[:, m, :]), writes=[("FS", f)], dsem=("FS", f))
        S.add("vector", CP(MASK[:, m, :], FS[:, f, :]), reads=[("FS", f)], writes=[("MASK",)], partial=True)
    S.add("sync", DMA(GF[:, :], nfin.partition_broadcast(128)), writes=[("GF",)], dsem="c1")
    S.add("sync", DMA(GSUB, subln), writes=[("GSUB0",)], dsem="c1")
    S.add("sync", DMA(LAMT[:, :, :].rearrange("p a b -> p (a b)"), lam4.rearrange("a b -> (a b)").partition_broadcast(128)),
          writes=[("LAMT",)], dsem="c1")
    gl = [nmix[0:1, :], nmix[1:2, :], ncross[0:1, :], ncross[1:2, :], nffn[0:1, :], nffn[1:2, :], nmem[0:1, :], nmem[1:2, :]]
    for gi, g in enumerate(gl):
        S.add("sync", DMA(GN[:, gi, :], g.rearrange("o (c p) -> p (o c)", p=128), slow=True),
              writes=[("GN",)], dsem="c1", partial=True)
    for j in range(3):
        S.add("sync", DMA(CW[:, :, j], convw[j:j + 1, :].rearrange("o (c p) -> p (o c)", p=128), slow=True),
              writes=[("CW",)], dsem="c1", partial=True)

    S.add("vector", TS(GSUB, GSUB, 1.0 - LAM_INIT0, ALU.mult), reads=[("GSUB0",)], writes=[("GSUB",)])
    S.add("vector", TT(LAMT[:, 0, :], LAMT[:, 0, :], LAMT[:, 1, :], ALU.mult), reads=[("LAMT",)], writes=[("LAMa",)])
    S.add("vector", TT(LAMT[:, 2, :], LAMT[:, 2, :], LAMT[:, 3, :], ALU.mult), reads=[("LAMT",)], writes=[("LAMb",)])
    S.add("vector", lambda e: e.reduce_sum(out=SM[:, 40:41], in_=LAMT[:, 0, :], axis=mybir.AxisListType.X),
          reads=[("LAMa",)], writes=[("LAMc",)])
    S.add("vector", lambda e: e.reduce_sum(out=SM[:, 41:42], in_=LAMT[:, 2, :], axis=mybir.AxisListType.X),
          reads=[("LAMb",)], writes=[("LAMd",)])
    S.add("scalar", ACTF(SM[:, 42:44], SM[:, 40:42], AF.Exp), reads=[("LAMc",), ("LAMd",)], writes=[("LAMe",)])
    S.add("vector", TT(SM[:, 44:45], SM[:, 43:44], SM[:, 42:43], ALU.subtract), reads=[("LAMe",)], writes=[("LAMf",)])
    S.add("vector", TS(NLAM, SM[:, 44:45], -LAM_INIT0, ALU.add), reads=[("LAMf",)], writes=[("LAM",)])
    if _dm:
        for dummy in dummies:
            S.add("sync", DMA(dummy[0:128, :], MASK[:, 0, :]), reads=[("MASK",)], writes=[("DUMMY",)], dsem="c0", partial=True)
            S.add("sync", DMA(dummy[_dm * 1024 - 128:_dm * 1024, :], MASK[:, 0, :]), reads=[("MASK",)], writes=[("DUMMY",)], dsem="c0", partial=True)
    ck(10)
    ctasks = []

    def cast_weight(src, dst, K, N, gi):
        for c in range(K // 128):
            for n0 in range(0, N, 2048):
                ctasks.append((src, dst, c, n0, min(2048, N - n0), gi))

    for l in range(2):
        cast_weight(wk[l], wb_k[l], D, D, 6 + l)
        cast_weight(wv[l], wb_v[l], D, D, 6 + l)
    cast_weight(w_in0, wb_in[0], D, 3072, 0)
    cast_weight(w_out0, wb_out[0], D, D, None)
    cast_weight(w_in1, wb_in[1], D, 3072, 1)
    cast_weight(w_out1, wb_out[1], D, D, None)
    for l in range(2):
        cast_weight(wq[l], wb_q[l], D, D, 2 + l)
        cast_weight(wo[l], wb_o[l], D, D, None)
        cast_weight(wup[l], wb_up[l], D, 4096, 4 + l)
        cast_weight(wdn[l], wb_dn[l], 4096, D, None)

    def c_fin(i):
        s = i % 2
        return R32[:, 8 * s:8 * s + 8, :].rearrange("p a b -> p (a b)").bitcast(F32)

    def c_fout(i):
        s = i % 2
        return R32[:, 16 + 4 * s:20 + 4 * s, :].rearrange("p a b -> p (a b)")

    def c_load(i):
        src, dst, c, n0, w, gi = ctasks[i]
        S.add("sync", DMA(c_fin(i)[:, 0:w], src[c * 128:(c + 1) * 128, n0:n0 + w]), writes=[("CI", i % 2)], dsem=("CI", i % 2))

    c_load(0)
    for i in range(len(ctasks)):
        src, dst, c, n0, w, gi = ctasks[i]
        s = i % 2
        if i + 1 < len(ctasks):
            c_load(i + 1)
        eng = ["vector", "gpsimd", "scalar"][i % 3]
        fin, fout = c_fin(i), c_fout(i)
        if gi is None:
            S.add(eng, copy_op(eng, fout[:, 0:w], fin[:, 0:w]), reads=[("CI", s)], writes=[("CO", s)])
        else:
            gcol = GN[:, gi, (c % 8):(c % 8) + 1]
            if eng == "scalar":
                S.add(eng, ACTF(fout[:, 0:w], fin[:, 0:w], AF.Copy, scale=gcol), reads=[("CI", s), ("GN",)], writes=[("CO", s)])
            else:
                S.add(eng, TS(fout[:, 0:w], fin[:, 0:w], gcol, ALU.mult), reads=[("CI", s), ("GN",)], writes=[("CO", s)])
        S.add("sync", DMA(dst[c * 128:(c + 1) * 128, n0:n0 + w], fout[:, 0:w]), reads=[("CO", s)], writes=[("WSCR",)],
              dsem=("CO", s), partial=True)

    ck(20)
    for l in range(2):
        for w in (wb_k[l], wb_v[l]):
            for c0 in (0, 512):
                panels.append(panel_ap(w, 0, c0))
    bp = block_panels()
    for _ in range(NB + 1):
        panels.extend(bp)

    S.add("sync", DMA(XR[:, 0:2, :], memp.rearrange("(t p) d -> p t d", p=128)), writes=[("XR", 0), ("XR", 1)], dsem="XR")
    ck(21)
    rms_to_HT(2)
    ck(22)
    for l in range(2):
        def evk(pi, tt, b, l=l):
            s = nstg()
            S.add("scalar", ACTF(STG[:, s, 0:512], PS(b), AF.Copy), reads=[("PS", b)], writes=[("STG", s)])
            store_rows(pmk[l, tt * 128:(tt + 1) * 128, pi * 512:(pi + 1) * 512], s, 512)
            kb = nbs()
            S.add("vector", CP(BS[:, kb, :], PS(b)), reads=[("PS", b)], writes=[("BS", kb)])
            transposes_to(lambda j: BS[:, kb, j * 128:(j + 1) * 128], [("BS", kb)], 4,
                          MEMK[:, l, 4 * pi:4 * pi + 4, tt * 128:(tt + 1) * 128], [("MEMK", l)])
        linear_tm(2, HT_lhs, HT_keys, 2, evk)
        ck(23 + 2 * l)

        def evv(pi, tt, b, l=l):
            s = nstg()
            S.add("scalar", ACTF(STG[:, s, 0:512], PS(b), AF.Copy), reads=[("PS", b)], writes=[("STG", s)])
            store_rows(pmv[l, tt * 128:(tt + 1) * 128, pi * 512:(pi + 1) * 512], s, 512)
            S.add("vector", CP(MEMV[:, l, tt, pi * 512:(pi + 1) * 512], PS(b)), reads=[("PS", b)], writes=[("MEMV", l)])
        linear_tm(2, HT_lhs, HT_keys, 2, evv)

    ck(30)
    def prep_cache(l, ck, cv):
        npair = NP[l]
        dv = DV[l]
        for j in range(NSEQ):
            for k0 in range(0, PAST, 512):
                for t in range(4):
                    s = nstg()
                    S.add("sync", DMA(STG[:, s, 0:dv], ck[j, k0 + t * 128:k0 + (t + 1) * 128, :]), writes=[("STG", s)], dsem=("STG", s))
                    for g in range(npair // 4):
                        b = nps()
                        for q in range(4):
                            pq = g * 4 + q
                            S.add("tensor", TR(PS(b)[:, q * 128:(q + 1) * 128], STG[:, s, pq * 128:(pq + 1) * 128], IDF[:, :]),
                                  reads=[("STG", s), ("IDF",)], writes=[("PS", b)], partial=True)
                        eng = evac_eng()
                        S.add(eng, copy_op(eng, R32[:, 16 + g * 4:16 + g * 4 + 4, t * 128:(t + 1) * 128],
                                           PS(b).rearrange("p (c t) -> p c t", t=128)),
                              reads=[("PS", b)], writes=[Rk(16 + g * 4 + q) for q in range(4)], partial=True)
                    s2 = nstg()
                    S.add("sync", DMA(STG[:, s2, 0:dv], cv[j, k0 + t * 128:k0 + (t + 1) * 128, :]), writes=[("STG", s2)], dsem=("STG", s2))
                    vo = R32[:, 24 + 2 * t:26 + 2 * t, :].rearrange("p a b -> p (a b)")
                    eng = evac_eng()
                    S.add(eng, copy_op(eng, vo[:, 0:dv], STG[:, s2, 0:dv]), reads=[("STG", s2)], writes=[Rk(24 + 2 * t), Rk(25 + 2 * t)])
                S.add("sync", DMA(KTs[l][j][:, :, k0:k0 + 512], R32[:, 16:16 + npair, :]),
                      reads=[Rk(16 + q) for q in range(npair)], writes=[("KVSCR",)], dsem="KTO", partial=True)
                S.add("sync", DMA(Vs[l][j][k0:k0 + 512, :].rearrange("(t p) d -> p t d", p=128),
                                  R32[:, 24:32, :].rearrange("p (t a) b -> p t (a b)", a=2)[:, :, 0:dv]),
                      reads=[Rk(24 + q) for q in range(8)], writes=[("KVSCR",)], dsem="VO", partial=True)

    prep_cache(0, cdk, cdv)
    ck(33)
    prep_cache(1, csk, csv)
    ck(35)
    for l in range(2):
        for j in range(NSEQ):
            for t in range(2):
                s = nstg()
                S.add("sync", DMA(STG[:, s, :], cmk[l, j, t * 128:(t + 1) * 128, :]), writes=[("STG", s)], dsem=("STG", s))
                for g in range(2):
                    b = nps()
                    for q in range(4):
                        pq = g * 4 + q
                        S.add("tensor", TR(PS(b)[:, q * 128:(q + 1) * 128], STG[:, s, pq * 128:(pq + 1) * 128], IDF[:, :]),
                              reads=[("STG", s), ("IDF",)], writes=[("PS", b)], partial=True)
                    eng = evac_eng()
                    S.add(eng, copy_op(eng, R32[:, 16 + g * 4:16 + g * 4 + 4, t * 128:(t + 1) * 128],
                                       PS(b).rearrange("p (c t) -> p c t", t=128)),
                          reads=[("PS", b)], writes=[Rk(16 + g * 4 + q) for q in range(4)], partial=True)
                s2 = nstg()
                S.add("sync", DMA(STG[:, s2, :], cmv[l, j, t * 128:(t + 1) * 128, :]), writes=[("STG", s2)], dsem=("STG", s2))
                vo = R32[:, 24 + 2 * t:26 + 2 * t, :].rearrange("p a b -> p (a b)")
                eng = evac_eng()
                S.add(eng, copy_op(eng, vo, STG[:, s2, :]), reads=[("STG", s2)], writes=[Rk(24 + 2 * t), Rk(25 + 2 * t)])
            S.add("sync", DMA(MKs[l][j][:, :, :], R32[:, 16:24, 0:256]),
                  reads=[Rk(16 + q) for q in range(8)], writes=[("MSCR",)], dsem="KTO", partial=True)
            S.add("sync", DMA(MVs[l][j][:, :].rearrange("(t p) d -> p t d", p=128),
                              R32[:, 24:28, :].rearrange("p (t a) b -> p t (a b)", a=2)),
                  reads=[Rk(24 + q) for q in range(4)], writes=[("MSCR",)], dsem="VO", partial=True)

    ck(38)
    def run_block(bi):
        sample = bi == NB
        ntt = 1 if sample else 4
        TB = ntt * 128
        nseq, L = (NSEQ, LS) if sample else (1, 512)
        t0 = bi * 512
        xsrc = xs if sample else xp[t0:t0 + TB, :]
        S.add("sync", DMA(XR[:, 0:ntt, :], xsrc.rearrange("(t p) d -> p t d", p=128)),
              writes=[("XR", t) for t in range(ntt)], dsem="XR")
        rrow = T if sample else t0
        S.add("sync", DMA(ROPE[:, 0:ntt, :], c_rope[rrow:rrow + TB, :].rearrange("(t p) d -> p t d", p=128)),
              writes=[("ROPE",)], dsem="ROPE")
        if sample:
            S.add("sync", DMA(FS[0:8, 0, :], sconv), writes=[("FS", 0)], dsem=("FS", 0))
            b = nps()
            for j in range(4):
                S.add("tensor", TR(PS(b)[:, j * 8:(j + 1) * 8], FS[0:8, 0, j * 128:(j + 1) * 128], IDF[0:8, 0:8]),
                      reads=[("FS", 0), ("IDF",)], writes=[("PS", b)], partial=True)
            S.add("vector", CP(CU[:, :, 0:NSEQ * 34].rearrange("p j (s x) -> p j s x", x=34)[:, :, :, 0:2],
                               PS(b)[:, 0:32].rearrange("p (j s x) -> p j s x", j=4, x=2)),
                  reads=[("PS", b)], writes=[("CU", j) for j in range(4)])
        W34 = L + 2
        cu3 = lambda j: CU[:, j, 0:nseq * W34].rearrange("p (s x) -> p s x", x=W34)
        v3 = lambda ap: ap.rearrange("p (s x) -> p s x", x=L)

        rms_to_HT(ntt)
        rq = lambda tt, a, b_: ROPE[:, tt, a:b_]

        def rope(tt, b, base, out_ap, okeys):
            x3 = PS(b).rearrange("p (h d) -> p h d", d=64)
            fa = nfs(); fb = nfs()
            A3 = FS[:, fa, :].rearrange("p (h d) -> p h d", d=64)
            B3 = FS[:, fb, :].rearrange("p (h d) -> p h d", d=64)
            cosb = rq(tt, base, base + 64).unsqueeze(1).to_broadcast([128, 8, 64])
            s1 = rq(tt, base + 64, base + 96).unsqueeze(1).to_broadcast([128, 8, 32])
            s2 = rq(tt, base + 96, base + 128).unsqueeze(1).to_broadcast([128, 8, 32])
            S.add("vector", TT(A3, x3, cosb, ALU.mult), reads=[("PS", b), ("ROPE",)], writes=[("FS", fa)])
            S.add("vector", TT(B3[:, :, 0:32], x3[:, :, 32:64], s1, ALU.mult), reads=[("PS", b), ("ROPE",)], writes=[("FS", fb)])
            S.add("vector", TT(B3[:, :, 32:64], x3[:, :, 0:32], s2, ALU.mult), reads=[("PS", b), ("ROPE",)], writes=[("FS", fb)], partial=True)
            S.add("gpsimd", TT(out_ap, FS[:, fa, :], FS[:, fb, :], ALU.add), reads=[("FS", fa), ("FS", fb)], writes=okeys)

        def ev_q0(pi, tt, b):
            qb = nbs()
            rope(tt, b, 0, BS[:, qb, :], [("BS", qb)])
            transposes_to(lambda j: BS[:, qb, j * 128:(j + 1) * 128], [("BS", qb)], 4,
                          R32[:, 0:4, tt * 128:(tt + 1) * 128], [Rk(q) for q in range(4)])
        linear_tm(ntt, HT_lhs, HT_keys, 1, ev_q0)

        kdst = sdk if sample else pdk[t0:t0 + TB, :]
        vdst = sdv if sample else pdv[t0:t0 + TB, :]

        def ev_k0(pi, tt, b):
            s = nstg()
            rope(tt, b, 128, STG[:, s, 0:512], [("STG", s)])
            store_rows(kdst[tt * 128:(tt + 1) * 128, :], s, 512)
            kb = nbs()
            S.add("scalar", ACTF(BS[:, kb, :], STG[:, s, 0:512], AF.Copy), reads=[("STG", s)], writes=[("BS", kb)])
            transposes_to(lambda j: BS[:, kb, j * 128:(j + 1) * 128], [("BS", kb)], 4,
                          R32[:, 16:20, tt * 128:(tt + 1) * 128], [Rk(16 + q) for q in range(4)])
        linear_tm(ntt, HT_lhs, HT_keys, 1, ev_k0)

        def VO(tt):
            return R32[:, 24 + 2 * tt:26 + 2 * tt, :].rearrange("p a b -> p (a b)")

        def ev_v0(pi, tt, b):
            s = nstg()
            S.add("scalar", ACTF(STG[:, s, 0:512], PS(b), AF.Copy), reads=[("PS", b)], writes=[("STG", s)])
            store_rows(vdst[tt * 128:(tt + 1) * 128, :], s, 512)
            S.add("vector", CP(VO(tt)[:, 0:512], PS(b)), reads=[("PS", b)], writes=[Rk(24 + 2 * tt), Rk(25 + 2 * tt)])
        linear_tm(ntt, HT_lhs, HT_keys, 1, ev_v0)

        def store_kv(l):
            npair, dv = NP[l], DV[l]
            if not sample:
                S.add("sync", DMA(KTp[l][:, :, t0:t0 + TB], R32[:, 16:16 + npair, 0:TB]),
                      reads=[Rk(16 + q) for q in range(npair)], writes=[("KVSCR",)], dsem="KTO", partial=True)
                S.add("sync", DMA(Vp[l][t0:t0 + TB, :].rearrange("(t p) d -> p t d", p=128),
                                  R32[:, 24:32, :].rearrange("p (t a) b -> p t (a b)", a=2)[:, :, 0:dv]),
                      reads=[Rk(24 + q) for q in range(8)], writes=[("KVSCR",)], dsem="VO", partial=True)
            else:
                for j in range(NSEQ):
                    S.add("sync", DMA(KTs[l][j][:, :, PAST:PAST + LS], R32[:, 16:16 + npair, j * LS:(j + 1) * LS]),
                          reads=[Rk(16 + q) for q in range(npair)], writes=[("KVSCR",)], dsem="KTO", partial=True)
                    S.add("sync", DMA(Vs[l][j][PAST:PAST + LS, :], VO(0)[j * LS:(j + 1) * LS, 0:dv]),
                          reads=[Rk(24), Rk(25)], writes=[("KVSCR",)], dsem="VO", partial=True)
        store_kv(0)

        def ev_gc(pi, j, b):
            S.add("scalar", ACTF(FS[:, j, 0:TB], PS(b)[:, 0:TB], AF.Copy), reads=[("PS", b)], writes=[("FS", j)])
        ctr["fs"] = 3
        linear_fm(ntt, 1, ev_gc)

        def ev_u(pi, j, b):
            S.add("vector", TT(cu3(j)[:, :, 2:2 + L], v3(PS(b)[:, 0:TB]), v3(FS[:, j, 0:TB]), ALU.mult),
                  reads=[("PS", b), ("FS", j)], writes=[("CU", j)])
            S.add("gpsimd", TS(v3(FS[:, j, 0:TB]), cu3(j)[:, :, 2:2 + L], CW[:, j, 2:3], ALU.mult),
                  reads=[("CU", j), ("CW",)], writes=[("FS", j)])
            S.add("vector", STT(v3(FS[:, j, 0:TB]), cu3(j)[:, :, 1:1 + L], CW[:, j, 1:2], v3(FS[:, j, 0:TB]), ALU.mult, ALU.add),
                  reads=[("CU", j), ("CW",), ("FS", j)], writes=[("FS", j)])
            S.add("vector", STT(v3(FS[:, j, 0:TB]), cu3(j)[:, :, 0:L], CW[:, j, 0:1], v3(FS[:, j, 0:TB]), ALU.mult, ALU.add),
                  reads=[("CU", j), ("CW",), ("FS", j)], writes=[("FS", j)])
        ctr["fs"] = 3
        linear_fm(ntt, 1, ev_u)

        def ev_gb(pi, j, b):
            S.add("vector", TT(ATT[:, 4 + j, 0:TB], PS(b)[:, 0:TB], FS[:, j, 0:TB], ALU.mult),
                  reads=[("PS", b), ("FS", j)], writes=[("ATT", 4 + j)])
        linear_fm(ntt, 1, ev_gb)
        if sample or bi == NB - 1:
            b = nps()
            nr = 2 * nseq
            f = 4
            for j in range(4):
                S.add("gpsimd", CP(FS[:, f, j * 8:j * 8 + nr].rearrange("p (s x) -> p s x", x=2), cu3(j)[:, :, L:L + 2]),
                      reads=[("CU", j)], writes=[("FS", f)], partial=True)
            for j in range(4):
                S.add("tensor", TR(PS(b)[0:nr, j * 128:(j + 1) * 128], FS[:, f, j * 8:j * 8 + nr], IDF[:, :]),
                      reads=[("FS", f), ("IDF",)], writes=[("PS", b)], partial=True)
            s = nstg()
            S.add("scalar", ACTF(STG[0:nr, s, 0:512], PS(b)[0:nr, :], AF.Copy), reads=[("PS", b)], writes=[("STG", s)])
            S.add("sync", DMA((sconvo if sample else pconv)[:, :], STG[0:nr, s, 0:512]), reads=[("STG", s)], writes=[("OUT",)],
                  dsem=("STG", s), partial=True)
        if not sample:
            for j in range(4):
                S.add("gpsimd", CP(CU[:, j, 0:2], CU[:, j, 512:514]), reads=[("CU", j)], writes=[("CU", j)])

        def streams_for(l):
            if not sample:
                chunks = [(kc * 512, 512) for kc in range(bi, -1, -1)]
                mbase = 4 if l == 0 else 0

                def maskfn(k0t, mbase=mbase):
                    if k0t >= t0:
                        return MASK[:, mbase + (k0t - t0) // 128, :]
                    return None
                return [(0, 512, KTp[l], Vp[l], chunks, maskfn)]
            st = []
            chunks = [(PAST, LS)] + [(k0, 512) for k0 in range(PAST - 512, -1, -512)]
            for j in range(NSEQ):
                if l == 0:
                    mf = lambda k0t: None
                else:
                    def mf(k0t, j=j):
                        if k0t >= PAST:
                            return MASK[:, 8, :]
                        return None
                st.append((j * LS, LS, KTs[l][j], Vs[l][j], chunks, mf))
            return st

        if sample:
            pass
        attention(0, ntt, streams_for(0))
        linear_tm(ntt, ATT_lhs, ATT_keys, 2, resid_add_evac(ntt))

        def mem_seqs(l):
            if not sample:
                return [(0, 512, l)]
            return None

        def do_mem(l):
            if not sample:
                mem_attention(ntt, l, [(0, 512, l)])
            else:
                seqs = []
                for j in range(NSEQ):
                    seqs.append((j * LS, LS, j % 2))
                mem_attention_sample(l, seqs)
        def mem_attention_sample(l, seqs):
            TBs = 128
            rms_to_HT(1)

            def evq(pi, j, b):
                eng = evac_eng()
                S.add(eng, copy_op(eng, R32[:, 4 * pi + j, 0:TBs], PS(b)[:, 0:TBs], scale=1.0 / 16),
                      reads=[("PS", b)], writes=[Rk(4 * pi + j)])
            linear_fm(1, 2, evq)
            for (qc0, nq, ms) in seqs:
                j = qc0 // LS
                S.add("sync", DMA(MEMK[:, ms, :, :], MKs[l][j][:, :, :]), reads=[("MSCR",)], writes=[("MEMK", ms)], dsem=("MEMK", ms))
                S.add("sync", DMA(MEMV[:, ms, :, :], MVs[l][j][:, :].rearrange("(t p) d -> p t d", p=128)),
                      reads=[("MSCR",)], writes=[("MEMV", ms)], dsem=("MEMV", ms))
                for h in range(4):
                    pts = []
                    for kt in range(2):
                        for dc in range(2):
                            S.add("tensor", MM(PS(kt)[:, 0:nq], MEMK[:, ms, 2 * h + dc, kt * 128:(kt + 1) * 128],
                                               R32[:, 2 * h + dc, qc0:qc0 + nq], dc == 0, dc == 1),
                                  reads=[("MEMK", ms), Rk(2 * h + dc)], writes=[("PS", kt)], partial=True)
                        pb = nbs()
                        S.add("scalar", ACTF(BS[:, pb, 0:nq], PS(kt)[:, 0:nq], AF.Exp), reads=[("PS", kt)], writes=[("BS", pb)])
                        pts.append(pb)
                    for dvc in range(2):
                        for kt in range(2):
                            S.add("tensor", MM(PS(2 + dvc)[:, 0:nq],
                                               MEMV[:, ms, kt, h * 256 + dvc * 128:h * 256 + (dvc + 1) * 128],
                                               BS[:, pts[kt], 0:nq], kt == 0, kt == 1),
                                  reads=[("MEMV", ms), ("BS", pts[kt])], writes=[("PS", 2 + dvc)], partial=True)
                    for kt in range(2):
                        S.add("tensor", MM(PS(4)[:, 0:nq], ONES, BS[:, pts[kt], 0:nq], kt == 0, kt == 1),
                              reads=[("BS", pts[kt]), ("CM",)], writes=[("PS", 4)], partial=True)
                    r = nfs()
                    S.add("vector", RECIP(FS[:, r, 0:nq], PS(4)[:, 0:nq]), reads=[("PS", 4)], writes=[("FS", r)])
                    for dvc in range(2):
                        S.add("vector", TT(ATT[:, 2 * h + dvc, qc0:qc0 + nq], PS(2 + dvc)[:, 0:nq], FS[:, r, 0:nq], ALU.mult),
                              reads=[("PS", 2 + dvc), ("FS", r)], writes=[("ATT", 2 * h + dvc)], partial=True)
            linear_tm(1, ATT_lhs, ATT_keys, 2, resid_add_evac(1))

        do_mem(0)
        ffn(ntt)

        rms_to_HT(ntt)

        def ev_q1(pi, tt, b):
            qb = nbs()
            S.add("scalar", ACTF(BS[:, qb, :], PS(b), AF.Copy, scale=0.125), reads=[("PS", b)], writes=[("BS", qb)])
            transposes_to(lambda j: BS[:, qb, j * 128:(j + 1) * 128], [("BS", qb)], 4,
                          R32[:, 4 * pi:4 * pi + 4, tt * 128:(tt + 1) * 128], [Rk(4 * pi + q) for q in range(4)],
                          scale_neg_dst=R32[:, 8 + 4 * pi:12 + 4 * pi, tt * 128:(tt + 1) * 128],
                          neg_keys=[Rk(8 + 4 * pi + q) for q in range(4)])
        linear_tm(ntt, HT_lhs, HT_keys, 2, ev_q1)
        kdst1 = ssk if sample else psk[t0:t0 + TB, :]
        vdst1 = ssv if sample else psv[t0:t0 + TB, :]

        def ev_k1(pi, tt, b):
            s = nstg()
            S.add("scalar", ACTF(STG[:, s, 0:512], PS(b), AF.Copy), reads=[("PS", b)], writes=[("STG", s)])
            store_rows(kdst1[tt * 128:(tt + 1) * 128, pi * 512:(pi + 1) * 512], s, 512)
            kb = nbs()
            S.add("vector", CP(BS[:, kb, :], PS(b)), reads=[("PS", b)], writes=[("BS", kb)])
            transposes_to(lambda j: BS[:, kb, j * 128:(j + 1) * 128], [("BS", kb)], 4,
                          R32[:, 16 + 4 * pi:20 + 4 * pi, tt * 128:(tt + 1) * 128], [Rk(16 + 4 * pi + q) for q in range(4)])
        linear_tm(ntt, HT_lhs, HT_keys, 2, ev_k1)

        def ev_v1(pi, tt, b):
            s = nstg()
            S.add("scalar", ACTF(STG[:, s, 0:512], PS(b), AF.Copy), reads=[("PS", b)], writes=[("STG", s)])
            store_rows(vdst1[tt * 128:(tt + 1) * 128, pi * 512:(pi + 1) * 512], s, 512)
            S.add("vector", CP(VO(tt)[:, pi * 512:(pi + 1) * 512], PS(b)), reads=[("PS", b)], writes=[Rk(24 + 2 * tt), Rk(25 + 2 * tt)],
                  partial=True)
        linear_tm(ntt, HT_lhs, HT_keys, 2, ev_v1)
        store_kv(1)
        attention(1, ntt, streams_for(1))
        linear_tm(ntt, ATT_lhs, ATT_keys, 2, resid_add_evac(ntt))
        do_mem(1)
        ffn(ntt)

        ydst = y_s if sample else y_p[t0:t0 + TB, :]
        for tt in range(ntt):
            s = ctr["hn"] = (ctr["hn"] + 1) % 2
            S.add("scalar", ACTF(HN[:, s, :], XR[:, tt, :], AF.Square, accum=SM[:, tt:tt + 1]),
                  reads=[("XR", tt)], writes=[("HN", s), ("SMa", tt)])
            S.add("scalar", ACTF(SM[:, 4 + tt:5 + tt], SM[:, tt:tt + 1], AF.Ln, scale=1.0 / D, bias=EPSC[:, 0:1]),
                  reads=[("SMa", tt), ("SMK",)], writes=[("SMb", tt)])
            S.add("scalar", ACTF(SM[:, 8 + tt:9 + tt], SM[:, 4 + tt:5 + tt], AF.Exp, scale=-0.5),
                  reads=[("SMb", tt)], writes=[("SMc", tt)])
            sg = nstg()
            S.add("vector", STT(STG[:, sg, :], XR[:, tt, :], SM[:, 8 + tt:9 + tt], GF[:, :], ALU.mult, ALU.mult),
                  reads=[("XR", tt), ("SMc", tt), ("GF",)], writes=[("STG", sg)])
            store_rows(ydst[tt * 128:(tt + 1) * 128, :], sg, D)

    try:
        ck(40)
        for bi in range(NB + 1):
            run_block(bi)
            ck(50 + bi)
    except _Stop:
        pass

    print("NOPS", len(S.ops), flush=True)
    if os.environ.get("KDUMP"):
        for i, op in enumerate(S.ops[-int(os.environ["KDUMP"]):]):
            print(len(S.ops) - int(os.environ["KDUMP"]) + i, op["eng"], op["dsem"], flush=True)
    S.emit(nc, es)
    es.close()
    return nc


def _consts(T, PAST):
    mats = np.zeros((128, 4, 128), np.float32)
    mats[:, 0, :] = np.eye(128)
    k = np.arange(128)[:, None]; kp = np.arange(128)[None, :]
    mats[:, 1, :] = (k >= kp)
    mats[:, 2, :] = 1.0
    mask = np.zeros((128, 9, 512), np.float32)
    q = np.arange(512)[None, :]
    for j in range(4):
        kk = j * 128 + np.arange(128)[:, None]
        mask[:, j, :] = (kk < q)
        mask[:, 4 + j, :] = ((kk // 64) <= (q // 64))
    kk = np.arange(128)[:, None]
    mask[:, 8, :] = (kk < (q % 32))
    half = 32
    inv = np.power(np.float32(10000.0), -np.arange(half, dtype=np.float32) * np.float32(2.0 / 64)).astype(np.float32)
    pos = np.concatenate([np.arange(T), PAST + (np.arange(128) % 32)]).astype(np.float32)
    ang = (pos[:, None] * inv[None, :]).astype(np.float32)
    cos = np.cos(ang).astype(np.float32); sin = np.sin(ang).astype(np.float32)
    cos2 = np.concatenate([cos, cos], 1); sinS = np.concatenate([-sin, sin], 1)
    rope = np.concatenate([cos2 * 0.125, sinS * 0.125, cos2, sinS], 1).astype(np.float32)
    return mats, mask, rope


_CACHE = {}


def _run(inputs, T, PAST, ncores):
    key = (T, PAST)
    if key not in _CACHE:
        _CACHE[key] = build(T, PAST)
    nc = _CACHE[key]
    mats, mask, rope = _consts(T, PAST)
    f = lambda a: np.ascontiguousarray(np.asarray(a, dtype=np.float32))
    I = {k: np.asarray(v) for k, v in inputs.items()}
    shared = dict(
        w_in0=f(I["w_in_even"][0]), w_out0=f(I["w_out_even"][0]),
        lam4=f(np.stack([I["lambda_q1"][0], I["lambda_k1"][0], I["lambda_q2"][0], I["lambda_k2"][0]])),
        subln=f(I["subln_gain"][0].reshape(128, 1)), convw=f(I["conv_w"][0]),
        w_in1=f(I["w_in_odd"][0]), w_out1=f(I["w_out_odd"][0]),
        nmix=f(I["norm_mix"]), nmem=f(I["norm_mem"]), ncross=f(I["norm_cross"]), nffn=f(I["norm_ffn"]),
        wq=f(I["w_q_mem"]), wk=f(I["w_k_mem"]), wv=f(I["w_v_mem"]), wo=f(I["w_o_mem"]),
        wup=f(I["w_ffn_up"]), wdn=f(I["w_ffn_down"]), nfin=f(I["norm_final"].reshape(1, D)),
        c_mats=mats, c_mask=mask, c_rope=rope,
    )
    def tagged(a, axis, c):
        shp = list(a.shape); shp[axis] = 1
        return np.ascontiguousarray(np.concatenate([a, np.full(shp, float(c), np.float32)], axis=axis))

    TAG = dict(w_in0=0, w_out0=0, w_in1=0, w_out1=0, wq=1, wk=1, wv=1, wo=1, wup=1, wdn=1, c_mask=1, c_rope=0)
    in_maps = []
    for c in range(ncores):
        sl = slice(NSEQ * c, NSEQ * (c + 1))
        m = dict(shared)
        for k, ax in TAG.items():
            m[k] = tagged(shared[k], ax, c)
        m.update(
            xp=f(I["x_prompt"][c]), xs=f(I["x_sample"][sl].reshape(128, D)),
            cdk=f(I["cache_diff_k"][0, sl].reshape(NSEQ, PAST, 512)), cdv=f(I["cache_diff_v"][0, sl].reshape(NSEQ, PAST, 512)),
            sconv=f(I["state_conv"][0, sl].reshape(NSEQ * 2, 512)),
            csk=f(I["cache_sb_k"][0, sl].reshape(NSEQ, PAST, 1024)), csv=f(I["cache_sb_v"][0, sl].reshape(NSEQ, PAST, 1024)),
            cmk=f(I["cache_mem_k"][:, sl].reshape(2, NSEQ, 256, 1024)), cmv=f(I["cache_mem_v"][:, sl].reshape(2, NSEQ, 256, 1024)),
            memp=f(I["mem_prompt"][c]),
        )
        in_maps.append(m)
    res = run_bass_kernel_spmd(nc, in_maps, core_ids=list(range(ncores)))
    R = res.results
    B = ncores
    g = lambda name: np.stack([np.asarray(R[c][name], dtype=np.float32) for c in range(B)])
    y_prompt = g("y_p")
    y_sample = g("y_s").reshape(B * NSEQ, LS, D)
    p_diff_k = g("pdk").reshape(1, B, T, 8, 64)
    p_diff_v = g("pdv").reshape(1, B, T, 4, 128)
    p_conv = g("pconv").reshape(1, B, 2, 512)
    p_sb_k = g("psk").reshape(1, B, T, 16, 64)
    p_sb_v = g("psv").reshape(1, B, T, 16, 64)
    p_mem_k = np.transpose(g("pmk"), (1, 0, 2, 3)).reshape(2, B, 256, 4, 256)
    p_mem_v = np.transpose(g("pmv"), (1, 0, 2, 3)).reshape(2, B, 256, 4, 256)
    s_diff_k = g("sdk").reshape(1, B * NSEQ, LS, 8, 64)
    s_diff_v = g("sdv").reshape(1, B * NSEQ, LS, 4, 128)
    s_conv = g("sconvo").reshape(1, B * NSEQ, 2, 512)
    s_sb_k = g("ssk").reshape(1, B * NSEQ, LS, 16, 64)
    s_sb_v = g("ssv").reshape(1, B * NSEQ, LS, 16, 64)
    return (y_prompt, y_sample, p_diff_k, p_diff_v, p_conv, p_sb_k, p_sb_v, p_mem_k, p_mem_v,
            s_diff_k, s_diff_v, s_conv, s_sb_k, s_sb_v)


def kernel(**inputs):
    T = int(np.asarray(inputs["x_prompt"]).shape[1])
    PAST = int(np.asarray(inputs["cache_diff_k"]).shape[2])
    ncores = int(np.asarray(inputs["x_prompt"]).shape[0])
    return _run(inputs, T, PAST, ncores)
```
